# Optimizing a Trainium2 kernel written in Bass

```python
import math
import jax, jax.numpy as jnp
from jax import lax
import numpy as np

D_MODEL = 1024
BATCH = 4
SEQ = 4096
DEPTH = 1
DEC_BATCH = 2
DEC_SEQ = 8192
PAST_LEN = 128

PLE_DIM = 256
SSD_WIDTH = 1024
SSD_HEAD_DIM = 64
SSD_HEADS = SSD_WIDTH // SSD_HEAD_DIM
SSD_GROUPS = 2
SSD_STATE = 128
SSD_CONV = 5
SSD_CHUNK = 128
XBC_WIDTH = SSD_WIDTH + 2 * SSD_GROUPS * SSD_STATE
HY_WIDTH = 1024
HY_GROUPS = 16
HY_SHORT = 3
HY_EMB = 33
HY_FILTER_HIDDEN = 64
HY_FAST_DECAY = 0.3
HY_SLOW_DECAY = 1.5
HY_DECAY_TARGET = 1e-2
MIX_WIDTH = SSD_WIDTH + HY_WIDTH
IN_COLS = SSD_WIDTH + XBC_WIDTH + 2 * SSD_HEADS + 3 * HY_WIDTH
FFN_HIDDEN = -(-8 * D_MODEL // (3 * 256)) * 256
EPS = 1e-6

kernel_name = "hymba_ssd_hyena_bidir_encoder"


def group_rmsnorm(x, w, groups=1):
    shp = x.shape
    xf = x.astype(jnp.float32).reshape(shp[:-1] + (groups, shp[-1] // groups))
    xf = xf * lax.rsqrt(jnp.mean(xf * xf, axis=-1, keepdims=True) + EPS)
    return (xf.reshape(shp) * w.astype(jnp.float32)).astype(x.dtype)


def depthwise_conv(x, w, b):
    width = w.shape[0]
    y = lax.conv_general_dilated(
        x, w[:, None, :].astype(x.dtype), window_strides=(1,),
        padding=[(width // 2, width // 2)],
        dimension_numbers=('NWC', 'WIO', 'NWC'),
        feature_group_count=x.shape[-1])
    return y + b.astype(x.dtype)


def ssd_chunked(x, dt, a_h, bm, cm):
    b, L, H, P = x.shape
    G, N = bm.shape[2], bm.shape[3]
    R = H // G
    T = SSD_CHUNK
    c = L // T
    xr = (x * dt[..., None]).reshape(b, c, T, G, R, P)
    a = (dt * a_h).reshape(b, c, T, G, R)
    a_cs = jnp.cumsum(a, axis=2)
    br = bm.reshape(b, c, T, G, N)
    cr = cm.reshape(b, c, T, G, N)
    a_t = jnp.moveaxis(a_cs, 2, -1)
    seg = a_t[..., :, None] - a_t[..., None, :]
    causal = jnp.tril(jnp.ones((T, T), dtype=bool))
    decay_ts = jnp.exp(jnp.where(causal, seg, -jnp.inf))
    scores = jnp.einsum('bctgn,bcsgn->bcgts', cr, br)
    y_diag = jnp.einsum('bcgrts,bcsgrp->bctgrp', scores[:, :, :, None] * decay_ts, xr)
    dec_end = jnp.exp(a_cs[:, :, -1:] - a_cs)
    chunk_states = jnp.einsum('bcsgn,bcsgrp->bcgrpn', br, xr * dec_end[..., None])
    chunk_decay = jnp.exp(a_cs[:, :, -1])

    def step(carry, inp):
        s_c, d_c = inp
        return carry * d_c[..., None, None] + s_c, carry

    init = jnp.zeros((b, G, R, P, N), jnp.float32)
    _, prev = lax.scan(step, init, (jnp.moveaxis(chunk_states, 1, 0), jnp.moveaxis(chunk_decay, 1, 0)))
    prev = jnp.moveaxis(prev, 0, 1)
    y_off = jnp.einsum('bctgn,bcgrpn->bctgrp', cr, prev) * jnp.exp(a_cs)[..., None]
    return (y_diag + y_off).reshape(b, L, H, P)


def hyena_filters(L, w1, b1, w2, b2, w3, b3, w4, freq):
    f32 = jnp.float32
    pos = jnp.arange(L, dtype=f32)[:, None]
    t = pos / (L - 1)
    bands = (HY_EMB - 1) // 2
    fb = jnp.linspace(1e-4, bands - 1, bands, dtype=f32)[None, :]
    ang = fb * (2.0 * math.pi * pos / L)
    z = jnp.concatenate([t, jnp.cos(ang), -jnp.sin(ang)], axis=-1)
    fr = freq.astype(f32)
    h = jnp.sin(fr * (z @ w1.astype(f32) + b1.astype(f32)))
    h = jnp.sin(fr * (h @ w2.astype(f32) + b2.astype(f32)))
    h = jnp.sin(fr * (h @ w3.astype(f32) + b3.astype(f32)))
    h = (h @ w4.astype(f32)).reshape(L, 2, HY_WIDTH)
    max_decay = math.log(HY_DECAY_TARGET) / HY_FAST_DECAY
    min_decay = math.log(HY_DECAY_TARGET) / HY_SLOW_DECAY
    deltas = jnp.linspace(min_decay, max_decay, HY_WIDTH, dtype=f32)
    window = jnp.exp(-t * jnp.abs(deltas)[None, :])
    h = h * window[:, None, :]
    return h[:, 0], h[:, 1]


def bidir_fftconv(u, h_f, h_b):
    L, C = h_f.shape
    k = jnp.concatenate([h_f, jnp.zeros((1, C), jnp.float32), h_b[:0:-1]], axis=0)
    k_hat = jnp.fft.rfft(k, n=2 * L, axis=0)
    u_hat = jnp.fft.rfft(u.astype(jnp.float32), n=2 * L, axis=1)
    return jnp.fft.irfft(u_hat * k_hat[None], n=2 * L, axis=1)[:, :L]


def encoder_layer(h, p_l, norm_mix_pre, w_in, ssd_conv_w, ssd_conv_b, ssd_dt_bias, ssd_a_log,
                  ssd_d, ssd_norm_w, hy_conv_w, hy_conv_b, hy_f_w1, hy_f_b1, hy_f_w2, hy_f_b2,
                  hy_f_w3, hy_f_b3, hy_f_w4, hy_f_freq, hy_bias, hy_norm_w, w_out, norm_mix_post,
                  norm_ffn_pre, w_gate, w_up, w_down, norm_ffn_post, ple_norm_pre, w_ple_gate,
                  w_ple_proj, ple_norm_post):
    f32 = jnp.float32
    b, L, _ = h.shape
    dtype = h.dtype
    u = group_rmsnorm(h, norm_mix_pre)
    proj = u @ w_in
    o1 = SSD_WIDTH
    o2 = o1 + XBC_WIDTH
    o3 = o2 + 2 * SSD_HEADS
    z, xbc, dt_raw, hy = proj[..., :o1], proj[..., o1:o2], proj[..., o2:o3], proj[..., o3:]

    xbc = jax.nn.silu(depthwise_conv(xbc, ssd_conv_w, ssd_conv_b)).astype(f32)
    xs = xbc[..., :SSD_WIDTH].reshape(b, L, SSD_HEADS, SSD_HEAD_DIM)
    bm = xbc[..., SSD_WIDTH:SSD_WIDTH + SSD_GROUPS * SSD_STATE].reshape(b, L, SSD_GROUPS, SSD_STATE)
    cm = xbc[..., SSD_WIDTH + SSD_GROUPS * SSD_STATE:].reshape(b, L, SSD_GROUPS, SSD_STATE)
    dt = jax.nn.softplus(dt_raw.astype(f32).reshape(b, L, 2, SSD_HEADS) + ssd_dt_bias.astype(f32))
    a_h = -jnp.exp(ssd_a_log.astype(f32))
    y_fwd = ssd_chunked(xs, dt[:, :, 0], a_h[0], bm, cm)
    y_bwd = jnp.flip(ssd_chunked(jnp.flip(xs, 1), jnp.flip(dt[:, :, 1], 1), a_h[1],
                                 jnp.flip(bm, 1), jnp.flip(cm, 1)), 1)
    y_ssd = (y_fwd + y_bwd + xs * ssd_d.astype(f32)[:, None]).reshape(b, L, SSD_WIDTH)
    y_ssd = group_rmsnorm(y_ssd * jax.nn.silu(z.astype(f32)), ssd_norm_w, SSD_GROUPS).astype(dtype)

    hy = depthwise_conv(hy, hy_conv_w, hy_conv_b)
    x0, x1, v = hy[..., :HY_WIDTH], hy[..., HY_WIDTH:2 * HY_WIDTH], hy[..., 2 * HY_WIDTH:]
    h_f, h_b = hyena_filters(L, hy_f_w1, hy_f_b1, hy_f_w2, hy_f_b2, hy_f_w3, hy_f_b3, hy_f_w4, hy_f_freq)
    g = (v * x1).astype(f32)
    conv = bidir_fftconv(g, h_f, h_b) + g * hy_bias.astype(f32)
    y_hy = group_rmsnorm(x0.astype(f32) * conv, hy_norm_w, HY_GROUPS).astype(dtype)

    mix = jnp.concatenate([y_ssd, y_hy], axis=-1) @ w_out
    h = h + group_rmsnorm(mix, norm_mix_post)

    u = group_rmsnorm(h, norm_ffn_pre)
    ff = (jax.nn.silu(u @ w_gate) * (u @ w_up)) @ w_down
    h = h + group_rmsnorm(ff, norm_ffn_post)

    gate = jax.nn.sigmoid((group_rmsnorm(h, ple_norm_pre) @ w_ple_gate).astype(f32))
    e = (p_l @ w_ple_proj).astype(f32)
    h = h + group_rmsnorm((gate * e).astype(dtype), ple_norm_post)
    return h


def run_trunk(x, p, weights):
    h = x
    for i in range(DEPTH):
        h = encoder_layer(h, p[i], *[w[i] for w in weights])
    return h


def setup_inputs(seed: int = 0) -> dict:
    key = jax.random.key(seed)
    ks = jax.random.split(key, 40)
    f32 = jnp.float32

    def nrm(k, shape, scale):
        return scale * jax.random.normal(k, shape, f32)

    def gain(k, shape):
        return 1.0 + 0.05 * jax.random.normal(k, shape, f32)

    dt0 = jnp.exp(jax.random.uniform(ks[8], (DEPTH, 2, SSD_HEADS), f32,
                                     minval=math.log(1e-3), maxval=math.log(1e-1)))
    ssd_dt_bias = dt0 + jnp.log(-jnp.expm1(-dt0))
    ssd_a_log = jnp.log(jax.random.uniform(ks[9], (DEPTH, 2, SSD_HEADS), f32, minval=1.0, maxval=16.0))
    return {
        "x_prompt": nrm(ks[0], (BATCH, SEQ, D_MODEL), 1.0),
        "x_sample": nrm(ks[1], (DEC_BATCH, DEC_SEQ, D_MODEL), 1.0),
        "p_prompt": nrm(ks[2], (DEPTH, BATCH, SEQ, PLE_DIM), 1.0),
        "p_sample": nrm(ks[3], (DEPTH, DEC_BATCH, DEC_SEQ, PLE_DIM), 1.0),
        "norm_mix_pre": gain(ks[4], (DEPTH, D_MODEL)),
        "w_in": nrm(ks[5], (DEPTH, D_MODEL, IN_COLS), D_MODEL ** -0.5),
        "ssd_conv_w": nrm(ks[6], (DEPTH, SSD_CONV, XBC_WIDTH), SSD_CONV ** -0.5),
        "ssd_conv_b": nrm(ks[7], (DEPTH, XBC_WIDTH), 0.02),
        "ssd_dt_bias": ssd_dt_bias,
        "ssd_a_log": ssd_a_log,
        "ssd_d": 1.0 + 0.1 * jax.random.normal(ks[10], (DEPTH, SSD_HEADS), f32),
        "ssd_norm_w": gain(ks[11], (DEPTH, SSD_WIDTH)),
        "hy_conv_w": nrm(ks[12], (DEPTH, HY_SHORT, 3 * HY_WIDTH), HY_SHORT ** -0.5),
        "hy_conv_b": nrm(ks[13], (DEPTH, 3 * HY_WIDTH), 0.02),
        "hy_f_w1": nrm(ks[14], (DEPTH, HY_EMB, HY_FILTER_HIDDEN), HY_EMB ** -0.5),
        "hy_f_b1": nrm(ks[15], (DEPTH, HY_FILTER_HIDDEN), 0.02),
        "hy_f_w2": nrm(ks[16], (DEPTH, HY_FILTER_HIDDEN, HY_FILTER_HIDDEN), HY_FILTER_HIDDEN ** -0.5),
        "hy_f_b2": nrm(ks[17], (DEPTH, HY_FILTER_HIDDEN), 0.02),
        "hy_f_w3": nrm(ks[18], (DEPTH, HY_FILTER_HIDDEN, HY_FILTER_HIDDEN), HY_FILTER_HIDDEN ** -0.5),
        "hy_f_b3": nrm(ks[19], (DEPTH, HY_FILTER_HIDDEN), 0.02),
        "hy_f_w4": nrm(ks[20], (DEPTH, HY_FILTER_HIDDEN, 2 * HY_WIDTH), HY_FILTER_HIDDEN ** -0.5),
        "hy_f_freq": 1.0 + 0.1 * jax.random.normal(ks[21], (DEPTH, HY_FILTER_HIDDEN), f32),
        "hy_bias": nrm(ks[22], (DEPTH, HY_WIDTH), 0.1),
        "hy_norm_w": gain(ks[23], (DEPTH, HY_WIDTH)),
        "w_out": nrm(ks[24], (DEPTH, MIX_WIDTH, D_MODEL), MIX_WIDTH ** -0.5),
        "norm_mix_post": gain(ks[25], (DEPTH, D_MODEL)),
        "norm_ffn_pre": gain(ks[26], (DEPTH, D_MODEL)),
        "w_gate": nrm(ks[27], (DEPTH, D_MODEL, FFN_HIDDEN), D_MODEL ** -0.5),
        "w_up": nrm(ks[28], (DEPTH, D_MODEL, FFN_HIDDEN), D_MODEL ** -0.5),
        "w_down": nrm(ks[29], (DEPTH, FFN_HIDDEN, D_MODEL), FFN_HIDDEN ** -0.5),
        "norm_ffn_post": gain(ks[30], (DEPTH, D_MODEL)),
        "ple_norm_pre": gain(ks[31], (DEPTH, D_MODEL)),
        "w_ple_gate": nrm(ks[32], (DEPTH, D_MODEL, D_MODEL), D_MODEL ** -0.5),
        "w_ple_proj": nrm(ks[33], (DEPTH, PLE_DIM, D_MODEL), PLE_DIM ** -0.5),
        "ple_norm_post": gain(ks[34], (DEPTH, D_MODEL)),
    }


def reference(x_prompt, x_sample, p_prompt, p_sample, norm_mix_pre, w_in, ssd_conv_w, ssd_conv_b,
              ssd_dt_bias, ssd_a_log, ssd_d, ssd_norm_w, hy_conv_w, hy_conv_b, hy_f_w1, hy_f_b1,
              hy_f_w2, hy_f_b2, hy_f_w3, hy_f_b3, hy_f_w4, hy_f_freq, hy_bias, hy_norm_w, w_out,
              norm_mix_post, norm_ffn_pre, w_gate, w_up, w_down, norm_ffn_post, ple_norm_pre,
              w_ple_gate, w_ple_proj, ple_norm_post):
    weights = (norm_mix_pre, w_in, ssd_conv_w, ssd_conv_b, ssd_dt_bias, ssd_a_log, ssd_d, ssd_norm_w,
               hy_conv_w, hy_conv_b, hy_f_w1, hy_f_b1, hy_f_w2, hy_f_b2, hy_f_w3, hy_f_b3, hy_f_w4,
               hy_f_freq, hy_bias, hy_norm_w, w_out, norm_mix_post, norm_ffn_pre, w_gate, w_up,
               w_down, norm_ffn_post, ple_norm_pre, w_ple_gate, w_ple_proj, ple_norm_post)
    y_prompt = run_trunk(x_prompt, p_prompt, weights)
    y_sample = run_trunk(x_sample, p_sample, weights)
    return (y_prompt, y_sample)
```

```python
import contextlib
import math
import numpy as np
import concourse.bass as bass
import concourse.mybir as mybir
from concourse.bass_utils import run_bass_kernel_spmd

F32 = mybir.dt.float32
BF16 = mybir.dt.bfloat16
AF = mybir.ActivationFunctionType
ALU = mybir.AluOpType
AX = mybir.AxisListType

D = 1024
KC = 8
PLE = 256
NH = 16
HP = 64
NS = 128
FFN = 2816
FC = 22
EPS = 1e-6
NEG = -30000.0


class Sched:
    NDMA = 40

    def __init__(self, nc, stack):
        self.nc = nc
        self.stack = stack
        self.gen = 0
        self.eng = {"pe": nc.tensor, "act": nc.scalar, "dve": nc.vector, "pool": nc.gpsimd, "sp": nc.sync}
        self.sem = {}
        self.cnt = {}
        for e in ("pe", "act", "dve", "pool"):
            self.sem[e] = stack.enter_context(nc.semaphore("sem_" + e))
            self.cnt[e] = 0
        self.dsem = [stack.enter_context(nc.semaphore("dsem%d" % i)) for i in range(self.NDMA)]
        self.dcnt = [0] * self.NDMA
        self.dnext = 0
        self.waited = {e: {} for e in self.eng}
        self.last_w = {}
        self.readers = {}
        self.nops = 0
        self.rr = 0
        self.rev = {}
        self.wgid = {}
        self.pre = {}
        self.gcount = 0

    def newgroup(self):
        self.gcount += 1
        return self.gcount

    def merge(self, new_key, keys):
        toks = []
        for k in keys:
            w = self.last_w.get(k)
            if w is not None:
                toks.extend(w if isinstance(w, list) else [w])
            self.rev.setdefault(k, set()).add(new_key)
        self.last_w[new_key] = toks
        self.readers[new_key] = []

    def _wait(self, e, tok):
        sem, val, name = tok
        if self.waited[e].get(name, 0) >= val:
            return
        self.waited[e][name] = val
        self.eng[e].wait_ge(sem, val)

    def _deps(self, e, reads, writes, wg=None):
        toks = []
        for k in reads:
            w = self.last_w.get(k)
            if w is not None:
                toks.extend(w if isinstance(w, list) else [w])
        for k in writes:
            if wg is not None and self.wgid.get(k) == wg:
                toks.extend(self.pre.get(k, ()))
                toks.extend(self.readers.get(k, ()))
                continue
            w = self.last_w.get(k)
            pre = []
            if w is not None:
                pre.extend(w if isinstance(w, list) else [w])
            pre.extend(self.readers.get(k, ()))
            for m in self.rev.get(k, ()):
                pre.extend(self.readers.get(m, ()))
            toks.extend(pre)
            if wg is not None:
                self.pre[k] = pre
        for t in toks:
            if t[2] == e and e == "pe":
                continue
            self._wait(e, t)

    def _commit(self, tok, reads, writes, wg=None):
        for k in reads:
            self.readers.setdefault(k, []).append(tok)
        for k in writes:
            if wg is not None and self.wgid.get(k) == wg:
                self.last_w[k].append(tok)
                continue
            self.last_w[k] = [tok] if wg is not None else tok
            self.readers[k] = []
            self.wgid[k] = wg

    def op(self, e, fn, reads=(), writes=(), wg=None):
        self._deps(e, reads, writes, wg)
        ins = fn(self.eng[e])
        self.cnt[e] += 1
        ins.then_inc(self.sem[e], 1)
        tok = (self.sem[e], self.cnt[e], e)
        self._commit(tok, reads, writes, wg)
        self.nops += 1
        return tok

    def dma(self, out, in_, reads=(), writes=(), q="sp", wg=None, **kw):
        i = self.dnext
        self.dnext = (self.dnext + 1) % self.NDMA
        name = "d%d" % i
        if self.dcnt[i] > 0:
            self._wait(q, (self.dsem[i], self.dcnt[i], name))
        self._deps(q, reads, writes, wg)
        ins = self.eng[q].dma_start(out=out, in_=in_, **kw)
        self.dcnt[i] += 16
        ins.then_inc(self.dsem[i], 16)
        tok = (self.dsem[i], self.dcnt[i], name)
        self._commit(tok, reads, writes, wg)
        self.nops += 1
        return tok

    def barrier(self):
        toks = [(self.sem[e], self.cnt[e], e) for e in ("pe", "act", "dve", "pool") if self.cnt[e] > 0]
        toks += [(self.dsem[i], self.dcnt[i], "d%d" % i) for i in range(self.NDMA) if self.dcnt[i] > 0]
        for e in self.eng:
            for t in toks:
                if t[2] == e:
                    continue
                self._wait(e, t)
        self.last_w = {}
        self.readers = {}
        self.rev = {}
        self.wgid = {}
        self.pre = {}
        self.gen += 1
        for e in ("pe", "act", "dve", "pool"):
            self.sem[e] = self.stack.enter_context(self.nc.semaphore("sem_%s_%d" % (e, self.gen)))
            self.cnt[e] = 0
            for e2 in self.eng:
                self.waited[e2][e] = 0

    def ev(self):
        self.rr += 1
        return "act" if self.rr % 2 else "dve"


class Ctx:
    pass


def build(NT, stages=("p0", "ssd", "hy", "wprep", "p2"), dbg=False, hy_stop=99):
    J = NT // 128
    NTH = NT + 4
    NFFT = 2 * NT
    CS = 128 // J
    NSUB = J
    nc = bass.Bass("TRN2", target_bir_lowering=False)
    st = contextlib.ExitStack()
    C = Ctx()
    C.nc = nc

    def din(name, shape, dt=F32):
        return nc.dram_tensor(name, list(shape), dt, kind="ExternalInput").ap()

    def dscr(name, shape, dt):
        kind = "ExternalOutput" if dbg else "Internal"
        return nc.dram_tensor(name, list(shape), dt, kind=kind).ap()

    I = {}
    for name, shape in [
        ("xo", (D, NTH)), ("xcs", (D, NTH)), ("xch", (D, NTH)), ("pT", (PLE, NT)),
        ("w_in_fm", (D, 4608)), ("w_in_tm", (D, 1056)), ("w_cs_fm", (D, 1280)), ("w_cs_tm", (D, 16)),
        ("w_ch_fm", (D, 2048)), ("nmp", (128, KC)),
        ("ssd_cw", (128, 12 * 5)), ("ssd_cw_ctx", (128, 10 * 5)), ("ssd_cb", (128, 12)), ("ssd_cb_ctx", (128, 10)),
        ("dtb", (1, 32)), ("alog", (1, 32)), ("dtb_ctx", (1, 16)), ("alog_ctx", (1, 16)),
        ("ssd_d", (1, 16)), ("ssd_nw", (128, 8)), ("sel", (128, 2)),
        ("hy_cw", (128, 24 * 3)), ("hy_cw_ctx", (128, 16 * 3)), ("hy_cb", (128, 24)),
        ("zt", (33, 2 * NFFT)), ("fw1", (33, 64)), ("fb1", (64, 1)), ("fw2", (64, 64)), ("fb2", (64, 1)),
        ("fw3", (64, 128)), ("fb3", (128, 1)), ("ffreq", (128, 1)), ("fw4s", (128, 1024)),
        ("mfb", (128, 2 * NFFT)), ("negt", (128, 4 * J)), ("absd", (128, 1024)),
        ("hy_bias", (1, 1024)), ("hy_nw", (128, 8)),
        ("ftab", (128, J * 4 * 128)), ("etab", (128, J * 2 * 128)), ("w2tab", (128, 3 * 128)),
        ("ident", (128, 128)), ("tri", (128, 128)), ("negmask", (128, 2 * 128)), ("selmat", (32, 32 * 128)),
        ("w_out", (2048, D)), ("nm_post", (128, KC)), ("nf_pre", (128, KC)), ("w_gate", (D, FFN)),
        ("w_up", (D, FFN)), ("w_down", (FFN, D)), ("nf_post", (128, KC)), ("ple_npre", (128, KC)),
        ("w_pg", (D, D)), ("w_pp", (PLE, D)), ("ple_npost", (128, KC)),
    ]:
        I[name] = din(name, shape)
    yT = nc.dram_tensor("yT", [D, NT], F32, kind="ExternalOutput").ap()

    pj_fm = dscr("pj_fm", (4608, NTH), BF16)
    pj_z = dscr("pj_z", (NT, 1024), BF16)
    pj_dt = dscr("pj_dt", (128, J * 32), F32)
    cs_fm = dscr("cs_fm", (1280, NTH), BF16)
    cs_dt = dscr("cs_dt", (128, J * 16), F32)
    ch_fm = dscr("ch_fm", (2048, NTH), BF16)
    mix = dscr("mix", (2048, NT), BF16)
    wg_s = dscr("wg_s", (FC, 128, KC * 128), BF16)
    wu_s = dscr("wu_s", (FC, 128, KC * 128), BF16)
    wd_s = dscr("wd_s", (KC, 128, FC * 128), BF16)
    wo_s = dscr("wo_s", (KC, 128, 16 * 128), BF16)
    wpg_s = dscr("wpg_s", (KC, 128, KC * 128), BF16)
    wpp_s = dscr("wpp_s", (KC, 128, 2 * 128), BF16)

    S = Sched(nc, st)
    C.S = S

    tcount = [0]

    def T(stack, name, shape, dt):
        tcount[0] += 1
        return stack.enter_context(nc.sbuf_tensor("sb%d_%s" % (tcount[0], name), list(shape), dt))

    ident_f = T(st, "ident_f", [128, 128], F32)
    ident_b = T(st, "ident_b", [128, 128], BF16)
    ones_b = T(st, "ones_b", [128, 128], BF16)
    psb = [st.enter_context(nc.psum_tensor("psb%d" % i, [128, 512], F32)) for i in range(8)]
    S.dma(ident_f[:, :], I["ident"][:, :], writes=["ident_f"])
    S.op("dve", lambda e: e.tensor_copy(out=ident_b[:, :], in_=ident_f[:, :]), reads=["ident_f"], writes=["ident_b"])
    S.op("pool", lambda e: e.memset(ones_b[:, :], 1.0), writes=["ones_b"])

    psn = [0]

    def bank(lo=0, hi=8):
        i = lo + psn[0] % (hi - lo)
        psn[0] += 1
        return i

    def rms_rstd(src, src_keys, w, rstd, rstd_key, sq, sq_key, banks=(6, 8)):
        S.op("act", lambda e: e.activation(out=sq[:, :, 0:w], in_=src[:, :, 0:w], func=AF.Square),
             reads=list(src_keys), writes=[sq_key])
        b = bank(*banks)
        for kc in range(KC):
            S.op("pe", lambda e, kc=kc: e.matmul(psb[b][:, 0:w], lhsT=ones_b[:, :], rhs=sq[:, kc, 0:w],
                                                 start=(kc == 0), stop=(kc == KC - 1)),
                 reads=[sq_key, "ones_b"], writes=[("ps", b)])
        S.op("act", lambda e: e.activation(out=rstd[:, 0:w], in_=psb[b][:, 0:w], func=AF.Ln, scale=1.0 / D, bias=eps_t[:, 0:1]),
             reads=[("ps", b), "eps_t"], writes=[rstd_key])
        S.op("act", lambda e: e.activation(out=rstd[:, 0:w], in_=rstd[:, 0:w], func=AF.Exp, scale=-0.5),
             reads=[rstd_key], writes=[rstd_key])

    eps_t = T(st, "eps_t", [128, 1], F32)
    S.op("pool", lambda e: e.memset(eps_t[:, :], EPS), writes=["eps_t"])
    C.rms_rstd = rms_rstd

    def wprep_gen(stack):
        wst = [T(stack, "wst%d" % i, [128, KC, 128], F32) for i in range(2)]
        wsb = [T(stack, "wsb%d" % i, [128, KC, 128], BF16) for i in range(2)]
        wc = 0
        for (src, nk, ncc, dst) in ((I["w_out"], 16, KC, wo_s), (I["w_gate"], KC, FC, wg_s), (I["w_up"], KC, FC, wu_s),
                                    (I["w_down"], FC, KC, wd_s), (I["w_pg"], KC, KC, wpg_s), (I["w_pp"], 2, KC, wpp_s)):
            for m in range(ncc):
                for k0 in range(0, nk, KC):
                    kn = min(KC, nk - k0)
                    i = wc % 2
                    wc += 1
                    S.dma(wst[i][:, 0:kn, :], src[k0 * 128:(k0 + kn) * 128, m * 128:(m + 1) * 128].rearrange("(kc p) c -> p kc c", p=128), writes=[("wst", i)])
                    en = ("act", "dve")[wc % 2]
                    if en == "act":
                        S.op("act", lambda e: e.activation(out=wsb[i][:, 0:kn, :], in_=wst[i][:, 0:kn, :], func=AF.Copy), reads=[("wst", i)], writes=[("wsb", i)])
                    else:
                        S.op("dve", lambda e: e.tensor_copy(out=wsb[i][:, 0:kn, :], in_=wst[i][:, 0:kn, :]), reads=[("wst", i)], writes=[("wsb", i)])
                    S.dma(dst[m, :, k0 * 128:(k0 + kn) * 128].rearrange("p (kc c) -> p kc c", c=128), wsb[i][:, 0:kn, :],
                          reads=[("wsb", i)], writes=[("wdst", id(dst), m, k0)])
                    yield

    wprep_inline = ("p0" in stages) and ("wprep" in stages)
    if "p0" in stages:
        with contextlib.ExitStack() as ph:
            u = T(ph, "u", [128, KC, NTH], BF16)
            xt = [T(ph, "xt%d" % i, [128, KC, 512], F32) for i in range(2)]
            sq = T(ph, "sq0", [128, KC, 512], BF16)
            rstd = T(ph, "rstd0", [128, 512], F32)
            nmp = T(ph, "nmp", [128, KC], F32)
            NWB = 4
            wf = [T(ph, "wf%d" % i, [128, KC, 128], F32) for i in range(NWB)] + [T(ph, "wfL", [128, KC, 512], F32)]
            wb = [T(ph, "wb%d" % i, [128, KC, 128], BF16) for i in range(NWB)] + [T(ph, "wbL", [128, KC, 512], BF16)]
            stg = [T(ph, "stg%d" % i, [128, NTH], BF16) for i in range(2)]
            stz = [T(ph, "stz%d" % i, [128, 512], BF16) for i in range(2)]
            std = [T(ph, "std%d" % i, [128, 32], F32) for i in range(2)]
            S.dma(nmp[:, :], I["nmp"][:, :], writes=["nmp"])
            wgen = wprep_gen(ph) if wprep_inline else iter(())
            tiles = [(o, 512) for o in range(0, NT, 512)] + [(NT, 4)]
            wn = [0]
            sn = [0]

            def load_w(src, c0, ncol):
                if ncol > 128:
                    i = NWB
                else:
                    i = wn[0] % NWB
                wn[0] += 1
                S.dma(wf[i][:, :, 0:ncol], src[:, c0:c0 + ncol].rearrange("(kc p) c -> p kc c", p=128),
                      writes=[("wf", i)])
                eng = "pool"
                S.op(eng, lambda e: e.tensor_copy(out=wb[i][:, :, 0:ncol], in_=wf[i][:, :, 0:ncol]),
                     reads=[("wf", i)], writes=[("wb", i)])
                return i

            def token_set(xsrc, fm_w, fm_dst, tm_w, tm_cols, tm_dsts):
                for ti, (o, w) in enumerate(tiles):
                    xi = ti % 2
                    S.dma(xt[xi][:, :, 0:w], xsrc[:, o:o + w].rearrange("(kc p) t -> p kc t", p=128),
                          writes=[("xt", xi)])
                    rms_rstd(xt[xi], [("xt", xi)], w, rstd, "n0rstd", sq, "n0sq")
                    for kc in range(KC):
                        S.op("dve", lambda e, kc=kc: e.scalar_tensor_tensor(
                            out=u[:, kc, o:o + w], in0=xt[xi][:, kc, 0:w], scalar=nmp[:, kc:kc + 1],
                            in1=rstd[:, 0:w], op0=ALU.mult, op1=ALU.mult),
                            reads=[("xt", xi), "nmp", "n0rstd"], writes=[("u", ti, kc)])
                ncols = fm_w.shape[1]
                ng = ncols // 128
                pre = {}
                for g in range(min(2, ng)):
                    pre[g] = load_w(fm_w, g * 128, 128)
                for g in range(ng):
                    if g + 2 < ng:
                        pre[g + 2] = load_w(fm_w, (g + 2) * 128, 128)
                    wi = pre.pop(g)
                    si = sn[0] % 2
                    sn[0] += 1
                    for ti, (o, w) in enumerate(tiles):
                        b = bank(0, 6)
                        for kc in range(KC):
                            S.op("pe", lambda e, kc=kc: e.matmul(psb[b][:, 0:w], lhsT=wb[wi][:, kc, 0:128], rhs=u[:, kc, o:o + w],
                                                                 start=(kc == 0), stop=(kc == KC - 1)),
                                 reads=[("wb", wi), ("u", ti, kc)], writes=[("ps", b)])
                        en = S.ev()
                        if en == "act":
                            S.op("act", lambda e: e.activation(out=stg[si][:, o:o + w], in_=psb[b][:, 0:w], func=AF.Copy),
                                 reads=[("ps", b)], writes=[("stg", si, ti)])
                        else:
                            S.op("dve", lambda e: e.tensor_copy(out=stg[si][:, o:o + w], in_=psb[b][:, 0:w]),
                                 reads=[("ps", b)], writes=[("stg", si, ti)])
                    for _ in range(2):
                        next(wgen, None)
                    S.dma(fm_dst[g * 128:(g + 1) * 128, :], stg[si][:, :],
                          reads=[("stg", si, ti) for ti in range(len(tiles))], writes=[("fm", id(fm_dst), g)])
                if tm_w is not None:
                    c0 = 0
                    for (ncol, dst, dcol0, stt) in tm_cols:
                        wi = load_w(tm_w, c0, ncol)
                        for blk in range(NT // 128):
                            b = bank(0, 6)
                            ti = (blk * 128) // 512
                            for kc in range(KC):
                                S.op("pe", lambda e, kc=kc: e.matmul(psb[b][:, 0:ncol], lhsT=u[:, kc, 2 + blk * 128:2 + (blk + 1) * 128],
                                                                     rhs=wb[wi][:, kc, 0:ncol], start=(kc == 0), stop=(kc == KC - 1)),
                                     reads=[("wb", wi), ("u", ti, kc), ("u", min(ti + 1, len(tiles) - 1), kc)], writes=[("ps", b)])
                            si = sn[0] % 2
                            sn[0] += 1
                            stile = stz if stt == "z" else std
                            en = S.ev()
                            if en == "act":
                                S.op("act", lambda e: e.activation(out=stile[si][:, 0:ncol], in_=psb[b][:, 0:ncol], func=AF.Copy),
                                     reads=[("ps", b)], writes=[("st" + stt, si)])
                            else:
                                S.op("dve", lambda e: e.tensor_copy(out=stile[si][:, 0:ncol], in_=psb[b][:, 0:ncol]),
                                     reads=[("ps", b)], writes=[("st" + stt, si)])
                            dap = dst[:, blk * ncol:(blk + 1) * ncol] if stt == "d" else dst[blk * 128:(blk + 1) * 128, dcol0:dcol0 + ncol]
                            S.dma(dap, stile[si][:, 0:ncol],
                                  reads=[("st" + stt, si)], writes=[("tm", id(dst), blk, dcol0)])
                        c0 += ncol

            token_set(I["xo"], I["w_in_fm"], pj_fm, I["w_in_tm"],
                      [(512, pj_z, 0, "z"), (512, pj_z, 512, "z"), (32, pj_dt, 0, "d")], None)
            token_set(I["xcs"], I["w_cs_fm"], cs_fm, I["w_cs_tm"], [(16, cs_dt, 0, "d")], None)
            token_set(I["xch"], I["w_ch_fm"], ch_fm, None, [], None)
            for _ in wgen:
                pass
            S.barrier()

    TW = min(512, NT)
    if "ssd" in stages:
        with contextlib.ExitStack() as ph:
            tri_f = T(ph, "tri_f", [128, 128], F32)
            ones_f = T(ph, "ones_f", [128, 128], F32)
            nmask_f = T(ph, "nmask_f", [128, 2, 128], F32)
            nmask = T(ph, "nmask", [128, 2, 128], BF16)
            selm = T(ph, "selm", [16, 16, 128], F32)
            cw = T(ph, "cw", [128, 12, 5], F32)
            cwc = T(ph, "cwc", [128, 10, 5], F32)
            cb = T(ph, "cb", [128, 12], F32)
            cbc = T(ph, "cbc", [128, 10], F32)
            dtb_bc = T(ph, "dtb_bc", [128, 32], F32)
            a_bc = T(ph, "a_bc", [128, 32], F32)
            dtbc_bc = T(ph, "dtbc_bc", [128, 16], F32)
            ac_bc = T(ph, "ac_bc", [128, 16], F32)
            d_bc = T(ph, "d_bc", [128, 16], F32)
            ddiag = T(ph, "ddiag", [128, 16, 128], BF16)
            nw = T(ph, "nw", [128, 8], F32)
            selt = T(ph, "selt", [128, 2], F32)
            dg = T(ph, "dg", [128, 5, 128], BF16)
            cin = [T(ph, "cin%d" % i, [128, NTH], BF16) for i in range(1)]
            ctmp = T(ph, "ctmp", [128, NT], BF16)
            Bf = T(ph, "Bf", [128, NT], BF16)
            Cf = T(ph, "Cf", [128, NT], BF16)
            x_tok = T(ph, "x_tok", [128, J, 512], BF16)
            B_tok = T(ph, "B_tok", [128, J, 128], BF16)
            dtr = T(ph, "dtr", [128, J, 32], F32)
            dtrc = T(ph, "dtrc", [128, J, 16], F32)
            dt_t = T(ph, "dt_t", [128, J, 16], F32)
            a_t = T(ph, "a_t", [128, J, 16], F32)
            lam = T(ph, "lam", [128, J, 16], F32)
            tot = T(ph, "tot", [128, J, 16], F32)
            tml = T(ph, "tml", [128, J, 16], F32)
            e1 = T(ph, "e1", [128, J, 16], F32)
            e2 = T(ph, "e2", [128, J, 16], F32)
            dec = T(ph, "dec", [128, J, 16], F32)
            w_t = T(ph, "w_t", [128, J, 16], F32)
            q_t = T(ph, "q_t", [128, J, 16], F32)
            qTc = [T(ph, "qTc%d" % i, [16, 128], F32) for i in range(2)]
            selbc = T(ph, "selbc", [16, 4, 4, 128], F32)
            nm4 = T(ph, "nm4", [128, 4, 128], BF16)
            xdt = [T(ph, "xdt%d" % i, [128, 512], BF16) for i in range(4)]
            stb_all = T(ph, "stb_all", [128, J, 512], BF16)
            st_f = T(ph, "st_f", [128, 512], F32)
            st_b = T(ph, "st_b", [128, 512], F32)
            st_c = T(ph, "st_c", [128, 512], F32)
            stf_bf_l = [T(ph, "stf_bf%d" % i, [128, 512], BF16) for i in range(2)]
            xw = [T(ph, "xw%d" % i, [128, 512], BF16) for i in range(2)]
            scT_l = [T(ph, "scT%d" % i, [128, 128], F32) for i in range(2)]
            Et = [T(ph, "Et%d" % i, [128, 4, 128], F32) for i in range(2)]
            MT = [T(ph, "MT%d" % i, [128, 4, 128], BF16) for i in range(2)]
            zt_ = [T(ph, "zt%d" % i, [128, 512], BF16) for i in range(2)]
            zs_l = [T(ph, "zs%d" % i, [128, 512], F32) for i in range(2)]
            ysb_l = [T(ph, "ysb%d" % i, [128, 512], F32) for i in range(2)]
            yt1_l = [T(ph, "yt1%d" % i, [128, 512], F32) for i in range(2)]
            yn_l = [T(ph, "yn%d" % i, [128, 512], BF16) for i in range(2)]
            ssq_l = [T(ph, "ssq%d" % i, [128, 1], F32) for i in range(2)]
            sqj_l = [T(ph, "sqj%d" % i, [128, 512], BF16) for i in range(2)]
            dummy = T(ph, "dummy", [128, 1], F32)
            SC = min(4, J)
            mst = [T(ph, "mst%d" % i, [128, 4, SC * 128], BF16) for i in range(2)]

            S.dma(tri_f[:, :], I["tri"][:, :], writes=["tri_f"])
            S.op("pool", lambda e: e.memset(ones_f[:, :], 1.0), writes=["ones_f"])
            S.dma(nmask_f[:, :, :], I["negmask"].rearrange("p (d t) -> p d t", d=2), writes=["nmask_f"])
            S.op("dve", lambda e: e.tensor_copy(out=nmask[:, :, :], in_=nmask_f[:, :, :]), reads=["nmask_f"], writes=["nmask"])
            S.dma(selm[:, :, :], I["selmat"][0:16, :].rearrange("k (h s) -> k h s", s=128)[:, 0:16, :], writes=["selm"])
            S.op("pool", lambda e: e.tensor_scalar(out=selm[:, :, :], in0=selm[:, :, :], scalar1=-1.0, scalar2=None, op0=ALU.mult),
                 reads=["selm"], writes=["selm"])
            for quad in range(4):
                for i_ in range(4):
                    col_ = (i_ % 2) * 8 + 2 * quad + i_ // 2
                    S.op("dve", lambda e, quad=quad, i_=i_, col_=col_: e.tensor_scalar(out=selbc[:, quad, i_, :], in0=selm[:, col_, :], scalar1=-1.0, scalar2=None, op0=ALU.mult),
                         reads=["selm"], writes=[("selbc", quad, i_)])
            S.merge("selbc", [("selbc", q_, i_) for q_ in range(4) for i_ in range(4)])
            for i_ in range(4):
                S.op("dve", lambda e, i_=i_: e.tensor_copy(out=nm4[:, i_, :], in_=nmask[:, i_ % 2, :]), reads=["nmask"], writes=[("nm4", i_)])
            S.merge("nm4", [("nm4", i_) for i_ in range(4)])
            S.dma(cw[:, :, :], I["ssd_cw"].rearrange("p (g j) -> p g j", j=5), writes=["cw"])
            S.dma(cwc[:, :, :], I["ssd_cw_ctx"].rearrange("p (g j) -> p g j", j=5), writes=["cwc"])
            S.dma(cb[:, :], I["ssd_cb"][:, :], writes=["cb"])
            S.dma(cbc[:, :], I["ssd_cb_ctx"][:, :], writes=["cbc"])
            S.dma(dtb_bc[:, :], I["dtb"][0, :].partition_broadcast(128), writes=["dtb_bc"])
            S.dma(a_bc[:, :], I["alog"][0, :].partition_broadcast(128), writes=["a_bc"])
            S.dma(dtbc_bc[:, :], I["dtb_ctx"][0, :].partition_broadcast(128), writes=["dtbc_bc"])
            S.dma(ac_bc[:, :], I["alog_ctx"][0, :].partition_broadcast(128), writes=["ac_bc"])
            S.dma(d_bc[:, :], I["ssd_d"][0, :].partition_broadcast(128), writes=["d_bc"])
            S.dma(nw[:, :], I["ssd_nw"][:, :], writes=["nw"])
            S.dma(selt[:, :], I["sel"][:, :], writes=["selt"])
            S.dma(dtr[:, :, :], pj_dt.rearrange("p (c k) -> p c k", k=32), writes=["dtr"])
            S.dma(dtrc[:, :, :], cs_dt.rearrange("p (c k) -> p c k", k=16), writes=["dtrc"])
            for t_, k_ in ((a_bc, "a_bc"), (ac_bc, "ac_bc")):
                S.op("act", lambda e, t_=t_: e.activation(out=t_[:, :], in_=t_[:, :], func=AF.Exp), reads=[k_], writes=[k_])
                S.op("dve", lambda e, t_=t_: e.tensor_scalar(out=t_[:, :], in0=t_[:, :], scalar1=-1.0, scalar2=None, op0=ALU.mult),
                     reads=[k_], writes=[k_])
            for h in range(16):
                S.op("dve", lambda e, h=h: e.tensor_scalar(out=ddiag[:, h, :], in0=ident_f[:, :], scalar1=d_bc[:, h:h + 1], scalar2=None, op0=ALU.mult),
                     reads=["ident_f", "d_bc"], writes=[("ddiag", h)])
            cn = [0]

            def conv_group(src, row0, wt, wkey, g, bt, bkey, dst, dkey):
                ci = 0
                S.dma(cin[ci][:, :], src[row0:row0 + 128, :], writes=[("cin", ci)])
                for j in range(5):
                    S.op("pool", lambda e, j=j: e.tensor_scalar(out=dg[:, j, :], in0=ident_f[:, :], scalar1=wt[:, g, j:j + 1], scalar2=None, op0=ALU.mult),
                         reads=["ident_f", wkey], writes=[("dg", j)])
                for o in range(0, NT, TW):
                    b = bank(6, 8)
                    for j in range(5):
                        S.op("pe", lambda e, j=j: e.matmul(psb[b][:, 0:TW], lhsT=dg[:, j, :], rhs=cin[ci][:, o + j:o + j + TW],
                                                           start=(j == 0), stop=(j == 4)),
                             reads=[("dg", j), ("cin", ci)], writes=[("ps", b)])
                    S.op("act", lambda e: e.activation(out=dst[:, o:o + TW], in_=psb[b][:, 0:TW], func=AF.Silu, bias=bt[:, g:g + 1]),
                         reads=[("ps", b), bkey], writes=[(dkey, o // TW)])
                return [(dkey, o // TW) for o in range(0, NT, TW)]

            def to_tok(srct, skeys, dst, dcol0, dkey):
                for c0 in range(0, J, 8):
                    nb = min(8, J - c0)
                    b = bank(6, 8)
                    pv = psb[b][:, :].bitcast(BF16).rearrange("p (a b) -> p a b", b=128)
                    for i in range(nb):
                        S.op("pe", lambda e, i=i: e.transpose(out=pv[:, i, :], in_=srct[:, (c0 + i) * 128:(c0 + i + 1) * 128], identity=ident_b[:, :]),
                             reads=list(skeys) + ["ident_b"], writes=[("ps", b)])
                    en = S.ev()
                    if en == "act":
                        S.op("act", lambda e: e.activation(out=dst[:, c0:c0 + nb, dcol0:dcol0 + 128], in_=pv[:, 0:nb, :], func=AF.Copy),
                             reads=[("ps", b)], writes=[(dkey, c0, dcol0)])
                    else:
                        S.op("dve", lambda e: e.tensor_copy(out=dst[:, c0:c0 + nb, dcol0:dcol0 + 128], in_=pv[:, 0:nb, :]),
                             reads=[("ps", b)], writes=[(dkey, c0, dcol0)])

            def dt_math(raw, rkey, segs, bias_t, bkey, a_src, akey, ncol, bwd_cols):
                nn = ncol
                for (d0, s0, n_) in segs:
                    S.op("dve", lambda e, d0=d0, s0=s0, n_=n_: e.tensor_tensor(
                        out=dt_t[:, :, d0:d0 + n_], in0=raw[:, :, s0:s0 + n_],
                        in1=bias_t[:, s0:s0 + n_].unsqueeze(1).broadcast_to([128, J, n_]), op=ALU.add),
                        reads=[rkey, bkey], writes=["dt_t"])
                S.op("act", lambda e: e.activation(out=dt_t[:, :, 0:nn], in_=dt_t[:, :, 0:nn], func=AF.Exp), reads=["dt_t"], writes=["dt_t"])
                S.op("act", lambda e: e.activation(out=dt_t[:, :, 0:nn], in_=dt_t[:, :, 0:nn], func=AF.Ln, bias=ones_f[:, 0:1]),
                     reads=["dt_t", "ones_f"], writes=["dt_t"])
                for (d0, s0, n_) in segs:
                    S.op("dve", lambda e, d0=d0, s0=s0, n_=n_: e.tensor_tensor(
                        out=a_t[:, :, d0:d0 + n_], in0=dt_t[:, :, d0:d0 + n_],
                        in1=a_src[:, s0:s0 + n_].unsqueeze(1).broadcast_to([128, J, n_]), op=ALU.mult),
                        reads=["dt_t", akey], writes=["a_t"])
                b1 = 4
                b2 = 5
                for c in range(J):
                    S.op("pe", lambda e, c=c: e.matmul(psb[b1][:, c * 16:c * 16 + nn], lhsT=tri_f[:, :], rhs=a_t[:, c, 0:nn], start=True, stop=True),
                         reads=["tri_f", "a_t"], writes=[("ps", b1)])
                    S.op("pe", lambda e, c=c: e.matmul(psb[b2][:, c * 16:c * 16 + nn], lhsT=ones_f[:, :], rhs=a_t[:, c, 0:nn], start=True, stop=True),
                         reads=["ones_f", "a_t"], writes=[("ps", b2)])
                pv1 = psb[b1][:, 0:J * 16].rearrange("p (c k) -> p c k", k=16)
                pv2 = psb[b2][:, 0:J * 16].rearrange("p (c k) -> p c k", k=16)
                S.op("dve", lambda e: e.tensor_copy(out=lam[:, :, 0:nn], in_=pv1[:, :, 0:nn]), reads=[("ps", b1)], writes=["lam"])
                S.op("act", lambda e: e.activation(out=tot[:, :, 0:nn], in_=pv2[:, :, 0:nn], func=AF.Copy), reads=[("ps", b2)], writes=["tot"])
                if bwd_cols:
                    lo, hi = bwd_cols
                    S.op("dve", lambda e: e.tensor_tensor(out=lam[:, :, lo:hi], in0=lam[:, :, lo:hi], in1=a_t[:, :, lo:hi], op=ALU.subtract),
                         reads=["lam", "a_t"], writes=["lam"])
                S.op("dve", lambda e: e.tensor_tensor(out=tml[:, :, 0:nn], in0=tot[:, :, 0:nn], in1=lam[:, :, 0:nn], op=ALU.subtract),
                     reads=["tot", "lam"], writes=["tml"])
                S.op("act", lambda e: e.activation(out=e1[:, :, 0:nn], in_=lam[:, :, 0:nn], func=AF.Exp), reads=["lam"], writes=["e1"])
                S.op("act", lambda e: e.activation(out=e2[:, :, 0:nn], in_=tml[:, :, 0:nn], func=AF.Exp), reads=["tml"], writes=["e2"])
                S.op("act", lambda e: e.activation(out=dec[:, :, 0:nn], in_=tot[:, :, 0:nn], func=AF.Exp), reads=["tot"], writes=["dec"])
                fhi = bwd_cols[0] if bwd_cols else nn
                S.op("dve", lambda e: e.tensor_tensor(out=w_t[:, :, 0:fhi], in0=dt_t[:, :, 0:fhi], in1=e2[:, :, 0:fhi], op=ALU.mult),
                     reads=["dt_t", "e2"], writes=["w_t"])
                S.op("dve", lambda e: e.tensor_scalar(out=q_t[:, :, 0:fhi], in0=lam[:, :, 0:fhi], scalar1=-1.0, scalar2=None, op0=ALU.mult),
                     reads=["lam"], writes=["q_t"])
                if bwd_cols:
                    lo, hi = bwd_cols
                    S.op("dve", lambda e: e.tensor_tensor(out=w_t[:, :, lo:hi], in0=dt_t[:, :, lo:hi], in1=e1[:, :, lo:hi], op=ALU.mult),
                         reads=["dt_t", "e1", "w_t"], writes=["w_t"])
                    S.op("dve", lambda e: e.tensor_copy(out=q_t[:, :, lo:hi], in_=lam[:, :, lo:hi]), reads=["lam", "q_t"], writes=["q_t"])

            def make_xw(c, col0, xi):
                S.op("pool", lambda e: e.tensor_tensor(out=xw[xi][:, :].rearrange("p (h d) -> p h d", d=64),
                                                       in0=x_tok[:, c, :].rearrange("p (h d) -> p h d", d=64),
                                                       in1=w_t[:, c, col0:col0 + 8].unsqueeze(2).broadcast_to([128, 8, 64]), op=ALU.mult),
                     reads=["x_tok_all", "w_t"], writes=[("xw", xi)])

            def state_step(c, col0, xi, stt, skey, pb):
                S.op("pe", lambda e: e.matmul(psb[pb][:, :], lhsT=B_tok[:, c, :], rhs=xw[xi][:, :], start=True, stop=True),
                     reads=["B_tok_all", ("xw", xi)], writes=[("ps", pb)])
                S.op("dve", lambda e: e.tensor_tensor(out=stt[:, :].rearrange("p (h d) -> p h d", d=64),
                                                      in0=stt[:, :].rearrange("p (h d) -> p h d", d=64),
                                                      in1=dec[:, c, col0:col0 + 8].unsqueeze(2).broadcast_to([128, 8, 64]), op=ALU.mult),
                     reads=[skey, "dec"], writes=[skey])
                S.op("dve", lambda e: e.tensor_tensor(out=stt[:, :], in0=stt[:, :], in1=psb[pb][:, :], op=ALU.add),
                     reads=[skey, ("ps", pb)], writes=[skey])

            for g2 in range(2):
                keys = conv_group(cs_fm, 1024 + g2 * 128, cwc, "cwc", 8 + g2, cbc, "cbc", ctmp, "ctmp")
                to_tok(ctmp, keys, B_tok, 0, "B_tok")
                for xg in range(4):
                    keys = conv_group(cs_fm, (g2 * 4 + xg) * 128, cwc, "cwc", g2 * 4 + xg, cbc, "cbc", ctmp, "ctmp")
                    to_tok(ctmp, keys, x_tok, xg * 128, "x_tok")
                S.op("pool", lambda e: e.memset(st_c[:, :], 0.0), writes=["st_c"])
                S.merge("x_tok_all", [("x_tok", c0, x0) for c0 in range(0, J, 8) for x0 in range(0, 512, 128)])
                S.merge("B_tok_all", [("B_tok", c0, 0) for c0 in range(0, J, 8)])
                dt_math(dtrc, "dtrc", [(0, g2 * 8, 8)], dtbc_bc, "dtbc_bc", ac_bc, "ac_bc", 8, None)
                for c in range(J):
                    make_xw(c, 0, c % 2)
                    state_step(c, 0, c % 2, st_c, "st_c", 6 + c % 2)
                S.op("dve", lambda e: e.tensor_scalar(out=st_f[:, :], in0=st_c[:, :], scalar1=selt[:, 0:1], scalar2=None, op0=ALU.mult),
                     reads=["st_c", "selt"], writes=["st_f"])
                S.op("dve", lambda e: e.tensor_scalar(out=st_b[:, :], in0=st_c[:, :], scalar1=selt[:, 1:2], scalar2=None, op0=ALU.mult),
                     reads=["st_c", "selt"], writes=["st_b"])
                kB = conv_group(pj_fm, 1024 + g2 * 128, cw, "cw", 8 + g2, cb, "cb", Bf, "Bf")
                to_tok(Bf, kB, B_tok, 0, "B_tok")
                kC = conv_group(pj_fm, 1280 + g2 * 128, cw, "cw", 10 + g2, cb, "cb", Cf, "Cf")
                for xg in range(4):
                    keys = conv_group(pj_fm, (g2 * 4 + xg) * 128, cw, "cw", g2 * 4 + xg, cb, "cb", ctmp, "ctmp")
                    to_tok(ctmp, keys, x_tok, xg * 128, "x_tok")
                S.merge("x_tok_all", [("x_tok", c0, x0) for c0 in range(0, J, 8) for x0 in range(0, 512, 128)])
                S.merge("B_tok_all", [("B_tok", c0, 0) for c0 in range(0, J, 8)])
                S.merge("BC_all", kB + kC)
                dt_math(dtr, "dtr", [(0, g2 * 8, 8), (8, 16 + g2 * 8, 8)], dtb_bc, "dtb_bc", a_bc, "a_bc", 16, (8, 16))
                for c in range(J - 1, -1, -1):
                    S.op("act", lambda e, c=c: e.activation(out=stb_all[:, c, :], in_=st_b[:, :], func=AF.Copy),
                         reads=["st_b"], writes=[("stb", c)])
                    make_xw(c, 8, c % 2)
                    state_step(c, 8, c % 2, st_b, "st_b", 6 + c % 2)
                def fwdA(c):
                    tsl = slice(c * 128, (c + 1) * 128)
                    zi = c % 2
                    scT, zs, ysb, yt1, yn, ssq, sqj, stf_bf = (scT_l[zi], zs_l[zi], ysb_l[zi], yt1_l[zi], yn_l[zi], ssq_l[zi], sqj_l[zi], stf_bf_l[zi])
                    kx = lambda k_: (k_, zi)
                    qi = c % 2
                    S.dma(zt_[zi][:, :], pj_z[c * 128:(c + 1) * 128, g2 * 512:(g2 + 1) * 512], writes=[("z", zi)])
                    S.op("act", lambda e: e.activation(out=stf_bf[:, :], in_=st_f[:, :], func=AF.Copy), reads=["st_f"], writes=[kx("stf_bf")])
                    S.op("pe", lambda e: e.matmul(psb[0][:, 0:128], lhsT=Bf[:, tsl], rhs=Cf[:, tsl], start=True, stop=True),
                         reads=["BC_all"], writes=[("ps", 0)])
                    S.op("act", lambda e: e.activation(out=scT[:, :], in_=psb[0][:, 0:128], func=AF.Copy), reads=[("ps", 0)], writes=[kx("scT")])
                    qi = c % 2
                    S.op("pe", lambda e: e.transpose(out=psb[0][0:16, 128:256], in_=q_t[:, c, :], identity=ident_f[:, :]),
                         reads=["q_t", "ident_f", kx("scT")], writes=[("ps", 0)])
                    S.op("act", lambda e: e.activation(out=qTc[qi][:, :], in_=psb[0][0:16, 128:256], func=AF.Copy),
                         reads=[("ps", 0)], writes=[("qTc", qi)])
                    for d_ in range(2):
                        xi_ = qi * 2 + d_
                        S.op("pool", lambda e, d_=d_, xi_=xi_: e.tensor_tensor(out=xdt[xi_][:, :].rearrange("p (h d) -> p h d", d=64),
                                                                               in0=x_tok[:, c, :].rearrange("p (h d) -> p h d", d=64),
                                                                               in1=dt_t[:, c, d_ * 8:d_ * 8 + 8].unsqueeze(2).broadcast_to([128, 8, 64]), op=ALU.mult),
                             reads=["x_tok_all", "dt_t"], writes=[("xdt", xi_)])
                    for quad in range(4):
                        rb = 1 + quad % 2
                        ei = quad % 2
                        S.op("pe", lambda e, quad=quad: e.matmul(psb[rb][:, :], lhsT=qTc[qi][:, :], rhs=selbc[:, quad, :, :].rearrange("k i t -> k (i t)"), start=True, stop=False),
                             reads=[("qTc", qi), "selbc"], writes=[("ps", rb)])
                        S.op("pe", lambda e: e.matmul(psb[rb][:, :], lhsT=ident_b[:, :], rhs=nm4[:, :, :].rearrange("p i t -> p (i t)"), start=False, stop=False),
                             reads=["ident_b", "nm4"], writes=[("ps", rb)])
                        for i_ in range(4):
                            col = (i_ % 2) * 8 + 2 * quad + i_ // 2
                            S.op("pe", lambda e, i_=i_, col=col: e.matmul(psb[rb][:, i_ * 128:(i_ + 1) * 128], lhsT=selm[:, col, :], rhs=qTc[qi][:, :], start=False, stop=(i_ == 3)),
                                 reads=["selm", ("qTc", qi)], writes=[("ps", rb)])
                        S.op("act", lambda e: e.activation(out=Et[ei][:, :, :].rearrange("p i t -> p (i t)"), in_=psb[rb][:, :], func=AF.Exp),
                             reads=[("ps", rb)], writes=[("Et", ei)])
                        S.op("pool", lambda e: e.tensor_tensor(out=MT[ei][:, :, :], in0=Et[ei][:, :, :], in1=scT[:, :].unsqueeze(1).broadcast_to([128, 4, 128]), op=ALU.mult),
                             reads=[("Et", ei), kx("scT")], writes=[("MT", ei)])
                        for hh_ in range(2):
                            hl = 2 * quad + hh_
                            h = g2 * 8 + hl
                            osl = slice(hl * 64, (hl + 1) * 64)
                            S.op("pe", lambda e, hh_=hh_, osl=osl: e.matmul(psb[3][:, osl], lhsT=MT[ei][:, hh_ * 2, :], rhs=xdt[qi * 2][:, osl], start=True, stop=False),
                                 reads=[("MT", ei), ("xdt", qi * 2)], writes=[("ps", 3)])
                            S.op("pe", lambda e, hh_=hh_, osl=osl: e.matmul(psb[3][:, osl], lhsT=MT[ei][:, hh_ * 2 + 1, :], rhs=xdt[qi * 2 + 1][:, osl], start=False, stop=False),
                                 reads=[("MT", ei), ("xdt", qi * 2 + 1)], writes=[("ps", 3)])
                            S.op("pe", lambda e, h=h, osl=osl: e.matmul(psb[3][:, osl], lhsT=ddiag[:, h, :], rhs=x_tok[:, c, osl], start=False, stop=True),
                                 reads=[("ddiag", h), "x_tok_all"], writes=[("ps", 3)])
                    S.op("pe", lambda e: e.matmul(psb[4][:, :], lhsT=Cf[:, tsl], rhs=stf_bf[:, :], start=True, stop=True),
                         reads=["BC_all", kx("stf_bf")], writes=[("ps", 4)])
                    S.op("pe", lambda e: e.matmul(psb[5][:, :], lhsT=Cf[:, tsl], rhs=stb_all[:, c, :], start=True, stop=True),
                         reads=["BC_all", ("stb", c)], writes=[("ps", 5)])
                def fwdC(c):
                    tsl = slice(c * 128, (c + 1) * 128)
                    zi = c % 2
                    scT, zs, ysb, yt1, yn, ssq, sqj, stf_bf = (scT_l[zi], zs_l[zi], ysb_l[zi], yt1_l[zi], yn_l[zi], ssq_l[zi], sqj_l[zi], stf_bf_l[zi])
                    kx = lambda k_: (k_, zi)
                    qi = c % 2
                    S.op("act", lambda e: e.activation(out=ysb[:, :], in_=psb[3][:, :], func=AF.Copy), reads=[("ps", 3)], writes=[kx("ysb")])
                    for d_, pb in ((0, 4), (1, 5)):
                        osrc = e1 if d_ == 0 else e2
                        okey = "e1" if d_ == 0 else "e2"
                        S.op("dve", lambda e, pb=pb, osrc=osrc, d_=d_: e.tensor_tensor(
                            out=yt1[:, :].rearrange("p (h d) -> p h d", d=64), in0=psb[pb][:, :].rearrange("p (h d) -> p h d", d=64),
                            in1=osrc[:, c, d_ * 8:d_ * 8 + 8].unsqueeze(2).broadcast_to([128, 8, 64]), op=ALU.mult),
                            reads=[("ps", pb), okey], writes=[kx("yt1")])
                        S.op("dve", lambda e: e.tensor_tensor(out=ysb[:, :], in0=ysb[:, :], in1=yt1[:, :], op=ALU.add),
                             reads=[kx("ysb"), kx("yt1")], writes=[kx("ysb")])
                    make_xw(c, 0, c % 2)
                    state_step(c, 0, c % 2, st_f, "st_f", 6 + c % 2)
                def fwdB(c):
                    tsl = slice(c * 128, (c + 1) * 128)
                    zi = c % 2
                    scT, zs, ysb, yt1, yn, ssq, sqj, stf_bf = (scT_l[zi], zs_l[zi], ysb_l[zi], yt1_l[zi], yn_l[zi], ssq_l[zi], sqj_l[zi], stf_bf_l[zi])
                    kx = lambda k_: (k_, zi)
                    qi = c % 2
                    S.op("act", lambda e: e.activation(out=zs[:, :], in_=zt_[zi][:, :], func=AF.Silu), reads=[("z", zi)], writes=[kx("zs")])
                    S.op("dve", lambda e: e.tensor_tensor(out=ysb[:, :], in0=ysb[:, :], in1=zs[:, :], op=ALU.mult), reads=[kx("ysb"), kx("zs")], writes=[kx("ysb")])
                    S.op("act", lambda e: e.activation(out=sqj[:, :], in_=ysb[:, :], func=AF.Square, accum_out=ssq[:, 0:1]),
                         reads=[kx("ysb")], writes=[kx("ssq"), kx("sqj")])
                    S.op("act", lambda e: e.activation(out=ssq[:, :], in_=ssq[:, :], func=AF.Ln, scale=1.0 / 512, bias=eps_t[:, 0:1]),
                         reads=[kx("ssq"), "eps_t"], writes=[kx("ssq")])
                    S.op("act", lambda e: e.activation(out=ssq[:, :], in_=ssq[:, :], func=AF.Exp, scale=-0.5), reads=[kx("ssq")], writes=[kx("ssq")])
                    S.op("dve", lambda e: e.tensor_scalar(out=yn[:, :], in0=ysb[:, :], scalar1=ssq[:, 0:1], scalar2=None, op0=ALU.mult),
                         reads=[kx("ysb"), kx("ssq")], writes=[kx("yn")])
                    pv = psb[6 + c % 2][:, :].bitcast(BF16).rearrange("p (a b) -> p a b", b=128)
                    mi_ = (c // SC) % 2
                    for xg in range(4):
                        S.op("pe", lambda e, xg=xg: e.transpose(out=pv[:, xg, :], in_=yn[:, xg * 128:(xg + 1) * 128], identity=ident_b[:, :]),
                             reads=[kx("yn"), "ident_b"], writes=[("ps", 6 + c % 2)])
                    for xg in range(4):
                        S.op("dve", lambda e, xg=xg: e.tensor_scalar(
                            out=mst[mi_][:, xg, (c % SC) * 128:(c % SC + 1) * 128], in0=pv[:, xg, :], scalar1=nw[:, g2 * 4 + xg:g2 * 4 + xg + 1],
                            scalar2=None, op0=ALU.mult),
                            reads=[("ps", 6 + c % 2), "nw"], writes=[("mst", mi_, c % SC, xg)])
                    if c % SC == SC - 1:
                        c0 = c - SC + 1
                        for xg in range(4):
                            S.dma(mix[(g2 * 4 + xg) * 128:(g2 * 4 + xg + 1) * 128, c0 * 128:(c + 1) * 128], mst[mi_][:, xg, :],
                                  reads=[("mst", mi_, cc, xg) for cc in range(SC)], writes=[("mixd", g2, xg, c0)])
                fwdA(0)
                for c in range(J):
                    fwdC(c)
                    if c + 1 < J:
                        fwdA(c + 1)
                    fwdB(c)
            S.barrier()

    if "hy" in stages:
        ft_s = dscr("ft_s", (J, 128, 4 * 128), BF16)
        et_s = dscr("et_s", (J, 128, 2 * 128), BF16)
        h3_s = dscr("h3_s", (128, 2 * NFFT), BF16)
        with contextlib.ExitStack() as ph:
            fw1 = T(ph, "fw1", [33, 64], F32)
            fw2 = T(ph, "fw2", [64, 64], F32)
            fw3 = T(ph, "fw3", [64, 128], F32)
            fr = T(ph, "fr", [128, 1], F32)
            fbs = T(ph, "fbs", [128, 3], F32)
            S.dma(fw1[:, :], I["fw1"][:, :], writes=["fw1"])
            S.dma(fw2[:, :], I["fw2"][:, :], writes=["fw2"])
            S.dma(fw3[:, :], I["fw3"][:, :], writes=["fw3"])
            S.dma(fr[:, :], I["ffreq"][:, :], writes=["fr"])
            S.op("pool", lambda e: e.memset(fbs[:, :], 0.0), writes=["fbs"])
            S.dma(fbs[0:64, 0:1], I["fb1"][:, :], reads=["fbs"], writes=["fbs1"])
            S.dma(fbs[0:64, 1:2], I["fb2"][:, :], reads=["fbs"], writes=["fbs2"])
            S.dma(fbs[:, 2:3], I["fb3"][:, :], reads=["fbs"], writes=["fbs3"])
            S.op("dve", lambda e: e.tensor_tensor(out=fbs[:, :], in0=fbs[:, :], in1=fr[:, 0:1].broadcast_to([128, 3]), op=ALU.mult),
                 reads=["fbs", "fbs1", "fbs2", "fbs3", "fr"], writes=["fbsx"])
            NB3 = 3
            hA_l = [T(ph, "hA%d" % i, [128, 512], F32) for i in range(NB3)]
            hB_l = [T(ph, "hB%d" % i, [128, 512], F32) for i in range(NB3)]
            mw_l = [T(ph, "mwrap%d" % i, [128, 512], F32) for i in range(NB3)]
            PI = math.pi

            def mlp_layer(ps_ap, np_, col, dst, mwrap, mwk):
                S.op("dve", lambda e: e.tensor_scalar(out=dst[0:np_, :], in0=ps_ap, scalar1=fr[0:np_, 0:1], scalar2=fbs[0:np_, col:col + 1],
                                                      op0=ALU.mult, op1=ALU.add), reads=[pkey[0], "fr", "fbsx"], writes=[id(dst)])
                for (cmp, thr, add) in ((ALU.is_gt, PI, -2 * PI), (ALU.is_lt, -PI, 2 * PI)):
                    S.op("dve", lambda e, cmp=cmp, thr=thr: e.tensor_scalar(out=mwrap[0:np_, :], in0=dst[0:np_, :], scalar1=thr, scalar2=None, op0=cmp),
                         reads=[id(dst)], writes=[mwk])
                    S.op("dve", lambda e, add=add: e.scalar_tensor_tensor(out=dst[0:np_, :], in0=mwrap[0:np_, :], scalar=add, in1=dst[0:np_, :],
                                                                         op0=ALU.mult, op1=ALU.add),
                         reads=[id(dst), mwk], writes=[id(dst)])
                S.op("act", lambda e: e.activation(out=dst[0:np_, :], in_=dst[0:np_, :], func=AF.Sin), reads=[id(dst)], writes=[id(dst)])

            pkey = [None]
            ztl = [T(ph, "ztl3_%d" % i, [33, 512], F32) for i in range(NB3)]
            mft = [T(ph, "mft3_%d" % i, [128, 512], F32) for i in range(NB3)]
            h3o = [T(ph, "h3o3_%d" % i, [128, 512], BF16) for i in range(NB3)]
            NTI = 2 * NFFT // 512
            for t0_ in range(0, NTI, NB3):
                tis = list(range(t0_, min(NTI, t0_ + NB3)))
                for ti in tis:
                    i = ti % NB3
                    S.dma(ztl[i][:, :], I["zt"][:, ti * 512:(ti + 1) * 512], writes=[("ztl", i)])
                    S.dma(mft[i][:, :], I["mfb"][:, ti * 512:(ti + 1) * 512], writes=[("mft", i)])
                for ti in tis:
                    i = ti % NB3
                    b_ = bank(0, 8)
                    pkey[0] = ("ps", b_)
                    S.op("pe", lambda e, i=i, b_=b_: e.matmul(psb[b_][0:64, :], lhsT=fw1[:, :], rhs=ztl[i][:, :], start=True, stop=True),
                         reads=["fw1", ("ztl", i)], writes=[("ps", b_)])
                    mlp_layer(psb[b_][0:64, :], 64, 0, hA_l[i], mw_l[i], ("mwrap", i))
                for ti in tis:
                    i = ti % NB3
                    b_ = bank(0, 8)
                    pkey[0] = ("ps", b_)
                    S.op("pe", lambda e, i=i, b_=b_: e.matmul(psb[b_][0:64, :], lhsT=fw2[:, :], rhs=hA_l[i][0:64, :], start=True, stop=True),
                         reads=["fw2", id(hA_l[i])], writes=[("ps", b_)])
                    mlp_layer(psb[b_][0:64, :], 64, 1, hB_l[i], mw_l[i], ("mwrap", i))
                for ti in tis:
                    i = ti % NB3
                    b_ = bank(0, 8)
                    pkey[0] = ("ps", b_)
                    S.op("pe", lambda e, i=i, b_=b_: e.matmul(psb[b_][:, :], lhsT=fw3[:, :], rhs=hB_l[i][0:64, :], start=True, stop=True),
                         reads=["fw3", id(hB_l[i])], writes=[("ps", b_)])
                    mlp_layer(psb[b_][:, :], 128, 2, hA_l[i], mw_l[i], ("mwrap", i))
                for ti in tis:
                    i = ti % NB3
                    S.op("pool", lambda e, i=i: e.tensor_tensor(out=h3o[i][:, :], in0=hA_l[i][:, :], in1=mft[i][:, :], op=ALU.mult),
                         reads=[id(hA_l[i]), ("mft", i)], writes=[("h3o", i)])
                    S.dma(h3_s[:, ti * 512:(ti + 1) * 512], h3o[i][:, :], reads=[("h3o", i)], writes=[("h3_s", ti)])
            S.barrier()

        with contextlib.ExitStack() as ph:
            NQ = NSUB * 256
            bufA = T(ph, "bufA", [128, NQ], BF16)
            bufB = T(ph, "bufB", [128, NQ], BF16)
            bufC = T(ph, "bufC", [128, NQ], BF16)
            bufD = T(ph, "bufD", [128, NQ], BF16)
            Yb = T(ph, "Yb", [128, NQ], BF16)
            Pt = T(ph, "Pt", [128, J * 128], F32)
            gt = T(ph, "gt", [128, J, 128], BF16)
            x0t = T(ph, "x0t", [128, J, 128], BF16)
            fm = [T(ph, "fm%d" % i, [128, NT], BF16) for i in range(2)]
            cin = T(ph, "hcin", [128, NTH], BF16)
            ftr = T(ph, "ftr", [128, J, 4, 128], BF16)
            etr = T(ph, "etr", [128, J, 2, 128], BF16)
            ftv = I["ftab"].rearrange("p (j m) -> p j m", m=512)
            etv = I["etab"].rearrange("p (j m) -> p j m", m=256)
            JT = max(1, (J * 128) // 512)
            kf, ke = [], []
            for j0_ in range(0, J, JT):
                S.dma(Pt[:, 0:JT * 512].rearrange("p (j m) -> p j m", m=512), ftv[:, j0_:j0_ + JT, :], writes=["P1", "P2"])
                S.op("act" if (j0_ // JT) % 2 else "dve",
                     (lambda e, j0_=j0_: e.activation(out=ftr[:, j0_:j0_ + JT, :, :].rearrange("p j m k -> p j (m k)"), in_=Pt[:, 0:JT * 512].rearrange("p (j m) -> p j m", m=512), func=AF.Copy))
                     if (j0_ // JT) % 2 else
                     (lambda e, j0_=j0_: e.tensor_copy(out=ftr[:, j0_:j0_ + JT, :, :].rearrange("p j m k -> p j (m k)"), in_=Pt[:, 0:JT * 512].rearrange("p (j m) -> p j m", m=512))),
                     reads=["P1", "P2"], writes=[("ftr", j0_)])
                kf.append(("ftr", j0_))
            JE = min(J, 2 * JT)
            for j0_ in range(0, J, JE):
                S.dma(Pt[:, 0:JE * 256].rearrange("p (j m) -> p j m", m=256), etv[:, j0_:j0_ + JE, :], writes=["P1", "P2"])
                S.op("dve", lambda e, j0_=j0_: e.tensor_copy(out=etr[:, j0_:j0_ + JE, :, :].rearrange("p j m k -> p j (m k)"), in_=Pt[:, 0:JE * 256].rearrange("p (j m) -> p j m", m=256)),
                     reads=["P1", "P2"], writes=[("etr", j0_)])
                ke.append(("etr", j0_))
            S.merge("ftr", kf)
            S.merge("etr", ke)
            h3l = [T(ph, "h3l%d" % i, [128, 512], BF16) for i in range(3)]
            wnd = [T(ph, "wnd%d" % i, [128, 4, 128], F32) for i in range(2)]
            w2f = T(ph, "w2f", [128, 3, 128], F32)
            w2b = T(ph, "w2b", [128, 3, 128], BF16)
            hcw = T(ph, "hcw", [128, 24, 3], F32)
            hcwc = T(ph, "hcwc", [128, 16, 3], F32)
            hcb = T(ph, "hcb", [128, 24], F32)
            hyb = T(ph, "hyb", [128, 1024], F32)
            hnw = T(ph, "hnw", [128, 8], F32)
            absd = T(ph, "absd", [128, 1024], F32)
            negt = T(ph, "negt", [128, 4 * J], F32)
            w4b = T(ph, "w4b", [128, 1024], BF16)
            dg3 = T(ph, "dg3", [128, 3, 128], BF16)
            ssq2 = T(ph, "ssq2", [128, 2 * J], F32)
            S.dma(w2f[:, :, :], I["w2tab"].rearrange("p (i m) -> p i m", i=3), writes=["w2f"])
            S.op("dve", lambda e: e.tensor_copy(out=w2b[:, :, :], in_=w2f[:, :, :]), reads=["w2f"], writes=["w2b"])
            S.dma(hcw[:, :, :], I["hy_cw"].rearrange("p (g j) -> p g j", j=3), writes=["hcw"])
            S.dma(hcwc[:, :, :], I["hy_cw_ctx"].rearrange("p (g j) -> p g j", j=3), writes=["hcwc"])
            S.dma(hcb[:, :], I["hy_cb"][:, :], writes=["hcb"])
            S.dma(hyb[:, :], I["hy_bias"][0, :].partition_broadcast(128), writes=["hyb"])
            S.dma(hnw[:, :], I["hy_nw"][:, :], writes=["hnw"])
            S.dma(absd[:, :], I["absd"][:, :], writes=["absd"])
            S.dma(negt[:, :], I["negt"][:, :], writes=["negt"])
            WCH = min(1024, J * 128)
            for c0_ in range(0, 1024, WCH):
                S.dma(Pt[:, 0:WCH], I["fw4s"][:, c0_:c0_ + WCH], writes=["P1"])
                S.op("dve", lambda e, c0_=c0_: e.tensor_copy(out=w4b[:, c0_:c0_ + WCH], in_=Pt[:, 0:WCH]), reads=["P1"], writes=[("w4b", c0_)])
            S.merge("w4b", [("w4b", c0_) for c0_ in range(0, 1024, WCH)])
            A4 = lambda t_: t_[:, :].rearrange("p (s r k) -> p s r k", r=2, k=128)
            B1v = lambda t_: t_[:, :].rearrange("p (r s j c) -> p r s j c", r=2, s=NSUB, c=CS)
            ZTv = lambda t_: t_[:, :].rearrange("p (r j s c) -> p r j s c", r=2, j=J, c=CS)
            cnt = {"ft": 0, "et": 0, "h3": 0, "w": 0}

            mixo = fm[0]
            gct = Pt[:, 0:J * 64].bitcast(BF16).rearrange("p (j c) -> p j c", c=128)

            def evac(pv_in, out_ap, rkeys, wkey, wg, en=None):
                en = en or S.ev()
                if hy_stop == 3.11:
                    en = "dve"
                if hy_stop == 3.12:
                    en = "act"
                if en == "act":
                    S.op("act", lambda e: e.activation(out=out_ap, in_=pv_in, func=AF.Copy), reads=rkeys, writes=[wkey], wg=wg)
                else:
                    S.op("dve", lambda e: e.tensor_copy(out=out_ap, in_=pv_in), reads=rkeys, writes=[wkey], wg=wg)

            def conv3(src, row0, wt, wkey, g, dst_fn):
                S.dma(cin[:, :], src[row0:row0 + 128, :], writes=["hcin"])
                for j in range(3):
                    S.op("pool", lambda e, j=j: e.tensor_scalar(out=dg3[:, j, :], in0=ident_f[:, :], scalar1=wt[:, g, j:j + 1], scalar2=None, op0=ALU.mult),
                         reads=["ident_f", wkey], writes=[("dg3", j)])
                wg = S.newgroup()
                for o in range(0, NT, TW):
                    b = bank(0, 8)
                    for j in range(3):
                        S.op("pe", lambda e, j=j: e.matmul(psb[b][:, 0:TW], lhsT=dg3[:, j, :], rhs=cin[:, o + 1 + j:o + 1 + j + TW],
                                                           start=(j == 0), stop=(j == 2)),
                             reads=[("dg3", j), "hcin"], writes=[("ps", b)])
                    dst_fn(psb[b][:, 0:TW], o, ("ps", b), wg)

            def to_perm(srct, skey, dst, dkey):
                wg = S.newgroup()
                for j0 in range(0, J, 8):
                    nb = min(8, J - j0)
                    b = bank(0, 8)
                    pv = psb[b][:, :].bitcast(BF16).rearrange("p (a b) -> p a b", b=128)
                    for i in range(nb):
                        S.op("pe", lambda e, i=i: e.transpose(out=pv[:, i, :], in_=srct[:, j0 + i:NT:J], identity=ident_b[:, :]),
                             reads=[skey, "ident_b"], writes=[("ps", b)])
                    evac(pv[:, 0:nb, :], dst[:, j0:j0 + nb, :], [("ps", b)], dkey, wg, "act")

            def fft_fwd(mov_fn, PCn, B1t, B1key, BTt, BTkey, Xt, Xkey):
                b1 = B1v(B1t)
                wg = S.newgroup()
                for j0 in range(0, J, 2):
                    b = bank(0, 8)
                    pv = psb[b][:, :].rearrange("p (r j c) -> p r j c", r=2, c=128)
                    for jj in range(2):
                        j = j0 + jj
                        for ri in range(2 if hy_stop != 3.05 else 0):
                            for pc in range(PCn):
                                mv, mk = mov_fn(pc, j)
                                S.op("pe", lambda e, ri=ri, pc=pc, mv=mv, jj=jj, j=j: e.matmul(pv[:, ri, jj, :], lhsT=ftr[:, j, pc * 2 + ri, :], rhs=mv,
                                                                                         start=(pc == 0), stop=(pc == PCn - 1)),
                                     reads=["ftr"] + list(mk), writes=[("ps", b)])
                    pv5 = psb[b][:, :].rearrange("p (r j s c) -> p r j s c", r=2, j=2, c=CS)
                    en = "dve"
                    for ri in range(2 if hy_stop not in (3.05, 3.07) else 0):
                        evac(pv5[:, ri, :, :, :], b1[:, ri, :, j0:j0 + 2, :].rearrange("p s j c -> p j s c"), [("ps", b)], B1key, wg, en)
                if hy_stop in (3.05, 3.07, 3.1, 3.11, 3.12):
                    return
                stage2(B1t, B1key, BTt, BTkey, Xt, Xkey)

            def stage2(srct, skey, BTt, BTkey, Xt, Xkey):
                b1 = B1v(srct)
                bt = A4(BTt)
                wg = S.newgroup()
                for s0 in range(0, NSUB, 4):
                    ns = min(4, NSUB - s0)
                    b = bank(0, 8)
                    pv = psb[b][:, :].bitcast(BF16).rearrange("p (s r k) -> p s r k", r=2, k=128)
                    for sl in range(ns):
                        for ri in range(2):
                            S.op("pe", lambda e, sl=sl, ri=ri: e.transpose(out=pv[:, sl, ri, :], in_=srct[:, (ri * NSUB + s0 + sl) * 128:(ri * NSUB + s0 + sl + 1) * 128],
                                                                           identity=ident_b[:, :]),
                                 reads=[skey, "ident_b"], writes=[("ps", b)])
                    evac(pv[:, 0:ns, :, :], bt[:, s0:s0 + ns, :, :], [("ps", b)], BTkey, wg, "act")
                if hy_stop == 3.2:
                    return
                dft_j(BTt, BTkey, Xt, Xkey, True)

            def dft_j(srct, skey, Xt, Xkey, fwd):
                bt = A4(srct)
                xo = A4(Xt)
                i2, i3 = (2, 1) if fwd else (1, 2)
                wg = S.newgroup()
                for s0 in range(0, NSUB, 2):
                    b = bank(0, 8)
                    pv = psb[b][:, :].rearrange("p (s r k) -> p s r k", r=2, k=128)
                    for sl in range(2):
                        sub = s0 + sl
                        o0 = sl * 256
                        S.op("pe", lambda e, o0=o0, sub=sub: e.matmul(psb[b][:, o0:o0 + 256], lhsT=w2b[:, 0, :], rhs=srct[:, sub * 256:(sub + 1) * 256], start=True, stop=False),
                             reads=[skey, "w2b"], writes=[("ps", b)])
                        S.op("pe", lambda e, o0=o0, sub=sub: e.matmul(psb[b][:, o0:o0 + 128], lhsT=w2b[:, i2, :], rhs=srct[:, sub * 256 + 128:(sub + 1) * 256], start=False, stop=False),
                             reads=[skey, "w2b"], writes=[("ps", b)])
                        S.op("pe", lambda e, o0=o0, sub=sub: e.matmul(psb[b][:, o0 + 128:o0 + 256], lhsT=w2b[:, i3, :], rhs=srct[:, sub * 256:sub * 256 + 128], start=False, stop=True),
                             reads=[skey, "w2b"], writes=[("ps", b)])
                    evac(psb[b][:, :], Xt[:, s0 * 256:(s0 + 2) * 256], [("ps", b)], Xkey, wg)

            def make_kappa(sig, cg, dstt, dkey):
                kv = dstt[:, :].rearrange("p (t c) -> p t c", c=128)
                wg = S.newgroup()
                for t0 in range(0, 2 * J, 4):
                    hi = cnt["h3"] % 3
                    cnt["h3"] += 1
                    q0 = (sig * 2 * J + t0) * 128
                    S.dma(h3l[hi][:, :], h3_s[:, q0:q0 + 512], writes=[("h3l", hi)])
                    b = bank(0, 8)
                    for tt in range(4):
                        S.op("pe", lambda e, tt=tt: e.matmul(psb[b][:, tt * 128:(tt + 1) * 128], lhsT=h3l[hi][:, tt * 128:(tt + 1) * 128],
                                                             rhs=w4b[:, cg * 128:(cg + 1) * 128], start=True, stop=True),
                             reads=[("h3l", hi), "w4b"], writes=[("ps", b)])
                    wi = cnt["w"] % 2
                    cnt["w"] += 1
                    wgw = S.newgroup()
                    for tt in range(4):
                        tcol = sig * 2 * J + t0 + tt
                        S.op("act", lambda e, tcol=tcol, tt=tt: e.activation(out=wnd[wi][:, tt, :], in_=absd[:, cg * 128:(cg + 1) * 128], func=AF.Exp,
                                                                             scale=negt[:, tcol:tcol + 1]),
                             reads=["absd", "negt"], writes=[("wnd", wi)], wg=wgw)
                    S.op("dve", lambda e: e.tensor_tensor(out=kv[:, t0:t0 + 4, :], in0=psb[b][:, :].rearrange("p (t c) -> p t c", c=128), in1=wnd[wi][:, :, :], op=ALU.mult),
                         reads=[("ps", b), ("wnd", wi)], writes=[dkey], wg=wg)

            def cmul(Gt, Gkey, Kt, Kkey, first):
                g4, k4, y4 = A4(Gt), A4(Kt), A4(Yb)
                hs = NSUB // 2
                wg = S.newgroup()
                for hf in range(2):
                    ss = slice(hf * hs, (hf + 1) * hs)
                    p1 = Pt[:, 0:hs * 64].bitcast(BF16).rearrange("p (s k) -> p s k", k=128)
                    p2 = Pt[:, hs * 128:hs * 128 + hs * 64].bitcast(BF16).rearrange("p (s k) -> p s k", k=128)
                    rd = [Gkey, Kkey]
                    for (ro, (a1, a2), (b1_, b2_), sgn) in ((0, (0, 0), (1, 1), ALU.subtract), (1, (0, 1), (1, 0), ALU.add)):
                        S.op("dve", lambda e, a1=a1, a2=a2: e.tensor_tensor(out=p1, in0=g4[:, ss, a1, :], in1=k4[:, ss, a2, :], op=ALU.mult),
                             reads=rd, writes=["P1"])
                        S.op("pool", lambda e, b1_=b1_, b2_=b2_: e.tensor_tensor(out=p2, in0=g4[:, ss, b1_, :], in1=k4[:, ss, b2_, :], op=ALU.mult),
                             reads=rd, writes=["P2"])
                        S.op("dve", lambda e, sgn=sgn: e.tensor_tensor(out=p1, in0=p1, in1=p2, op=sgn), reads=["P1", "P2"], writes=["P1"])
                        if first:
                            S.op("pool", lambda e, ro=ro: e.tensor_copy(out=y4[:, ss, ro, :], in_=p1), reads=["P1"], writes=["Yb"], wg=wg)
                        else:
                            S.op("pool", lambda e, ro=ro: e.tensor_tensor(out=y4[:, ss, ro, :], in0=y4[:, ss, ro, :], in1=p1, op=ALU.add),
                                 reads=["P1", "Yb"], writes=["Yb"], wg=wg)

            kvA = bufA[:, :].rearrange("p (t c) -> p t c", c=128)
            for cg in range(8 if hy_stop >= 5 else (1 if hy_stop >= 2 else 0)):
                def ev_x0(ps_ap, o, bkey, wg):
                    S.op("act", lambda e: e.activation(out=fm[1][:, o:o + TW], in_=ps_ap, func=AF.Identity, bias=hcb[:, cg:cg + 1]),
                         reads=[bkey, "hcb"], writes=["fm1"], wg=wg)
                def ev_x1(ps_ap, o, bkey, wg):
                    S.op("act", lambda e: e.activation(out=fm[0][:, o:o + TW], in_=ps_ap, func=AF.Identity, bias=hcb[:, 8 + cg:9 + cg]),
                         reads=[bkey, "hcb"], writes=["fm0"], wg=wg)
                def ev_v(ps_ap, o, bkey, wg):
                    S.op("dve", lambda e: e.scalar_tensor_tensor(out=fm[1][:, o:o + TW], in0=ps_ap, scalar=hcb[:, 16 + cg:17 + cg], in1=fm[0][:, o:o + TW],
                                                                 op0=ALU.add, op1=ALU.mult),
                         reads=[bkey, "hcb", "fm0"], writes=["fm1"], wg=wg)
                conv3(pj_fm, 1536 + cg * 128, hcw, "hcw", cg, ev_x0)
                to_perm(fm[1], "fm1", x0t, "x0t")
                conv3(pj_fm, 2560 + cg * 128, hcw, "hcw", 8 + cg, ev_x1)
                conv3(pj_fm, 3584 + cg * 128, hcw, "hcw", 16 + cg, ev_v)
                to_perm(fm[1], "fm1", gt, "gt")
                if hy_stop == 2:
                    break
                fft_fwd(lambda pc, j: (gt[:, j, :], ["gt"]), 1, bufA, "bufA", bufB, "bufB", bufC, "bufC")
                if 3 <= hy_stop < 3.5:
                    break
                make_kappa(0, cg, bufA, "bufA")
                if hy_stop == 3.5:
                    break
                fft_fwd(lambda pc, j: (kvA[:, pc * J + j, :], ["bufA"]), 2, bufD, "bufD", bufB, "bufB", bufA, "bufA")
                cmul(bufC, "bufC", bufA, "bufA", True)
                conv3(ch_fm, cg * 128, hcwc, "hcwc", cg, ev_x1)
                conv3(ch_fm, 1024 + cg * 128, hcwc, "hcwc", 8 + cg, ev_v)
                to_perm(fm[1], "fm1", gct, "P1")
                fft_fwd(lambda pc, j: (gct[:, j, :], ["P1"]), 1, bufD, "bufD", bufB, "bufB", bufC, "bufC")
                make_kappa(1, cg, bufA, "bufA")
                fft_fwd(lambda pc, j: (kvA[:, pc * J + j, :], ["bufA"]), 2, bufD, "bufD", bufB, "bufB", bufA, "bufA")
                cmul(bufC, "bufC", bufA, "bufA", False)
                dft_j(Yb, "Yb", bufD, "bufD", False)
                zs4 = A4(bufD)
                zt4 = ZTv(bufB)
                wg = S.newgroup()
                for s0 in range(0, NSUB, 4):
                    ns = min(4, NSUB - s0)
                    b = bank(0, 8)
                    pv = psb[b][:, :].bitcast(BF16).rearrange("p (r s m) -> p r s m", r=2, m=128)
                    pv5 = psb[b][:, :].bitcast(BF16).rearrange("p (r s j c) -> p r s j c", r=2, s=4, c=CS)
                    for ri in range(2):
                        for sl in range(ns):
                            S.op("pe", lambda e, sl=sl, ri=ri: e.transpose(out=pv[:, ri, sl, :], in_=bufD[:, ((s0 + sl) * 2 + ri) * 128:((s0 + sl) * 2 + ri + 1) * 128], identity=ident_b[:, :]),
                                 reads=["bufD", "ident_b"], writes=[("ps", b)])
                    en = "dve"
                    for ri in range(2):
                        evac(pv5[:, ri, 0:ns, :, :], zt4[:, ri, :, s0:s0 + ns, :].rearrange("p j s c -> p s j c"), [("ps", b)], "bufB", wg, en)
                yh = Pt[:, :].rearrange("p (j c) -> p j c", c=128)
                wgy = S.newgroup()
                for j0 in range(0, J, 4):
                    b = bank(0, 8)
                    pv = psb[b][:, :].rearrange("p (j c) -> p j c", c=128)
                    for jj in range(4):
                        j = j0 + jj
                        for ri in range(2):
                            S.op("pe", lambda e, ri=ri, jj=jj, j=j: e.matmul(pv[:, jj, :], lhsT=etr[:, j, ri, :], rhs=bufB[:, (ri * J + j) * 128:(ri * J + j + 1) * 128],
                                                                             start=(ri == 0), stop=(ri == 1)),
                                 reads=["etr", "bufB"], writes=[("ps", b)])
                    S.op("pool", lambda e: e.tensor_tensor(out=yh[:, j0:j0 + 4, :], in0=gt[:, j0:j0 + 4, :],
                                                           in1=hyb[:, cg * 128:(cg + 1) * 128].unsqueeze(1).broadcast_to([128, 4, 128]), op=ALU.mult),
                         reads=["gt", "hyb"], writes=["P1", "P2", ("yh", j0)], wg=wgy)
                    S.op("dve", lambda e: e.tensor_tensor(out=yh[:, j0:j0 + 4, :], in0=pv, in1=yh[:, j0:j0 + 4, :], op=ALU.add),
                         reads=[("ps", b), ("yh", j0)], writes=[("yh", j0)])
                    S.op("pool", lambda e: e.tensor_tensor(out=yh[:, j0:j0 + 4, :], in0=yh[:, j0:j0 + 4, :], in1=x0t[:, j0:j0 + 4, :], op=ALU.mult),
                         reads=[("yh", j0), "x0t"], writes=[("yh", j0), "yhdone"], wg=wgy)
                sqv = bufD[:, 0:J * 128]
                S.op("act", lambda e: e.activation(out=sqv, in_=Pt[:, :], func=AF.Square), reads=["yhdone", "P1", "P2"], writes=["bufD"])
                S.op("dve", lambda e: e.tensor_reduce(out=ssq2[:, :], in_=sqv.rearrange("p (g d) -> p g d", d=64), axis=AX.X, op=ALU.add),
                     reads=["bufD"], writes=["ssq2"])
                S.op("act", lambda e: e.activation(out=ssq2[:, :], in_=ssq2[:, :], func=AF.Ln, scale=1.0 / 64, bias=eps_t[:, 0:1]), reads=["ssq2", "eps_t"], writes=["ssq2"])
                S.op("act", lambda e: e.activation(out=ssq2[:, :], in_=ssq2[:, :], func=AF.Exp, scale=-0.5), reads=["ssq2"], writes=["ssq2"])
                ynv = bufA[:, 0:J * 128]
                S.op("dve", lambda e: e.tensor_tensor(out=ynv.rearrange("p (g d) -> p g d", d=64), in0=Pt[:, :].rearrange("p (g d) -> p g d", d=64),
                                                      in1=ssq2[:, :].unsqueeze(2).broadcast_to([128, 2 * J, 64]), op=ALU.mult),
                     reads=["yhdone", "P1", "P2", "ssq2"], writes=["bufA"])
                yn3 = ynv.rearrange("p (j c) -> p j c", c=128)
                wg = S.newgroup()
                for j0 in range(0, J, 8):
                    nb = min(8, J - j0)
                    b = bank(0, 8)
                    pv = psb[b][:, :].bitcast(BF16).rearrange("p (a b) -> p a b", b=128)
                    for i in range(nb):
                        S.op("pe", lambda e, i=i: e.transpose(out=pv[:, i, :], in_=yn3[:, j0 + i, :], identity=ident_b[:, :]),
                             reads=["bufA", "ident_b"], writes=[("ps", b)])
                    for i in range(nb):
                        S.op("dve", lambda e, i=i: e.tensor_scalar(out=mixo[:, j0 + i:NT:J], in0=pv[:, i, :], scalar1=hnw[:, cg:cg + 1], scalar2=None, op0=ALU.mult),
                             reads=[("ps", b), "hnw"], writes=["fm0"], wg=wg)
                S.dma(mix[1024 + cg * 128:1024 + (cg + 1) * 128, :], mixo[:, :], reads=["fm0"], writes=[("mixh", cg)])
            S.barrier()

    if "wprep" in stages and not wprep_inline:
        with contextlib.ExitStack() as ph:
            for _ in wprep_gen(ph):
                pass
            S.barrier()

    if "p2" in stages:
        with contextlib.ExitStack() as ph:
            W2 = min(512, NT)
            xt = T(ph, "p2xt", [128, KC, W2], F32)
            mixt = T(ph, "p2mix", [128, 16, W2], BF16)
            acc = T(ph, "p2acc", [128, KC, W2], F32)
            h1 = T(ph, "p2h1", [128, KC, W2], F32)
            ub = T(ph, "p2u", [128, KC, W2], BF16)
            sq = T(ph, "p2sq", [128, KC, W2], BF16)
            hm = T(ph, "p2hm", [128, FC, W2], BF16)
            rstd = T(ph, "p2rstd", [128, W2], F32)
            av = [T(ph, "p2a%d" % i, [128, W2], F32) for i in range(2)]
            ptf = T(ph, "p2ptf", [128, 2, W2], F32)
            ptb = T(ph, "p2ptb", [128, 2, W2], BF16)
            wo = [T(ph, "p2wo%d" % i, [128, 16, 128], BF16) for i in range(3)]
            wg_ = [T(ph, "p2wg%d" % i, [128, KC, 128], BF16) for i in range(6)]
            wu_ = [T(ph, "p2wu%d" % i, [128, KC, 128], BF16) for i in range(6)]
            wd_ = [T(ph, "p2wd%d" % i, [128, FC, 128], BF16) for i in range(3)]
            wpg_ = [T(ph, "p2wpg%d" % i, [128, KC, 128], BF16) for i in range(4)]
            wpp_ = [T(ph, "p2wpp%d" % i, [128, 2, 128], BF16) for i in range(4)]
            nws = T(ph, "p2nws", [128, 5, KC], F32)
            for i, nm in enumerate(["nm_post", "nf_pre", "nf_post", "ple_npre", "ple_npost"]):
                S.dma(nws[:, i, :], I[nm][:, :], writes=[("nws", i)])
            S.merge("nws", [("nws", i) for i in range(5)])
            wn = {"wo": 0, "wg": 0, "wd": 0, "wp": 0, "a": 0}

            def normed_residual(src, skey, widx, res, rkey, dst, dkey):
                rms_rstd(src, [skey], W2, rstd, "p2rstd", sq, "p2sq", banks=(0, 8))
                wgd = S.newgroup()
                for kc in range(KC):
                    S.op("dve", lambda e, kc=kc: e.scalar_tensor_tensor(out=dst[:, kc, :], in0=src[:, kc, :], scalar=nws[:, widx, kc:kc + 1],
                                                                        in1=rstd[:, :], op0=ALU.mult, op1=ALU.mult),
                         reads=[skey, "nws", "p2rstd"], writes=[(dkey, kc)])
                    S.op("pool", lambda e, kc=kc: e.tensor_tensor(out=dst[:, kc, :], in0=dst[:, kc, :], in1=res[:, kc, :], op=ALU.add),
                         reads=[(dkey, kc), rkey], writes=[(dkey, kc), dkey], wg=wgd)

            def normed_bf16(src, skey, widx, dst, dkey):
                rms_rstd(src, [skey], W2, rstd, "p2rstd", sq, "p2sq", banks=(0, 8))
                wgd = S.newgroup()
                for kc in range(KC):
                    S.op("dve", lambda e, kc=kc: e.scalar_tensor_tensor(out=dst[:, kc, :], in0=src[:, kc, :], scalar=nws[:, widx, kc:kc + 1],
                                                                        in1=rstd[:, :], op0=ALU.mult, op1=ALU.mult),
                         reads=[skey, "nws", "p2rstd"], writes=[dkey], wg=wgd)

            for o in range(0, NT, W2):
                S.dma(xt[:, :, :], I["xo"][:, 2 + o:2 + o + W2].rearrange("(kc p) t -> p kc t", p=128), writes=["xt"])
                S.dma(mixt[:, :, :], mix[:, o:o + W2].rearrange("(kc p) t -> p kc t", p=128), writes=["mixt"])
                S.dma(ptf[:, :, :], I["pT"][:, o:o + W2].rearrange("(kc p) t -> p kc t", p=128), writes=["ptf"])
                S.op("pool", lambda e: e.tensor_copy(out=ptb[:, :, :], in_=ptf[:, :, :]), reads=["ptf"], writes=["ptb"])
                wga = S.newgroup()
                for m in range(KC):
                    wi = wn["wo"] % 3
                    wn["wo"] += 1
                    S.dma(wo[wi][:, :, :], wo_s[m, :, :].rearrange("p (kc c) -> p kc c", c=128), writes=[("wo", wi)])
                    b = bank(0, 8)
                    for kc in range(16):
                        S.op("pe", lambda e, kc=kc: e.matmul(psb[b][:, 0:W2], lhsT=wo[wi][:, kc, :], rhs=mixt[:, kc, :], start=(kc == 0), stop=(kc == 15)),
                             reads=[("wo", wi), "mixt"], writes=[("ps", b)])
                    en = S.ev()
                    S.op(en, (lambda e: e.activation(out=acc[:, m, :], in_=psb[b][:, 0:W2], func=AF.Copy)) if en == "act" else
                         (lambda e: e.tensor_copy(out=acc[:, m, :], in_=psb[b][:, 0:W2])), reads=[("ps", b)], writes=["acc"], wg=wga)
                normed_residual(acc, "acc", 0, xt, "xt", h1, "h1")
                normed_bf16(h1, "h1", 1, ub, "ub")
                wgh = S.newgroup()
                for j in range(FC):
                    wi = wn["wg"] % 6
                    wn["wg"] += 1
                    S.dma(wg_[wi][:, :, :], wg_s[j, :, :].rearrange("p (kc c) -> p kc c", c=128), writes=[("wg", wi)])
                    S.dma(wu_[wi][:, :, :], wu_s[j, :, :].rearrange("p (kc c) -> p kc c", c=128), writes=[("wu", wi)])
                    bg = bank(0, 8)
                    for kc in range(KC):
                        S.op("pe", lambda e, kc=kc: e.matmul(psb[bg][:, 0:W2], lhsT=wg_[wi][:, kc, :], rhs=ub[:, kc, :], start=(kc == 0), stop=(kc == KC - 1)),
                             reads=[("wg", wi), "ub"], writes=[("ps", bg)])
                    bu = bank(0, 8)
                    for kc in range(KC):
                        S.op("pe", lambda e, kc=kc: e.matmul(psb[bu][:, 0:W2], lhsT=wu_[wi][:, kc, :], rhs=ub[:, kc, :], start=(kc == 0), stop=(kc == KC - 1)),
                             reads=[("wu", wi), "ub"], writes=[("ps", bu)])
                    ai = wn["a"] % 2
                    wn["a"] += 1
                    S.op("act", lambda e: e.activation(out=av[ai][:, :], in_=psb[bg][:, 0:W2], func=AF.Silu), reads=[("ps", bg)], writes=[("av", ai)])
                    S.op("dve", lambda e, j=j: e.tensor_tensor(out=hm[:, j, :], in0=psb[bu][:, 0:W2], in1=av[ai][:, :], op=ALU.mult),
                         reads=[("ps", bu), ("av", ai)], writes=["hm"], wg=wgh)
                wga = S.newgroup()
                for m in range(KC):
                    wi = wn["wd"] % 3
                    wn["wd"] += 1
                    S.dma(wd_[wi][:, :, :], wd_s[m, :, :].rearrange("p (kc c) -> p kc c", c=128), writes=[("wd", wi)])
                    b = bank(0, 8)
                    for fc in range(FC):
                        S.op("pe", lambda e, fc=fc: e.matmul(psb[b][:, 0:W2], lhsT=wd_[wi][:, fc, :], rhs=hm[:, fc, :], start=(fc == 0), stop=(fc == FC - 1)),
                             reads=[("wd", wi), "hm"], writes=[("ps", b)])
                    en = S.ev()
                    S.op(en, (lambda e: e.activation(out=acc[:, m, :], in_=psb[b][:, 0:W2], func=AF.Copy)) if en == "act" else
                         (lambda e: e.tensor_copy(out=acc[:, m, :], in_=psb[b][:, 0:W2])), reads=[("ps", b)], writes=["acc"], wg=wga)
                normed_residual(acc, "acc", 2, h1, "h1", xt, "xt")
                normed_bf16(xt, "xt", 3, ub, "ub")
                wga = S.newgroup()
                for m in range(KC):
                    wi = wn["wp"] % 4
                    wn["wp"] += 1
                    S.dma(wpg_[wi][:, :, :], wpg_s[m, :, :].rearrange("p (kc c) -> p kc c", c=128), writes=[("wpg", wi)])
                    S.dma(wpp_[wi][:, :, :], wpp_s[m, :, :].rearrange("p (kc c) -> p kc c", c=128), writes=[("wpp", wi)])
                    bg = bank(0, 8)
                    for kc in range(KC):
                        S.op("pe", lambda e, kc=kc: e.matmul(psb[bg][:, 0:W2], lhsT=wpg_[wi][:, kc, :], rhs=ub[:, kc, :], start=(kc == 0), stop=(kc == KC - 1)),
                             reads=[("wpg", wi), "ub"], writes=[("ps", bg)])
                    be = bank(0, 8)
                    for kc in range(2):
                        S.op("pe", lambda e, kc=kc: e.matmul(psb[be][:, 0:W2], lhsT=wpp_[wi][:, kc, :], rhs=ptb[:, kc, :], start=(kc == 0), stop=(kc == 1)),
                             reads=[("wpp", wi), "ptb"], writes=[("ps", be)])
                    ai = wn["a"] % 2
                    wn["a"] += 1
                    S.op("act", lambda e: e.activation(out=av[ai][:, :], in_=psb[bg][:, 0:W2], func=AF.Sigmoid), reads=[("ps", bg)], writes=[("av", ai)])
                    S.op("dve", lambda e, m=m: e.tensor_tensor(out=acc[:, m, :], in0=psb[be][:, 0:W2], in1=av[ai][:, :], op=ALU.mult),
                         reads=[("ps", be), ("av", ai)], writes=["acc"], wg=wga)
                normed_residual(acc, "acc", 4, xt, "xt", h1, "h1")
                S.dma(yT[:, o:o + W2].rearrange("(kc p) t -> p kc t", p=128), h1[:, :, :], reads=["h1"], writes=[("yT", o)])
            S.barrier()

    C.I = I
    C.scr = dict(pj_fm=pj_fm, pj_z=pj_z, pj_dt=pj_dt, cs_fm=cs_fm, cs_dt=cs_dt, ch_fm=ch_fm, mix=mix,
                 wg_s=wg_s, wu_s=wu_s, wd_s=wd_s, wo_s=wo_s, wpg_s=wpg_s, wpp_s=wpp_s)

    S.barrier()
    st.close()
    return nc


def _pcol(v):
    return np.ascontiguousarray(np.asarray(v, np.float32).reshape(-1, 128).T)


def host_consts(NT):
    J = NT // 128
    N = 2 * NT
    c = {}
    c["ident"] = np.eye(128, dtype=np.float32)
    r = np.arange(128)
    c["tri"] = (r[:, None] <= r[None, :]).astype(np.float32)
    nm = np.zeros((128, 2, 128), np.float32)
    nm[:, 0, :] = np.where(r[:, None] <= r[None, :], 0.0, NEG)
    nm[:, 1, :] = np.where(r[:, None] >= r[None, :], 0.0, NEG)
    c["negmask"] = nm.reshape(128, 256)
    sm = np.zeros((32, 32, 128), np.float32)
    for k in range(32):
        sm[k, k, :] = 1.0
    c["selmat"] = sm.reshape(32, 32 * 128)
    p = np.arange(128, dtype=np.float64)
    k1 = np.arange(128, dtype=np.float64)
    ft = np.zeros((128, J, 2, 2, 128), np.float64)
    et = np.zeros((128, J, 2, 128), np.float64)
    for j in range(J):
        for pc in range(2):
            n = J * (p + 128 * pc) + j
            ang = 2 * np.pi * (k1[None, :] + 0.5) * n[:, None] / N
            ft[:, j, pc, 0, :] = np.cos(ang)
            ft[:, j, pc, 1, :] = -np.sin(ang)
        n = J * p + j
        ang = 2 * np.pi * (k1[:, None] + 0.5) * n[None, :] / N
        et[:, j, 0, :] = (2.0 / N) * np.cos(ang)
        et[:, j, 1, :] = -(2.0 / N) * np.sin(ang)
    c["ftab"] = ft.reshape(128, -1).astype(np.float32)
    c["etab"] = et.reshape(128, -1).astype(np.float32)
    CS = 128 // J
    w2 = np.zeros((128, 3, 128), np.float64)
    for a in range(J):
        for b in range(J):
            ang = 2 * np.pi * a * b / J
            for cc in range(CS):
                w2[a * CS + cc, 0, b * CS + cc] = np.cos(ang)
                w2[a * CS + cc, 1, b * CS + cc] = -np.sin(ang)
                w2[a * CS + cc, 2, b * CS + cc] = np.sin(ang)
    c["w2tab"] = w2.reshape(128, -1).astype(np.float32)
    return c


def _grp(w2d):
    taps, Cc = w2d.shape
    a = np.asarray(w2d, np.float32).T.reshape(Cc // 128, 128, taps).transpose(1, 0, 2)
    return np.ascontiguousarray(a.reshape(128, -1))


def _hyena_tables(NT, L, mode):
    J = NT // 128
    N = 2 * NT
    sgn = np.zeros((2, N), np.float64)
    dr = np.zeros((2, N), np.int64)
    pos = np.zeros((2, N), np.float64)
    i = np.arange(N)
    lo = i < NT
    hi = i > NT
    sgn[0, lo] = 1.0; dr[0, lo] = 0; pos[0, lo] = i[lo]
    sgn[0, hi] = -1.0; dr[0, hi] = 1; pos[0, hi] = N - i[hi]
    if mode == "first":
        sgn[1, lo] = 1.0; dr[1, lo] = 1; pos[1, lo] = NT - i[lo]
        sgn[1, hi] = -1.0; dr[1, hi] = 1; pos[1, hi] = NT + N - i[hi]
    elif mode == "second":
        sgn[1, lo] = 1.0; dr[1, lo] = 0; pos[1, lo] = NT + i[lo]
        sgn[1, hi] = -1.0; dr[1, hi] = 0; pos[1, hi] = i[hi] - NT
    t = pos / (L - 1)
    fb = np.linspace(1e-4, 15.0, 16)
    ang = fb[None, None, :] * (2.0 * np.pi * pos[..., None] / L)
    z = np.concatenate([t[..., None], np.cos(ang), -np.sin(ang)], axis=-1)
    def reord(a):
        sh = a.shape[2:]
        a = a.reshape((2, 2, 128, J) + sh)
        a = np.moveaxis(a, 3, 2)
        return a.reshape((2 * N,) + sh)
    zt = np.ascontiguousarray(reord(z).T.astype(np.float32))
    mf = reord(sgn * (dr == 0))
    mb = reord(sgn * (dr == 1))
    mfb = np.concatenate([np.tile(mf[None], (64, 1)), np.tile(mb[None], (64, 1))], 0).astype(np.float32)
    tq = reord(t).reshape(2 * 2 * J, 128)
    negt = np.ascontiguousarray((-tq.T).astype(np.float32))
    return zt, mfb, negt


def host_inputs(inp, NT):
    J = NT // 128
    NTH = NT + 4
    g = lambda k: np.asarray(inp[k], np.float32)
    W = g("w_in")[0]
    consts = host_consts(NT)
    xp, xs = g("x_prompt"), g("x_sample")
    pp, psm = g("p_prompt")[0], g("p_sample")[0]
    cw, cb = g("ssd_conv_w")[0], g("ssd_conv_b")[0]
    hw, hb = g("hy_conv_w")[0], g("hy_conv_b")[0]
    dtb, alog = g("ssd_dt_bias")[0], g("ssd_a_log")[0]
    max_decay = math.log(1e-2) / 0.3
    min_decay = math.log(1e-2) / 1.5
    deltas = np.abs(np.linspace(min_decay, max_decay, 1024)).astype(np.float32)
    w4 = g("hy_f_w4")[0]
    common = dict(consts)
    common.update(
        w_in_fm=np.ascontiguousarray(np.concatenate([W[:, 1024:2560], W[:, 2592:5664]], 1)),
        w_in_tm=np.ascontiguousarray(np.concatenate([W[:, 0:1024], W[:, 2560:2592]], 1)),
        w_cs_fm=np.ascontiguousarray(W[:, 1024:2304]),
        w_ch_fm=np.ascontiguousarray(W[:, 3616:5664]),
        nmp=_pcol(g("norm_mix_pre")[0]),
        ssd_cw=_grp(cw), ssd_cb=_pcol(cb),
        dtb=dtb.reshape(1, 32), alog=alog.reshape(1, 32),
        ssd_d=g("ssd_d")[0].reshape(1, 16), ssd_nw=_pcol(g("ssd_norm_w")[0]),
        hy_cw=_grp(hw), hy_cw_ctx=_grp(hw[:, 1024:]), hy_cb=_pcol(hb),
        fw1=g("hy_f_w1")[0], fb1=g("hy_f_b1")[0].reshape(64, 1), fw2=g("hy_f_w2")[0], fb2=g("hy_f_b2")[0].reshape(64, 1),
        fw3=np.ascontiguousarray(np.concatenate([g("hy_f_w3")[0]] * 2, 1)),
        fb3=np.concatenate([g("hy_f_b3")[0]] * 2).reshape(128, 1),
        ffreq=np.concatenate([g("hy_f_freq")[0]] * 2).reshape(128, 1),
        fw4s=np.ascontiguousarray(np.concatenate([w4[:, :1024], w4[:, 1024:]], 0)),
        absd=np.ascontiguousarray(np.tile(deltas[None], (128, 1))),
        hy_bias=g("hy_bias")[0].reshape(1, 1024), hy_nw=_pcol(g("hy_norm_w")[0]),
        w_out=g("w_out")[0], nm_post=_pcol(g("norm_mix_post")[0]), nf_pre=_pcol(g("norm_ffn_pre")[0]),
        w_gate=g("w_gate")[0], w_up=g("w_up")[0], w_down=g("w_down")[0], nf_post=_pcol(g("norm_ffn_post")[0]),
        ple_npre=_pcol(g("ple_norm_pre")[0]), w_pg=g("w_ple_gate")[0], w_pp=g("w_ple_proj")[0],
        ple_npost=_pcol(g("ple_norm_post")[0]),
    )
    tabs = {m: _hyena_tables(NT, (NT if m == "prompt" else 2 * NT), m) for m in ("prompt", "first", "second")}

    def halo_T(seq, a, b):
        L = seq.shape[0]
        out = np.zeros((b - a + 4, seq.shape[1]), np.float32)
        lo, hi = max(a - 2, 0), min(b + 2, L)
        out[lo - (a - 2):hi - (a - 2)] = seq[lo:hi]
        return np.ascontiguousarray(out.T)

    maps = []
    nprompt = xp.shape[0]
    for c in range(8):
        m = dict(common)
        if c < nprompt:
            mode = "prompt"
            m["xo"] = halo_T(xp[c], 0, NT)
            m["xcs"] = np.zeros((D, NTH), np.float32)
            m["xch"] = np.zeros((D, NTH), np.float32)
            m["pT"] = np.ascontiguousarray(pp[c].T)
            dirc, rev = 0, False
            sel = (0.0, 0.0)
        else:
            s, hh = (c - nprompt) // 2, (c - nprompt) % 2
            oh = 1 - hh
            mode = "first" if hh == 0 else "second"
            m["xo"] = halo_T(xs[s], hh * NT, (hh + 1) * NT)
            ctx = halo_T(xs[s], oh * NT, (oh + 1) * NT)
            m["xch"] = ctx
            rev = hh == 0
            m["xcs"] = np.ascontiguousarray(ctx[:, ::-1]) if rev else ctx
            m["pT"] = np.ascontiguousarray(psm[s, hh * NT:(hh + 1) * NT].T)
            dirc = 1 if hh == 0 else 0
            sel = (1.0, 0.0) if hh == 1 else (0.0, 1.0)
        m["w_cs_tm"] = np.ascontiguousarray(W[:, 2560 + dirc * 16:2560 + dirc * 16 + 16])
        cwc = cw[:, :1280]
        m["ssd_cw_ctx"] = _grp(cwc[::-1] if rev else cwc)
        m["ssd_cb_ctx"] = _pcol(cb[:1280])
        m["dtb_ctx"] = dtb[dirc].reshape(1, 16)
        m["alog_ctx"] = alog[dirc].reshape(1, 16)
        m["sel"] = np.tile(np.asarray(sel, np.float32)[None], (128, 1))
        m["zt"], m["mfb"], m["negt"] = tabs[mode]
        maps.append({k: np.ascontiguousarray(v, dtype=np.float32) for k, v in m.items()})
    return maps


_NC_CACHE = {}


def kernel(**inputs):
    NT = 4096
    if NT not in _NC_CACHE:
        _NC_CACHE[NT] = build(NT)
    nc = _NC_CACHE[NT]
    maps = host_inputs(inputs, NT)
    res = run_bass_kernel_spmd(nc, maps, core_ids=list(range(8)))
    outs = [np.asarray(r["yT"], np.float32) for r in res.results]
    y_prompt = np.stack([outs[c].T for c in range(4)], 0)
    y_sample = np.stack([np.concatenate([outs[4 + 2 * s].T, outs[5 + 2 * s].T], 0) for s in range(2)], 0)
    return (np.ascontiguousarray(y_prompt), np.ascontiguousarray(y_sample))
```

```python
import contextlib
import math
import numpy as np
import concourse.bass as bass
import concourse.mybir as mybir
from concourse.bass_utils import run_bass_kernel_spmd

F32 = mybir.dt.float32
BF16 = mybir.dt.bfloat16
AF = mybir.ActivationFunctionType
ALU = mybir.AluOpType
AX = mybir.AxisListType

D = 1024
KC = 8
PLE = 256
NH = 16
HP = 64
NS = 128
FFN = 2816
FC = 22
EPS = 1e-6
NEG = -30000.0


class Sched:
    NDMA = 40

    def __init__(self, nc, stack):
        self.nc = nc
        self.stack = stack
        self.gen = 0
        self.eng = {"pe": nc.tensor, "act": nc.scalar, "dve": nc.vector, "pool": nc.gpsimd, "sp": nc.sync}
        self.sem = {}
        self.cnt = {}
        for e in ("pe", "act", "dve", "pool"):
            self.sem[e] = stack.enter_context(nc.semaphore("sem_" + e))
            self.cnt[e] = 0
        self.dsem = [stack.enter_context(nc.semaphore("dsem%d" % i)) for i in range(self.NDMA)]
        self.dcnt = [0] * self.NDMA
        self.dnext = 0
        self.waited = {e: {} for e in self.eng}
        self.last_w = {}
        self.readers = {}
        self.nops = 0
        self.rr = 0
        self.rev = {}
        self.wgid = {}
        self.pre = {}
        self.gcount = 0

    def newgroup(self):
        self.gcount += 1
        return self.gcount

    def merge(self, new_key, keys):
        toks = []
        for k in keys:
            w = self.last_w.get(k)
            if w is not None:
                toks.extend(w if isinstance(w, list) else [w])
            self.rev.setdefault(k, set()).add(new_key)
        self.last_w[new_key] = toks
        self.readers[new_key] = []

    def _wait(self, e, tok):
        sem, val, name = tok
        if self.waited[e].get(name, 0) >= val:
            return
        self.waited[e][name] = val
        self.eng[e].wait_ge(sem, val)

    def _deps(self, e, reads, writes, wg=None):
        toks = []
        for k in reads:
            w = self.last_w.get(k)
            if w is not None:
                toks.extend(w if isinstance(w, list) else [w])
        for k in writes:
            if wg is not None and self.wgid.get(k) == wg:
                toks.extend(self.pre.get(k, ()))
                toks.extend(self.readers.get(k, ()))
                continue
            w = self.last_w.get(k)
            pre = []
            if w is not None:
                pre.extend(w if isinstance(w, list) else [w])
            pre.extend(self.readers.get(k, ()))
            for m in self.rev.get(k, ()):
                pre.extend(self.readers.get(m, ()))
            toks.extend(pre)
            if wg is not None:
                self.pre[k] = pre
        for t in toks:
            if t[2] == e and e == "pe":
                continue
            self._wait(e, t)

    def _commit(self, tok, reads, writes, wg=None):
        for k in reads:
            self.readers.setdefault(k, []).append(tok)
        for k in writes:
            if wg is not None and self.wgid.get(k) == wg:
                self.last_w[k].append(tok)
                continue
            self.last_w[k] = [tok] if wg is not None else tok
            self.readers[k] = []
            self.wgid[k] = wg

    def op(self, e, fn, reads=(), writes=(), wg=None):
        self._deps(e, reads, writes, wg)
        ins = fn(self.eng[e])
        self.cnt[e] += 1
        ins.then_inc(self.sem[e], 1)
        tok = (self.sem[e], self.cnt[e], e)
        self._commit(tok, reads, writes, wg)
        self.nops += 1
        return tok

    def dma(self, out, in_, reads=(), writes=(), q="sp", wg=None, **kw):
        i = self.dnext
        self.dnext = (self.dnext + 1) % self.NDMA
        name = "d%d" % i
        if self.dcnt[i] > 0:
            self._wait(q, (self.dsem[i], self.dcnt[i], name))
        self._deps(q, reads, writes, wg)
        ins = self.eng[q].dma_start(out=out, in_=in_, **kw)
        self.dcnt[i] += 16
        ins.then_inc(self.dsem[i], 16)
        tok = (self.dsem[i], self.dcnt[i], name)
        self._commit(tok, reads, writes, wg)
        self.nops += 1
        return tok

    def barrier(self):
        toks = [(self.sem[e], self.cnt[e], e) for e in ("pe", "act", "dve", "pool") if self.cnt[e] > 0]
        toks += [(self.dsem[i], self.dcnt[i], "d%d" % i) for i in range(self.NDMA) if self.dcnt[i] > 0]
        for e in self.eng:
            for t in toks:
                if t[2] == e:
                    continue
                self._wait(e, t)
        self.last_w = {}
        self.readers = {}
        self.rev = {}
        self.wgid = {}
        self.pre = {}
        self.gen += 1
        for e in ("pe", "act", "dve", "pool"):
            self.sem[e] = self.stack.enter_context(self.nc.semaphore("sem_%s_%d" % (e, self.gen)))
            self.cnt[e] = 0
            for e2 in self.eng:
                self.waited[e2][e] = 0

    def ev(self):
        self.rr += 1
        return "act" if self.rr % 2 else "dve"


class Ctx:
    pass


def build(NT, stages=("p0", "ssd", "hy", "wprep", "p2"), dbg=False, hy_stop=99):
    J = NT // 128
    NTH = NT + 4
    NFFT = 2 * NT
    CS = 128 // J
    NSUB = J
    nc = bass.Bass("TRN2", target_bir_lowering=False)
    st = contextlib.ExitStack()
    C = Ctx()
    C.nc = nc

    def din(name, shape, dt=F32):
        return nc.dram_tensor(name, list(shape), dt, kind="ExternalInput").ap()

    def dscr(name, shape, dt):
        kind = "ExternalOutput" if dbg else "Internal"
        return nc.dram_tensor(name, list(shape), dt, kind=kind).ap()

    I = {}
    for name, shape in [
        ("xo", (D, NTH)), ("xcs", (D, NTH)), ("xch", (D, NTH)), ("pT", (PLE, NT)),
        ("w_in_fm", (D, 4608)), ("w_in_tm", (D, 1056)), ("w_cs_fm", (D, 1280)), ("w_cs_tm", (D, 16)),
        ("w_ch_fm", (D, 2048)), ("nmp", (128, KC)),
        ("ssd_cw", (128, 12 * 5)), ("ssd_cw_ctx", (128, 10 * 5)), ("ssd_cb", (128, 12)), ("ssd_cb_ctx", (128, 10)),
        ("dtb", (1, 32)), ("alog", (1, 32)), ("dtb_ctx", (1, 16)), ("alog_ctx", (1, 16)),
        ("ssd_d", (1, 16)), ("ssd_nw", (128, 8)), ("sel", (128, 2)),
        ("hy_cw", (128, 24 * 3)), ("hy_cw_ctx", (128, 16 * 3)), ("hy_cb", (128, 24)),
        ("zt", (33, 2 * NFFT)), ("fw1", (33, 64)), ("fb1", (64, 1)), ("fw2", (64, 64)), ("fb2", (64, 1)),
        ("fw3", (64, 128)), ("fb3", (128, 1)), ("ffreq", (128, 1)), ("fw4s", (128, 1024)),
        ("mfb", (128, 2 * NFFT)), ("negt", (128, 4 * J)), ("absd", (128, 1024)),
        ("hy_bias", (1, 1024)), ("hy_nw", (128, 8)),
        ("ftab", (128, J * 4 * 128)), ("etab", (128, J * 2 * 128)), ("w2tab", (128, 3 * 128)),
        ("ident", (128, 128)), ("tri", (128, 128)), ("negmask", (128, 2 * 128)), ("selmat", (32, 32 * 128)),
        ("w_out", (2048, D)), ("nm_post", (128, KC)), ("nf_pre", (128, KC)), ("w_gate", (D, FFN)),
        ("w_up", (D, FFN)), ("w_down", (FFN, D)), ("nf_post", (128, KC)), ("ple_npre", (128, KC)),
        ("w_pg", (D, D)), ("w_pp", (PLE, D)), ("ple_npost", (128, KC)),
    ]:
        I[name] = din(name, shape)
    yT = nc.dram_tensor("yT", [D, NT], F32, kind="ExternalOutput").ap()

    pj_fm = dscr("pj_fm", (4608, NTH), BF16)
    pj_z = dscr("pj_z", (NT, 1024), BF16)
    pj_dt = dscr("pj_dt", (128, J * 32), F32)
    cs_fm = dscr("cs_fm", (1280, NTH), BF16)
    cs_dt = dscr("cs_dt", (128, J * 16), F32)
    ch_fm = dscr("ch_fm", (2048, NTH), BF16)
    mix = dscr("mix", (2048, NT), BF16)
    wg_s = dscr("wg_s", (FC, 128, KC * 128), BF16)
    wu_s = dscr("wu_s", (FC, 128, KC * 128), BF16)
    wd_s = dscr("wd_s", (KC, 128, FC * 128), BF16)
    wo_s = dscr("wo_s", (KC, 128, 16 * 128), BF16)
    wpg_s = dscr("wpg_s", (KC, 128, KC * 128), BF16)
    wpp_s = dscr("wpp_s", (KC, 128, 2 * 128), BF16)

    S = Sched(nc, st)
    C.S = S

    tcount = [0]

    def T(stack, name, shape, dt):
        tcount[0] += 1
        return stack.enter_context(nc.sbuf_tensor("sb%d_%s" % (tcount[0], name), list(shape), dt))

    ident_f = T(st, "ident_f", [128, 128], F32)
    ident_b = T(st, "ident_b", [128, 128], BF16)
    ones_b = T(st, "ones_b", [128, 128], BF16)
    psb = [st.enter_context(nc.psum_tensor("psb%d" % i, [128, 512], F32)) for i in range(8)]
    S.dma(ident_f[:, :], I["ident"][:, :], writes=["ident_f"])
    S.op("dve", lambda e: e.tensor_copy(out=ident_b[:, :], in_=ident_f[:, :]), reads=["ident_f"], writes=["ident_b"])
    S.op("pool", lambda e: e.memset(ones_b[:, :], 1.0), writes=["ones_b"])

    psn = [0]

    def bank(lo=0, hi=8):
        i = lo + psn[0] % (hi - lo)
        psn[0] += 1
        return i

    def rms_rstd(src, src_keys, w, rstd, rstd_key, sq, sq_key, banks=(6, 8)):
        S.op("act", lambda e: e.activation(out=sq[:, :, 0:w], in_=src[:, :, 0:w], func=AF.Square),
             reads=list(src_keys), writes=[sq_key])
        b = bank(*banks)
        for kc in range(KC):
            S.op("pe", lambda e, kc=kc: e.matmul(psb[b][:, 0:w], lhsT=ones_b[:, :], rhs=sq[:, kc, 0:w],
                                                 start=(kc == 0), stop=(kc == KC - 1)),
                 reads=[sq_key, "ones_b"], writes=[("ps", b)])
        S.op("act", lambda e: e.activation(out=rstd[:, 0:w], in_=psb[b][:, 0:w], func=AF.Ln, scale=1.0 / D, bias=eps_t[:, 0:1]),
             reads=[("ps", b), "eps_t"], writes=[rstd_key])
        S.op("act", lambda e: e.activation(out=rstd[:, 0:w], in_=rstd[:, 0:w], func=AF.Exp, scale=-0.5),
             reads=[rstd_key], writes=[rstd_key])

    eps_t = T(st, "eps_t", [128, 1], F32)
    S.op("pool", lambda e: e.memset(eps_t[:, :], EPS), writes=["eps_t"])
    C.rms_rstd = rms_rstd

    def wprep_gen(stack):
        wst = [T(stack, "wst%d" % i, [128, KC, 128], F32) for i in range(2)]
        wsb = [T(stack, "wsb%d" % i, [128, KC, 128], BF16) for i in range(2)]
        wc = 0
        for (src, nk, ncc, dst) in ((I["w_out"], 16, KC, wo_s), (I["w_gate"], KC, FC, wg_s), (I["w_up"], KC, FC, wu_s),
                                    (I["w_down"], FC, KC, wd_s), (I["w_pg"], KC, KC, wpg_s), (I["w_pp"], 2, KC, wpp_s)):
            for m in range(ncc):
                for k0 in range(0, nk, KC):
                    kn = min(KC, nk - k0)
                    i = wc % 2
                    wc += 1
                    S.dma(wst[i][:, 0:kn, :], src[k0 * 128:(k0 + kn) * 128, m * 128:(m + 1) * 128].rearrange("(kc p) c -> p kc c", p=128), writes=[("wst", i)])
                    en = ("act", "dve")[wc % 2]
                    if en == "act":
                        S.op("act", lambda e: e.activation(out=wsb[i][:, 0:kn, :], in_=wst[i][:, 0:kn, :], func=AF.Copy), reads=[("wst", i)], writes=[("wsb", i)])
                    else:
                        S.op("dve", lambda e: e.tensor_copy(out=wsb[i][:, 0:kn, :], in_=wst[i][:, 0:kn, :]), reads=[("wst", i)], writes=[("wsb", i)])
                    S.dma(dst[m, :, k0 * 128:(k0 + kn) * 128].rearrange("p (kc c) -> p kc c", c=128), wsb[i][:, 0:kn, :],
                          reads=[("wsb", i)], writes=[("wdst", id(dst), m, k0)])
                    yield

    wprep_inline = ("p0" in stages) and ("wprep" in stages)
    if "p0" in stages:
        with contextlib.ExitStack() as ph:
            u = T(ph, "u", [128, KC, NTH], BF16)
            xt = [T(ph, "xt%d" % i, [128, KC, 512], F32) for i in range(2)]
            sq = T(ph, "sq0", [128, KC, 512], BF16)
            rstd = T(ph, "rstd0", [128, 512], F32)
            nmp = T(ph, "nmp", [128, KC], F32)
            NWB = 4
            wf = [T(ph, "wf%d" % i, [128, KC, 128], F32) for i in range(NWB)] + [T(ph, "wfL", [128, KC, 512], F32)]
            wb = [T(ph, "wb%d" % i, [128, KC, 128], BF16) for i in range(NWB)] + [T(ph, "wbL", [128, KC, 512], BF16)]
            stg = [T(ph, "stg%d" % i, [128, NTH], BF16) for i in range(2)]
            stz = [T(ph, "stz%d" % i, [128, 512], BF16) for i in range(2)]
            std = [T(ph, "std%d" % i, [128, 32], F32) for i in range(2)]
            S.dma(nmp[:, :], I["nmp"][:, :], writes=["nmp"])
            wgen = wprep_gen(ph) if wprep_inline else iter(())
            tiles = [(o, 512) for o in range(0, NT, 512)] + [(NT, 4)]
            wn = [0]
            sn = [0]

            def load_w(src, c0, ncol):
                if ncol > 128:
                    i = NWB
                else:
                    i = wn[0] % NWB
                wn[0] += 1
                S.dma(wf[i][:, :, 0:ncol], src[:, c0:c0 + ncol].rearrange("(kc p) c -> p kc c", p=128),
                      writes=[("wf", i)])
                eng = "pool"
                S.op(eng, lambda e: e.tensor_copy(out=wb[i][:, :, 0:ncol], in_=wf[i][:, :, 0:ncol]),
                     reads=[("wf", i)], writes=[("wb", i)])
                return i

            def token_set(xsrc, fm_w, fm_dst, tm_w, tm_cols, tm_dsts):
                for ti, (o, w) in enumerate(tiles):
                    xi = ti % 2
                    S.dma(xt[xi][:, :, 0:w], xsrc[:, o:o + w].rearrange("(kc p) t -> p kc t", p=128),
                          writes=[("xt", xi)])
                    rms_rstd(xt[xi], [("xt", xi)], w, rstd, "n0rstd", sq, "n0sq")
                    for kc in range(KC):
                        S.op("dve", lambda e, kc=kc: e.scalar_tensor_tensor(
                            out=u[:, kc, o:o + w], in0=xt[xi][:, kc, 0:w], scalar=nmp[:, kc:kc + 1],
                            in1=rstd[:, 0:w], op0=ALU.mult, op1=ALU.mult),
                            reads=[("xt", xi), "nmp", "n0rstd"], writes=[("u", ti, kc)])
                ncols = fm_w.shape[1]
                ng = ncols // 128
                pre = {}
                for g in range(min(2, ng)):
                    pre[g] = load_w(fm_w, g * 128, 128)
                for g in range(ng):
                    if g + 2 < ng:
                        pre[g + 2] = load_w(fm_w, (g + 2) * 128, 128)
                    wi = pre.pop(g)
                    si = sn[0] % 2
                    sn[0] += 1
                    for ti, (o, w) in enumerate(tiles):
                        b = bank(0, 6)
                        for kc in range(KC):
                            S.op("pe", lambda e, kc=kc: e.matmul(psb[b][:, 0:w], lhsT=wb[wi][:, kc, 0:128], rhs=u[:, kc, o:o + w],
                                                                 start=(kc == 0), stop=(kc == KC - 1)),
                                 reads=[("wb", wi), ("u", ti, kc)], writes=[("ps", b)])
                        en = S.ev()
                        if en == "act":
                            S.op("act", lambda e: e.activation(out=stg[si][:, o:o + w], in_=psb[b][:, 0:w], func=AF.Copy),
                                 reads=[("ps", b)], writes=[("stg", si, ti)])
                        else:
                            S.op("dve", lambda e: e.tensor_copy(out=stg[si][:, o:o + w], in_=psb[b][:, 0:w]),
                                 reads=[("ps", b)], writes=[("stg", si, ti)])
                    for _ in range(2):
                        next(wgen, None)
                    S.dma(fm_dst[g * 128:(g + 1) * 128, :], stg[si][:, :],
                          reads=[("stg", si, ti) for ti in range(len(tiles))], writes=[("fm", id(fm_dst), g)])
                if tm_w is not None:
                    c0 = 0
                    for (ncol, dst, dcol0, stt) in tm_cols:
                        wi = load_w(tm_w, c0, ncol)
                        for blk in range(NT // 128):
                            b = bank(0, 6)
                            ti = (blk * 128) // 512
                            for kc in range(KC):
                                S.op("pe", lambda e, kc=kc: e.matmul(psb[b][:, 0:ncol], lhsT=u[:, kc, 2 + blk * 128:2 + (blk + 1) * 128],
                                                                     rhs=wb[wi][:, kc, 0:ncol], start=(kc == 0), stop=(kc == KC - 1)),
                                     reads=[("wb", wi), ("u", ti, kc), ("u", min(ti + 1, len(tiles) - 1), kc)], writes=[("ps", b)])
                            si = sn[0] % 2
                            sn[0] += 1
                            stile = stz if stt == "z" else std
                            en = S.ev()
                            if en == "act":
                                S.op("act", lambda e: e.activation(out=stile[si][:, 0:ncol], in_=psb[b][:, 0:ncol], func=AF.Copy),
                                     reads=[("ps", b)], writes=[("st" + stt, si)])
                            else:
                                S.op("dve", lambda e: e.tensor_copy(out=stile[si][:, 0:ncol], in_=psb[b][:, 0:ncol]),
                                     reads=[("ps", b)], writes=[("st" + stt, si)])
                            dap = dst[:, blk * ncol:(blk + 1) * ncol] if stt == "d" else dst[blk * 128:(blk + 1) * 128, dcol0:dcol0 + ncol]
                            S.dma(dap, stile[si][:, 0:ncol],
                                  reads=[("st" + stt, si)], writes=[("tm", id(dst), blk, dcol0)])
                        c0 += ncol

            token_set(I["xo"], I["w_in_fm"], pj_fm, I["w_in_tm"],
                      [(512, pj_z, 0, "z"), (512, pj_z, 512, "z"), (32, pj_dt, 0, "d")], None)
            token_set(I["xcs"], I["w_cs_fm"], cs_fm, I["w_cs_tm"], [(16, cs_dt, 0, "d")], None)
            token_set(I["xch"], I["w_ch_fm"], ch_fm, None, [], None)
            for _ in wgen:
                pass
            S.barrier()

    TW = min(512, NT)
    if "ssd" in stages:
        with contextlib.ExitStack() as ph:
            tri_f = T(ph, "tri_f", [128, 128], F32)
            ones_f = T(ph, "ones_f", [128, 128], F32)
            nmask_f = T(ph, "nmask_f", [128, 2, 128], F32)
            nmask = T(ph, "nmask", [128, 2, 128], BF16)
            selm = T(ph, "selm", [16, 16, 128], F32)
            cw = T(ph, "cw", [128, 12, 5], F32)
            cwc = T(ph, "cwc", [128, 10, 5], F32)
            cb = T(ph, "cb", [128, 12], F32)
            cbc = T(ph, "cbc", [128, 10], F32)
            dtb_bc = T(ph, "dtb_bc", [128, 32], F32)
            a_bc = T(ph, "a_bc", [128, 32], F32)
            dtbc_bc = T(ph, "dtbc_bc", [128, 16], F32)
            ac_bc = T(ph, "ac_bc", [128, 16], F32)
            d_bc = T(ph, "d_bc", [128, 16], F32)
            ddiag = T(ph, "ddiag", [128, 16, 128], BF16)
            nw = T(ph, "nw", [128, 8], F32)
            selt = T(ph, "selt", [128, 2], F32)
            dg = T(ph, "dg", [128, 5, 128], BF16)
            cin = [T(ph, "cin%d" % i, [128, NTH], BF16) for i in range(1)]
            ctmp = T(ph, "ctmp", [128, NT], BF16)
            Bf = T(ph, "Bf", [128, NT], BF16)
            Cf = T(ph, "Cf", [128, NT], BF16)
            x_tok = T(ph, "x_tok", [128, J, 512], BF16)
            B_tok = T(ph, "B_tok", [128, J, 128], BF16)
            dtr = T(ph, "dtr", [128, J, 32], F32)
            dtrc = T(ph, "dtrc", [128, J, 16], F32)
            dt_t = T(ph, "dt_t", [128, J, 16], F32)
            a_t = T(ph, "a_t", [128, J, 16], F32)
            lam = T(ph, "lam", [128, J, 16], F32)
            tot = T(ph, "tot", [128, J, 16], F32)
            tml = T(ph, "tml", [128, J, 16], F32)
            e1 = T(ph, "e1", [128, J, 16], F32)
            e2 = T(ph, "e2", [128, J, 16], F32)
            dec = T(ph, "dec", [128, J, 16], F32)
            w_t = T(ph, "w_t", [128, J, 16], F32)
            q_t = T(ph, "q_t", [128, J, 16], F32)
            qTc = [T(ph, "qTc%d" % i, [16, 128], F32) for i in range(2)]
            selbc = T(ph, "selbc", [16, 4, 4, 128], F32)
            nm4 = T(ph, "nm4", [128, 4, 128], BF16)
            xdt = [T(ph, "xdt%d" % i, [128, 512], BF16) for i in range(4)]
            stb_all = T(ph, "stb_all", [128, J, 512], BF16)
            st_f = T(ph, "st_f", [128, 512], F32)
            st_b = T(ph, "st_b", [128, 512], F32)
            st_c = T(ph, "st_c", [128, 512], F32)
            stf_bf_l = [T(ph, "stf_bf%d" % i, [128, 512], BF16) for i in range(2)]
            xw = [T(ph, "xw%d" % i, [128, 512], BF16) for i in range(2)]
            scT_l = [T(ph, "scT%d" % i, [128, 128], F32) for i in range(2)]
            Et = [T(ph, "Et%d" % i, [128, 4, 128], F32) for i in range(2)]
            MT = [T(ph, "MT%d" % i, [128, 4, 128], BF16) for i in range(2)]
            zt_ = [T(ph, "zt%d" % i, [128, 512], BF16) for i in range(2)]
            zs_l = [T(ph, "zs%d" % i, [128, 512], F32) for i in range(2)]
            ysb_l = [T(ph, "ysb%d" % i, [128, 512], F32) for i in range(2)]
            yt1_l = [T(ph, "yt1%d" % i, [128, 512], F32) for i in range(2)]
            yn_l = [T(ph, "yn%d" % i, [128, 512], BF16) for i in range(2)]
            ssq_l = [T(ph, "ssq%d" % i, [128, 1], F32) for i in range(2)]
            sqj_l = [T(ph, "sqj%d" % i, [128, 512], BF16) for i in range(2)]
            dummy = T(ph, "dummy", [128, 1], F32)
            SC = min(4, J)
            mst = [T(ph, "mst%d" % i, [128, 4, SC * 128], BF16) for i in range(2)]

            S.dma(tri_f[:, :], I["tri"][:, :], writes=["tri_f"])
            S.op("pool", lambda e: e.memset(ones_f[:, :], 1.0), writes=["ones_f"])
            S.dma(nmask_f[:, :, :], I["negmask"].rearrange("p (d t) -> p d t", d=2), writes=["nmask_f"])
            S.op("dve", lambda e: e.tensor_copy(out=nmask[:, :, :], in_=nmask_f[:, :, :]), reads=["nmask_f"], writes=["nmask"])
            S.dma(selm[:, :, :], I["selmat"][0:16, :].rearrange("k (h s) -> k h s", s=128)[:, 0:16, :], writes=["selm"])
            S.op("pool", lambda e: e.tensor_scalar(out=selm[:, :, :], in0=selm[:, :, :], scalar1=-1.0, scalar2=None, op0=ALU.mult),
                 reads=["selm"], writes=["selm"])
            for quad in range(4):
                for i_ in range(4):
                    col_ = (i_ % 2) * 8 + 2 * quad + i_ // 2
                    S.op("dve", lambda e, quad=quad, i_=i_, col_=col_: e.tensor_scalar(out=selbc[:, quad, i_, :], in0=selm[:, col_, :], scalar1=-1.0, scalar2=None, op0=ALU.mult),
                         reads=["selm"], writes=[("selbc", quad, i_)])
            S.merge("selbc", [("selbc", q_, i_) for q_ in range(4) for i_ in range(4)])
            for i_ in range(4):
                S.op("dve", lambda e, i_=i_: e.tensor_copy(out=nm4[:, i_, :], in_=nmask[:, i_ % 2, :]), reads=["nmask"], writes=[("nm4", i_)])
            S.merge("nm4", [("nm4", i_) for i_ in range(4)])
            S.dma(cw[:, :, :], I["ssd_cw"].rearrange("p (g j) -> p g j", j=5), writes=["cw"])
            S.dma(cwc[:, :, :], I["ssd_cw_ctx"].rearrange("p (g j) -> p g j", j=5), writes=["cwc"])
            S.dma(cb[:, :], I["ssd_cb"][:, :], writes=["cb"])
            S.dma(cbc[:, :], I["ssd_cb_ctx"][:, :], writes=["cbc"])
            S.dma(dtb_bc[:, :], I["dtb"][0, :].partition_broadcast(128), writes=["dtb_bc"])
            S.dma(a_bc[:, :], I["alog"][0, :].partition_broadcast(128), writes=["a_bc"])
            S.dma(dtbc_bc[:, :], I["dtb_ctx"][0, :].partition_broadcast(128), writes=["dtbc_bc"])
            S.dma(ac_bc[:, :], I["alog_ctx"][0, :].partition_broadcast(128), writes=["ac_bc"])
            S.dma(d_bc[:, :], I["ssd_d"][0, :].partition_broadcast(128), writes=["d_bc"])
            S.dma(nw[:, :], I["ssd_nw"][:, :], writes=["nw"])
            S.dma(selt[:, :], I["sel"][:, :], writes=["selt"])
            S.dma(dtr[:, :, :], pj_dt.rearrange("p (c k) -> p c k", k=32), writes=["dtr"])
            S.dma(dtrc[:, :, :], cs_dt.rearrange("p (c k) -> p c k", k=16), writes=["dtrc"])
            for t_, k_ in ((a_bc, "a_bc"), (ac_bc, "ac_bc")):
                S.op("act", lambda e, t_=t_: e.activation(out=t_[:, :], in_=t_[:, :], func=AF.Exp), reads=[k_], writes=[k_])
                S.op("dve", lambda e, t_=t_: e.tensor_scalar(out=t_[:, :], in0=t_[:, :], scalar1=-1.0, scalar2=None, op0=ALU.mult),
                     reads=[k_], writes=[k_])
            for h in range(16):
                S.op("dve", lambda e, h=h: e.tensor_scalar(out=ddiag[:, h, :], in0=ident_f[:, :], scalar1=d_bc[:, h:h + 1], scalar2=None, op0=ALU.mult),
                     reads=["ident_f", "d_bc"], writes=[("ddiag", h)])
            cn = [0]

            def conv_group(src, row0, wt, wkey, g, bt, bkey, dst, dkey):
                ci = 0
                S.dma(cin[ci][:, :], src[row0:row0 + 128, :], writes=[("cin", ci)])
                for j in range(5):
                    S.op("pool", lambda e, j=j: e.tensor_scalar(out=dg[:, j, :], in0=ident_f[:, :], scalar1=wt[:, g, j:j + 1], scalar2=None, op0=ALU.mult),
                         reads=["ident_f", wkey], writes=[("dg", j)])
                for o in range(0, NT, TW):
                    b = bank(6, 8)
                    for j in range(5):
                        S.op("pe", lambda e, j=j: e.matmul(psb[b][:, 0:TW], lhsT=dg[:, j, :], rhs=cin[ci][:, o + j:o + j + TW],
                                                           start=(j == 0), stop=(j == 4)),
                             reads=[("dg", j), ("cin", ci)], writes=[("ps", b)])
                    S.op("act", lambda e: e.activation(out=dst[:, o:o + TW], in_=psb[b][:, 0:TW], func=AF.Silu, bias=bt[:, g:g + 1]),
                         reads=[("ps", b), bkey], writes=[(dkey, o // TW)])
                return [(dkey, o // TW) for o in range(0, NT, TW)]

            def to_tok(srct, skeys, dst, dcol0, dkey):
                for c0 in range(0, J, 8):
                    nb = min(8, J - c0)
                    b = bank(6, 8)
                    pv = psb[b][:, :].bitcast(BF16).rearrange("p (a b) -> p a b", b=128)
                    for i in range(nb):
                        S.op("pe", lambda e, i=i: e.transpose(out=pv[:, i, :], in_=srct[:, (c0 + i) * 128:(c0 + i + 1) * 128], identity=ident_b[:, :]),
                             reads=list(skeys) + ["ident_b"], writes=[("ps", b)])
                    en = S.ev()
                    if en == "act":
                        S.op("act", lambda e: e.activation(out=dst[:, c0:c0 + nb, dcol0:dcol0 + 128], in_=pv[:, 0:nb, :], func=AF.Copy),
                             reads=[("ps", b)], writes=[(dkey, c0, dcol0)])
                    else:
                        S.op("dve", lambda e: e.tensor_copy(out=dst[:, c0:c0 + nb, dcol0:dcol0 + 128], in_=pv[:, 0:nb, :]),
                             reads=[("ps", b)], writes=[(dkey, c0, dcol0)])

            def dt_math(raw, rkey, segs, bias_t, bkey, a_src, akey, ncol, bwd_cols):
                nn = ncol
                for (d0, s0, n_) in segs:
                    S.op("dve", lambda e, d0=d0, s0=s0, n_=n_: e.tensor_tensor(
                        out=dt_t[:, :, d0:d0 + n_], in0=raw[:, :, s0:s0 + n_],
                        in1=bias_t[:, s0:s0 + n_].unsqueeze(1).broadcast_to([128, J, n_]), op=ALU.add),
                        reads=[rkey, bkey], writes=["dt_t"])
                S.op("act", lambda e: e.activation(out=dt_t[:, :, 0:nn], in_=dt_t[:, :, 0:nn], func=AF.Exp), reads=["dt_t"], writes=["dt_t"])
                S.op("act", lambda e: e.activation(out=dt_t[:, :, 0:nn], in_=dt_t[:, :, 0:nn], func=AF.Ln, bias=ones_f[:, 0:1]),
                     reads=["dt_t", "ones_f"], writes=["dt_t"])
                for (d0, s0, n_) in segs:
                    S.op("dve", lambda e, d0=d0, s0=s0, n_=n_: e.tensor_tensor(
                        out=a_t[:, :, d0:d0 + n_], in0=dt_t[:, :, d0:d0 + n_],
                        in1=a_src[:, s0:s0 + n_].unsqueeze(1).broadcast_to([128, J, n_]), op=ALU.mult),
                        reads=["dt_t", akey], writes=["a_t"])
                b1 = 4
                b2 = 5
                for c in range(J):
                    S.op("pe", lambda e, c=c: e.matmul(psb[b1][:, c * 16:c * 16 + nn], lhsT=tri_f[:, :], rhs=a_t[:, c, 0:nn], start=True, stop=True),
                         reads=["tri_f", "a_t"], writes=[("ps", b1)])
                    S.op("pe", lambda e, c=c: e.matmul(psb[b2][:, c * 16:c * 16 + nn], lhsT=ones_f[:, :], rhs=a_t[:, c, 0:nn], start=True, stop=True),
                         reads=["ones_f", "a_t"], writes=[("ps", b2)])
                pv1 = psb[b1][:, 0:J * 16].rearrange("p (c k) -> p c k", k=16)
                pv2 = psb[b2][:, 0:J * 16].rearrange("p (c k) -> p c k", k=16)
                S.op("dve", lambda e: e.tensor_copy(out=lam[:, :, 0:nn], in_=pv1[:, :, 0:nn]), reads=[("ps", b1)], writes=["lam"])
                S.op("act", lambda e: e.activation(out=tot[:, :, 0:nn], in_=pv2[:, :, 0:nn], func=AF.Copy), reads=[("ps", b2)], writes=["tot"])
                if bwd_cols:
                    lo, hi = bwd_cols
                    S.op("dve", lambda e: e.tensor_tensor(out=lam[:, :, lo:hi], in0=lam[:, :, lo:hi], in1=a_t[:, :, lo:hi], op=ALU.subtract),
                         reads=["lam", "a_t"], writes=["lam"])
                S.op("dve", lambda e: e.tensor_tensor(out=tml[:, :, 0:nn], in0=tot[:, :, 0:nn], in1=lam[:, :, 0:nn], op=ALU.subtract),
                     reads=["tot", "lam"], writes=["tml"])
                S.op("act", lambda e: e.activation(out=e1[:, :, 0:nn], in_=lam[:, :, 0:nn], func=AF.Exp), reads=["lam"], writes=["e1"])
                S.op("act", lambda e: e.activation(out=e2[:, :, 0:nn], in_=tml[:, :, 0:nn], func=AF.Exp), reads=["tml"], writes=["e2"])
                S.op("act", lambda e: e.activation(out=dec[:, :, 0:nn], in_=tot[:, :, 0:nn], func=AF.Exp), reads=["tot"], writes=["dec"])
                fhi = bwd_cols[0] if bwd_cols else nn
                S.op("dve", lambda e: e.tensor_tensor(out=w_t[:, :, 0:fhi], in0=dt_t[:, :, 0:fhi], in1=e2[:, :, 0:fhi], op=ALU.mult),
                     reads=["dt_t", "e2"], writes=["w_t"])
                S.op("dve", lambda e: e.tensor_scalar(out=q_t[:, :, 0:fhi], in0=lam[:, :, 0:fhi], scalar1=-1.0, scalar2=None, op0=ALU.mult),
                     reads=["lam"], writes=["q_t"])
                if bwd_cols:
                    lo, hi = bwd_cols
                    S.op("dve", lambda e: e.tensor_tensor(out=w_t[:, :, lo:hi], in0=dt_t[:, :, lo:hi], in1=e1[:, :, lo:hi], op=ALU.mult),
                         reads=["dt_t", "e1", "w_t"], writes=["w_t"])
                    S.op("dve", lambda e: e.tensor_copy(out=q_t[:, :, lo:hi], in_=lam[:, :, lo:hi]), reads=["lam", "q_t"], writes=["q_t"])

            def make_xw(c, col0, xi):
                S.op("pool", lambda e: e.tensor_tensor(out=xw[xi][:, :].rearrange("p (h d) -> p h d", d=64),
                                                       in0=x_tok[:, c, :].rearrange("p (h d) -> p h d", d=64),
                                                       in1=w_t[:, c, col0:col0 + 8].unsqueeze(2).broadcast_to([128, 8, 64]), op=ALU.mult),
                     reads=["x_tok_all", "w_t"], writes=[("xw", xi)])

            def state_step(c, col0, xi, stt, skey, pb):
                S.op("pe", lambda e: e.matmul(psb[pb][:, :], lhsT=B_tok[:, c, :], rhs=xw[xi][:, :], start=True, stop=True),
                     reads=["B_tok_all", ("xw", xi)], writes=[("ps", pb)])
                S.op("dve", lambda e: e.tensor_tensor(out=stt[:, :].rearrange("p (h d) -> p h d", d=64),
                                                      in0=stt[:, :].rearrange("p (h d) -> p h d", d=64),
                                                      in1=dec[:, c, col0:col0 + 8].unsqueeze(2).broadcast_to([128, 8, 64]), op=ALU.mult),
                     reads=[skey, "dec"], writes=[skey])
                S.op("dve", lambda e: e.tensor_tensor(out=stt[:, :], in0=stt[:, :], in1=psb[pb][:, :], op=ALU.add),
                     reads=[skey, ("ps", pb)], writes=[skey])

            for g2 in range(2):
                keys = conv_group(cs_fm, 1024 + g2 * 128, cwc, "cwc", 8 + g2, cbc, "cbc", ctmp, "ctmp")
                to_tok(ctmp, keys, B_tok, 0, "B_tok")
                for xg in range(4):
                    keys = conv_group(cs_fm, (g2 * 4 + xg) * 128, cwc, "cwc", g2 * 4 + xg, cbc, "cbc", ctmp, "ctmp")
                    to_tok(ctmp, keys, x_tok, xg * 128, "x_tok")
                S.op("pool", lambda e: e.memset(st_c[:, :], 0.0), writes=["st_c"])
                S.merge("x_tok_all", [("x_tok", c0, x0) for c0 in range(0, J, 8) for x0 in range(0, 512, 128)])
                S.merge("B_tok_all", [("B_tok", c0, 0) for c0 in range(0, J, 8)])
                dt_math(dtrc, "dtrc", [(0, g2 * 8, 8)], dtbc_bc, "dtbc_bc", ac_bc, "ac_bc", 8, None)
                for c in range(J):
                    make_xw(c, 0, c % 2)
                    state_step(c, 0, c % 2, st_c, "st_c", 6 + c % 2)
                S.op("dve", lambda e: e.tensor_scalar(out=st_f[:, :], in0=st_c[:, :], scalar1=selt[:, 0:1], scalar2=None, op0=ALU.mult),
                     reads=["st_c", "selt"], writes=["st_f"])
                S.op("dve", lambda e: e.tensor_scalar(out=st_b[:, :], in0=st_c[:, :], scalar1=selt[:, 1:2], scalar2=None, op0=ALU.mult),
                     reads=["st_c", "selt"], writes=["st_b"])
                kB = conv_group(pj_fm, 1024 + g2 * 128, cw, "cw", 8 + g2, cb, "cb", Bf, "Bf")
                to_tok(Bf, kB, B_tok, 0, "B_tok")
                kC = conv_group(pj_fm, 1280 + g2 * 128, cw, "cw", 10 + g2, cb, "cb", Cf, "Cf")
                for xg in range(4):
                    keys = conv_group(pj_fm, (g2 * 4 + xg) * 128, cw, "cw", g2 * 4 + xg, cb, "cb", ctmp, "ctmp")
                    to_tok(ctmp, keys, x_tok, xg * 128, "x_tok")
                S.merge("x_tok_all", [("x_tok", c0, x0) for c0 in range(0, J, 8) for x0 in range(0, 512, 128)])
                S.merge("B_tok_all", [("B_tok", c0, 0) for c0 in range(0, J, 8)])
                S.merge("BC_all", kB + kC)
                dt_math(dtr, "dtr", [(0, g2 * 8, 8), (8, 16 + g2 * 8, 8)], dtb_bc, "dtb_bc", a_bc, "a_bc", 16, (8, 16))
                for c in range(J - 1, -1, -1):
                    S.op("act", lambda e, c=c: e.activation(out=stb_all[:, c, :], in_=st_b[:, :], func=AF.Copy),
                         reads=["st_b"], writes=[("stb", c)])
                    make_xw(c, 8, c % 2)
                    state_step(c, 8, c % 2, st_b, "st_b", 6 + c % 2)
                def fwdA(c):
                    tsl = slice(c * 128, (c + 1) * 128)
                    zi = c % 2
                    scT, zs, ysb, yt1, yn, ssq, sqj, stf_bf = (scT_l[zi], zs_l[zi], ysb_l[zi], yt1_l[zi], yn_l[zi], ssq_l[zi], sqj_l[zi], stf_bf_l[zi])
                    kx = lambda k_: (k_, zi)
                    qi = c % 2
                    S.dma(zt_[zi][:, :], pj_z[c * 128:(c + 1) * 128, g2 * 512:(g2 + 1) * 512], writes=[("z", zi)])
                    S.op("act", lambda e: e.activation(out=stf_bf[:, :], in_=st_f[:, :], func=AF.Copy), reads=["st_f"], writes=[kx("stf_bf")])
                    S.op("pe", lambda e: e.matmul(psb[0][:, 0:128], lhsT=Bf[:, tsl], rhs=Cf[:, tsl], start=True, stop=True),
                         reads=["BC_all"], writes=[("ps", 0)])
                    S.op("act", lambda e: e.activation(out=scT[:, :], in_=psb[0][:, 0:128], func=AF.Copy), reads=[("ps", 0)], writes=[kx("scT")])
                    qi = c % 2
                    S.op("pe", lambda e: e.transpose(out=psb[0][0:16, 128:256], in_=q_t[:, c, :], identity=ident_f[:, :]),
                         reads=["q_t", "ident_f", kx("scT")], writes=[("ps", 0)])
                    S.op("act", lambda e: e.activation(out=qTc[qi][:, :], in_=psb[0][0:16, 128:256], func=AF.Copy),
                         reads=[("ps", 0)], writes=[("qTc", qi)])
                    for d_ in range(2):
                        xi_ = qi * 2 + d_
                        S.op("pool", lambda e, d_=d_, xi_=xi_: e.tensor_tensor(out=xdt[xi_][:, :].rearrange("p (h d) -> p h d", d=64),
                                                                               in0=x_tok[:, c, :].rearrange("p (h d) -> p h d", d=64),
                                                                               in1=dt_t[:, c, d_ * 8:d_ * 8 + 8].unsqueeze(2).broadcast_to([128, 8, 64]), op=ALU.mult),
                             reads=["x_tok_all", "dt_t"], writes=[("xdt", xi_)])
                    for quad in range(4):
                        rb = 1 + quad % 2
                        ei = quad % 2
                        S.op("pe", lambda e, quad=quad: e.matmul(psb[rb][:, :], lhsT=qTc[qi][:, :], rhs=selbc[:, quad, :, :].rearrange("k i t -> k (i t)"), start=True, stop=False),
                             reads=[("qTc", qi), "selbc"], writes=[("ps", rb)])
                        S.op("pe", lambda e: e.matmul(psb[rb][:, :], lhsT=ident_b[:, :], rhs=nm4[:, :, :].rearrange("p i t -> p (i t)"), start=False, stop=False),
                             reads=["ident_b", "nm4"], writes=[("ps", rb)])
                        for i_ in range(4):
                            col = (i_ % 2) * 8 + 2 * quad + i_ // 2
                            S.op("pe", lambda e, i_=i_, col=col: e.matmul(psb[rb][:, i_ * 128:(i_ + 1) * 128], lhsT=selm[:, col, :], rhs=qTc[qi][:, :], start=False, stop=(i_ == 3)),
                                 reads=["selm", ("qTc", qi)], writes=[("ps", rb)])
                        S.op("act", lambda e: e.activation(out=Et[ei][:, :, :].rearrange("p i t -> p (i t)"), in_=psb[rb][:, :], func=AF.Exp),
                             reads=[("ps", rb)], writes=[("Et", ei)])
                        S.op("dve", lambda e: e.tensor_tensor(out=MT[ei][:, :, :], in0=Et[ei][:, :, :], in1=scT[:, :].unsqueeze(1).broadcast_to([128, 4, 128]), op=ALU.mult),
                             reads=[("Et", ei), kx("scT")], writes=[("MT", ei)])
                        for hh_ in range(2):
                            hl = 2 * quad + hh_
                            h = g2 * 8 + hl
                            osl = slice(hl * 64, (hl + 1) * 64)
                            S.op("pe", lambda e, hh_=hh_, osl=osl: e.matmul(psb[3][:, osl], lhsT=MT[ei][:, hh_ * 2, :], rhs=xdt[qi * 2][:, osl], start=True, stop=False),
                                 reads=[("MT", ei), ("xdt", qi * 2)], writes=[("ps", 3)])
                            S.op("pe", lambda e, hh_=hh_, osl=osl: e.matmul(psb[3][:, osl], lhsT=MT[ei][:, hh_ * 2 + 1, :], rhs=xdt[qi * 2 + 1][:, osl], start=False, stop=False),
                                 reads=[("MT", ei), ("xdt", qi * 2 + 1)], writes=[("ps", 3)])
                            S.op("pe", lambda e, h=h, osl=osl: e.matmul(psb[3][:, osl], lhsT=ddiag[:, h, :], rhs=x_tok[:, c, osl], start=False, stop=True),
                                 reads=[("ddiag", h), "x_tok_all"], writes=[("ps", 3)])
                    S.op("pe", lambda e: e.matmul(psb[4][:, :], lhsT=Cf[:, tsl], rhs=stf_bf[:, :], start=True, stop=True),
                         reads=["BC_all", kx("stf_bf")], writes=[("ps", 4)])
                    S.op("pe", lambda e: e.matmul(psb[5][:, :], lhsT=Cf[:, tsl], rhs=stb_all[:, c, :], start=True, stop=True),
                         reads=["BC_all", ("stb", c)], writes=[("ps", 5)])
                def fwdC(c):
                    tsl = slice(c * 128, (c + 1) * 128)
                    zi = c % 2
                    scT, zs, ysb, yt1, yn, ssq, sqj, stf_bf = (scT_l[zi], zs_l[zi], ysb_l[zi], yt1_l[zi], yn_l[zi], ssq_l[zi], sqj_l[zi], stf_bf_l[zi])
                    kx = lambda k_: (k_, zi)
                    qi = c % 2
                    S.op("act", lambda e: e.activation(out=ysb[:, :], in_=psb[3][:, :], func=AF.Copy), reads=[("ps", 3)], writes=[kx("ysb")])
                    for d_, pb in ((0, 4), (1, 5)):
                        osrc = e1 if d_ == 0 else e2
                        okey = "e1" if d_ == 0 else "e2"
                        S.op("dve", lambda e, pb=pb, osrc=osrc, d_=d_: e.tensor_tensor(
                            out=yt1[:, :].rearrange("p (h d) -> p h d", d=64), in0=psb[pb][:, :].rearrange("p (h d) -> p h d", d=64),
                            in1=osrc[:, c, d_ * 8:d_ * 8 + 8].unsqueeze(2).broadcast_to([128, 8, 64]), op=ALU.mult),
                            reads=[("ps", pb), okey], writes=[kx("yt1")])
                        S.op("pool", lambda e: e.tensor_tensor(out=ysb[:, :], in0=ysb[:, :], in1=yt1[:, :], op=ALU.add),
                             reads=[kx("ysb"), kx("yt1")], writes=[kx("ysb")])
                    make_xw(c, 0, c % 2)
                    state_step(c, 0, c % 2, st_f, "st_f", 6 + c % 2)
                def fwdB(c):
                    tsl = slice(c * 128, (c + 1) * 128)
                    zi = c % 2
                    scT, zs, ysb, yt1, yn, ssq, sqj, stf_bf = (scT_l[zi], zs_l[zi], ysb_l[zi], yt1_l[zi], yn_l[zi], ssq_l[zi], sqj_l[zi], stf_bf_l[zi])
                    kx = lambda k_: (k_, zi)
                    qi = c % 2
                    S.op("act", lambda e: e.activation(out=zs[:, :], in_=zt_[zi][:, :], func=AF.Silu), reads=[("z", zi)], writes=[kx("zs")])
                    S.op("dve", lambda e: e.tensor_tensor(out=ysb[:, :], in0=ysb[:, :], in1=zs[:, :], op=ALU.mult), reads=[kx("ysb"), kx("zs")], writes=[kx("ysb")])
                    S.op("act", lambda e: e.activation(out=sqj[:, :], in_=ysb[:, :], func=AF.Square, accum_out=ssq[:, 0:1]),
                         reads=[kx("ysb")], writes=[kx("ssq"), kx("sqj")])
                    S.op("act", lambda e: e.activation(out=ssq[:, :], in_=ssq[:, :], func=AF.Ln, scale=1.0 / 512, bias=eps_t[:, 0:1]),
                         reads=[kx("ssq"), "eps_t"], writes=[kx("ssq")])
                    S.op("act", lambda e: e.activation(out=ssq[:, :], in_=ssq[:, :], func=AF.Exp, scale=-0.5), reads=[kx("ssq")], writes=[kx("ssq")])
                    S.op("dve", lambda e: e.tensor_scalar(out=yn[:, :], in0=ysb[:, :], scalar1=ssq[:, 0:1], scalar2=None, op0=ALU.mult),
                         reads=[kx("ysb"), kx("ssq")], writes=[kx("yn")])
                    pv = psb[6 + c % 2][:, :].bitcast(BF16).rearrange("p (a b) -> p a b", b=128)
                    mi_ = (c // SC) % 2
                    for xg in range(4):
                        S.op("pe", lambda e, xg=xg: e.transpose(out=pv[:, xg, :], in_=yn[:, xg * 128:(xg + 1) * 128], identity=ident_b[:, :]),
                             reads=[kx("yn"), "ident_b"], writes=[("ps", 6 + c % 2)])
                    for xg in range(4):
                        S.op("dve", lambda e, xg=xg: e.tensor_scalar(
                            out=mst[mi_][:, xg, (c % SC) * 128:(c % SC + 1) * 128], in0=pv[:, xg, :], scalar1=nw[:, g2 * 4 + xg:g2 * 4 + xg + 1],
                            scalar2=None, op0=ALU.mult),
                            reads=[("ps", 6 + c % 2), "nw"], writes=[("mst", mi_, c % SC, xg)])
                    if c % SC == SC - 1:
                        c0 = c - SC + 1
                        for xg in range(4):
                            S.dma(mix[(g2 * 4 + xg) * 128:(g2 * 4 + xg + 1) * 128, c0 * 128:(c + 1) * 128], mst[mi_][:, xg, :],
                                  reads=[("mst", mi_, cc, xg) for cc in range(SC)], writes=[("mixd", g2, xg, c0)])
                fwdA(0)
                for c in range(J):
                    fwdC(c)
                    if c + 1 < J:
                        fwdA(c + 1)
                    fwdB(c)
            S.barrier()

    if "hy" in stages:
        ft_s = dscr("ft_s", (J, 128, 4 * 128), BF16)
        et_s = dscr("et_s", (J, 128, 2 * 128), BF16)
        h3_s = dscr("h3_s", (128, 2 * NFFT), BF16)
        with contextlib.ExitStack() as ph:
            fw1 = T(ph, "fw1", [33, 64], F32)
            fw2 = T(ph, "fw2", [64, 64], F32)
            fw3 = T(ph, "fw3", [64, 128], F32)
            fr = T(ph, "fr", [128, 1], F32)
            fbs = T(ph, "fbs", [128, 3], F32)
            S.dma(fw1[:, :], I["fw1"][:, :], writes=["fw1"])
            S.dma(fw2[:, :], I["fw2"][:, :], writes=["fw2"])
            S.dma(fw3[:, :], I["fw3"][:, :], writes=["fw3"])
            S.dma(fr[:, :], I["ffreq"][:, :], writes=["fr"])
            S.op("pool", lambda e: e.memset(fbs[:, :], 0.0), writes=["fbs"])
            S.dma(fbs[0:64, 0:1], I["fb1"][:, :], reads=["fbs"], writes=["fbs1"])
            S.dma(fbs[0:64, 1:2], I["fb2"][:, :], reads=["fbs"], writes=["fbs2"])
            S.dma(fbs[:, 2:3], I["fb3"][:, :], reads=["fbs"], writes=["fbs3"])
            S.op("dve", lambda e: e.tensor_tensor(out=fbs[:, :], in0=fbs[:, :], in1=fr[:, 0:1].broadcast_to([128, 3]), op=ALU.mult),
                 reads=["fbs", "fbs1", "fbs2", "fbs3", "fr"], writes=["fbsx"])
            NB3 = 3
            hA_l = [T(ph, "hA%d" % i, [128, 512], F32) for i in range(NB3)]
            hB_l = [T(ph, "hB%d" % i, [128, 512], F32) for i in range(NB3)]
            mw_l = [T(ph, "mwrap%d" % i, [128, 512], F32) for i in range(NB3)]
            PI = math.pi

            def mlp_layer(ps_ap, np_, col, dst, mwrap, mwk):
                S.op("dve", lambda e: e.tensor_scalar(out=dst[0:np_, :], in0=ps_ap, scalar1=fr[0:np_, 0:1], scalar2=fbs[0:np_, col:col + 1],
                                                      op0=ALU.mult, op1=ALU.add), reads=[pkey[0], "fr", "fbsx"], writes=[id(dst)])
                for (cmp, thr, add) in ((ALU.is_gt, PI, -2 * PI), (ALU.is_lt, -PI, 2 * PI)):
                    S.op("dve", lambda e, cmp=cmp, thr=thr: e.tensor_scalar(out=mwrap[0:np_, :], in0=dst[0:np_, :], scalar1=thr, scalar2=None, op0=cmp),
                         reads=[id(dst)], writes=[mwk])
                    S.op("dve", lambda e, add=add: e.scalar_tensor_tensor(out=dst[0:np_, :], in0=mwrap[0:np_, :], scalar=add, in1=dst[0:np_, :],
                                                                         op0=ALU.mult, op1=ALU.add),
                         reads=[id(dst), mwk], writes=[id(dst)])
                S.op("act", lambda e: e.activation(out=dst[0:np_, :], in_=dst[0:np_, :], func=AF.Sin), reads=[id(dst)], writes=[id(dst)])

            pkey = [None]
            ztl = [T(ph, "ztl3_%d" % i, [33, 512], F32) for i in range(NB3)]
            mft = [T(ph, "mft3_%d" % i, [128, 512], F32) for i in range(NB3)]
            h3o = [T(ph, "h3o3_%d" % i, [128, 512], BF16) for i in range(NB3)]
            NTI = 2 * NFFT // 512
            for t0_ in range(0, NTI, NB3):
                tis = list(range(t0_, min(NTI, t0_ + NB3)))
                for ti in tis:
                    i = ti % NB3
                    S.dma(ztl[i][:, :], I["zt"][:, ti * 512:(ti + 1) * 512], writes=[("ztl", i)])
                    S.dma(mft[i][:, :], I["mfb"][:, ti * 512:(ti + 1) * 512], writes=[("mft", i)])
                for ti in tis:
                    i = ti % NB3
                    b_ = bank(0, 8)
                    pkey[0] = ("ps", b_)
                    S.op("pe", lambda e, i=i, b_=b_: e.matmul(psb[b_][0:64, :], lhsT=fw1[:, :], rhs=ztl[i][:, :], start=True, stop=True),
                         reads=["fw1", ("ztl", i)], writes=[("ps", b_)])
                    mlp_layer(psb[b_][0:64, :], 64, 0, hA_l[i], mw_l[i], ("mwrap", i))
                for ti in tis:
                    i = ti % NB3
                    b_ = bank(0, 8)
                    pkey[0] = ("ps", b_)
                    S.op("pe", lambda e, i=i, b_=b_: e.matmul(psb[b_][0:64, :], lhsT=fw2[:, :], rhs=hA_l[i][0:64, :], start=True, stop=True),
                         reads=["fw2", id(hA_l[i])], writes=[("ps", b_)])
                    mlp_layer(psb[b_][0:64, :], 64, 1, hB_l[i], mw_l[i], ("mwrap", i))
                for ti in tis:
                    i = ti % NB3
                    b_ = bank(0, 8)
                    pkey[0] = ("ps", b_)
                    S.op("pe", lambda e, i=i, b_=b_: e.matmul(psb[b_][:, :], lhsT=fw3[:, :], rhs=hB_l[i][0:64, :], start=True, stop=True),
                         reads=["fw3", id(hB_l[i])], writes=[("ps", b_)])
                    mlp_layer(psb[b_][:, :], 128, 2, hA_l[i], mw_l[i], ("mwrap", i))
                for ti in tis:
                    i = ti % NB3
                    S.op("pool", lambda e, i=i: e.tensor_tensor(out=h3o[i][:, :], in0=hA_l[i][:, :], in1=mft[i][:, :], op=ALU.mult),
                         reads=[id(hA_l[i]), ("mft", i)], writes=[("h3o", i)])
                    S.dma(h3_s[:, ti * 512:(ti + 1) * 512], h3o[i][:, :], reads=[("h3o", i)], writes=[("h3_s", ti)])
            S.barrier()

        with contextlib.ExitStack() as ph:
            NQ = NSUB * 256
            bufA = T(ph, "bufA", [128, NQ], BF16)
            bufB = T(ph, "bufB", [128, NQ], BF16)
            bufC = T(ph, "bufC", [128, NQ], BF16)
            bufD = T(ph, "bufD", [128, NQ], BF16)
            Yb = T(ph, "Yb", [128, NQ], BF16)
            Pt = T(ph, "Pt", [128, J * 128], F32)
            gt = T(ph, "gt", [128, J, 128], BF16)
            x0t = T(ph, "x0t", [128, J, 128], BF16)
            fm = [T(ph, "fm%d" % i, [128, NT], BF16) for i in range(2)]
            cin = T(ph, "hcin", [128, NTH], BF16)
            ftr = T(ph, "ftr", [128, J, 4, 128], BF16)
            etr = T(ph, "etr", [128, J, 2, 128], BF16)
            ftv = I["ftab"].rearrange("p (j m) -> p j m", m=512)
            etv = I["etab"].rearrange("p (j m) -> p j m", m=256)
            JT = max(1, (J * 128) // 512)
            kf, ke = [], []
            for j0_ in range(0, J, JT):
                S.dma(Pt[:, 0:JT * 512].rearrange("p (j m) -> p j m", m=512), ftv[:, j0_:j0_ + JT, :], writes=["P1", "P2"])
                S.op("act" if (j0_ // JT) % 2 else "dve",
                     (lambda e, j0_=j0_: e.activation(out=ftr[:, j0_:j0_ + JT, :, :].rearrange("p j m k -> p j (m k)"), in_=Pt[:, 0:JT * 512].rearrange("p (j m) -> p j m", m=512), func=AF.Copy))
                     if (j0_ // JT) % 2 else
                     (lambda e, j0_=j0_: e.tensor_copy(out=ftr[:, j0_:j0_ + JT, :, :].rearrange("p j m k -> p j (m k)"), in_=Pt[:, 0:JT * 512].rearrange("p (j m) -> p j m", m=512))),
                     reads=["P1", "P2"], writes=[("ftr", j0_)])
                kf.append(("ftr", j0_))
            JE = min(J, 2 * JT)
            for j0_ in range(0, J, JE):
                S.dma(Pt[:, 0:JE * 256].rearrange("p (j m) -> p j m", m=256), etv[:, j0_:j0_ + JE, :], writes=["P1", "P2"])
                S.op("dve", lambda e, j0_=j0_: e.tensor_copy(out=etr[:, j0_:j0_ + JE, :, :].rearrange("p j m k -> p j (m k)"), in_=Pt[:, 0:JE * 256].rearrange("p (j m) -> p j m", m=256)),
                     reads=["P1", "P2"], writes=[("etr", j0_)])
                ke.append(("etr", j0_))
            S.merge("ftr", kf)
            S.merge("etr", ke)
            h3l = [T(ph, "h3l%d" % i, [128, 512], BF16) for i in range(3)]
            wnd = [T(ph, "wnd%d" % i, [128, 4, 128], F32) for i in range(2)]
            w2f = T(ph, "w2f", [128, 3, 128], F32)
            w2b = T(ph, "w2b", [128, 3, 128], BF16)
            hcw = T(ph, "hcw", [128, 24, 3], F32)
            hcwc = T(ph, "hcwc", [128, 16, 3], F32)
            hcb = T(ph, "hcb", [128, 24], F32)
            hyb = T(ph, "hyb", [128, 1024], F32)
            hnw = T(ph, "hnw", [128, 8], F32)
            absd = T(ph, "absd", [128, 1024], F32)
            negt = T(ph, "negt", [128, 4 * J], F32)
            w4b = T(ph, "w4b", [128, 1024], BF16)
            dg3 = T(ph, "dg3", [128, 3, 128], BF16)
            ssq2 = T(ph, "ssq2", [128, 2 * J], F32)
            S.dma(w2f[:, :, :], I["w2tab"].rearrange("p (i m) -> p i m", i=3), writes=["w2f"])
            S.op("dve", lambda e: e.tensor_copy(out=w2b[:, :, :], in_=w2f[:, :, :]), reads=["w2f"], writes=["w2b"])
            S.dma(hcw[:, :, :], I["hy_cw"].rearrange("p (g j) -> p g j", j=3), writes=["hcw"])
            S.dma(hcwc[:, :, :], I["hy_cw_ctx"].rearrange("p (g j) -> p g j", j=3), writes=["hcwc"])
            S.dma(hcb[:, :], I["hy_cb"][:, :], writes=["hcb"])
            S.dma(hyb[:, :], I["hy_bias"][0, :].partition_broadcast(128), writes=["hyb"])
            S.dma(hnw[:, :], I["hy_nw"][:, :], writes=["hnw"])
            S.dma(absd[:, :], I["absd"][:, :], writes=["absd"])
            S.dma(negt[:, :], I["negt"][:, :], writes=["negt"])
            WCH = min(1024, J * 128)
            for c0_ in range(0, 1024, WCH):
                S.dma(Pt[:, 0:WCH], I["fw4s"][:, c0_:c0_ + WCH], writes=["P1"])
                S.op("dve", lambda e, c0_=c0_: e.tensor_copy(out=w4b[:, c0_:c0_ + WCH], in_=Pt[:, 0:WCH]), reads=["P1"], writes=[("w4b", c0_)])
            S.merge("w4b", [("w4b", c0_) for c0_ in range(0, 1024, WCH)])
            A4 = lambda t_: t_[:, :].rearrange("p (s r k) -> p s r k", r=2, k=128)
            B1v = lambda t_: t_[:, :].rearrange("p (r s j c) -> p r s j c", r=2, s=NSUB, c=CS)
            ZTv = lambda t_: t_[:, :].rearrange("p (r j s c) -> p r j s c", r=2, j=J, c=CS)
            cnt = {"ft": 0, "et": 0, "h3": 0, "w": 0}

            mixo = fm[0]
            gct = Pt[:, 0:J * 64].bitcast(BF16).rearrange("p (j c) -> p j c", c=128)

            def evac(pv_in, out_ap, rkeys, wkey, wg, en=None):
                en = en or S.ev()
                if hy_stop == 3.11:
                    en = "dve"
                if hy_stop == 3.12:
                    en = "act"
                if en == "act":
                    S.op("act", lambda e: e.activation(out=out_ap, in_=pv_in, func=AF.Copy), reads=rkeys, writes=[wkey], wg=wg)
                else:
                    S.op("dve", lambda e: e.tensor_copy(out=out_ap, in_=pv_in), reads=rkeys, writes=[wkey], wg=wg)

            def conv3(src, row0, wt, wkey, g, dst_fn):
                S.dma(cin[:, :], src[row0:row0 + 128, :], writes=["hcin"])
                for j in range(3):
                    S.op("pool", lambda e, j=j: e.tensor_scalar(out=dg3[:, j, :], in0=ident_f[:, :], scalar1=wt[:, g, j:j + 1], scalar2=None, op0=ALU.mult),
                         reads=["ident_f", wkey], writes=[("dg3", j)])
                wg = S.newgroup()
                for o in range(0, NT, TW):
                    b = bank(0, 8)
                    for j in range(3):
                        S.op("pe", lambda e, j=j: e.matmul(psb[b][:, 0:TW], lhsT=dg3[:, j, :], rhs=cin[:, o + 1 + j:o + 1 + j + TW],
                                                           start=(j == 0), stop=(j == 2)),
                             reads=[("dg3", j), "hcin"], writes=[("ps", b)])
                    dst_fn(psb[b][:, 0:TW], o, ("ps", b), wg)

            def to_perm(srct, skey, dst, dkey):
                wg = S.newgroup()
                for j0 in range(0, J, 8):
                    nb = min(8, J - j0)
                    b = bank(0, 8)
                    pv = psb[b][:, :].bitcast(BF16).rearrange("p (a b) -> p a b", b=128)
                    for i in range(nb):
                        S.op("pe", lambda e, i=i: e.transpose(out=pv[:, i, :], in_=srct[:, j0 + i:NT:J], identity=ident_b[:, :]),
                             reads=[skey, "ident_b"], writes=[("ps", b)])
                    evac(pv[:, 0:nb, :], dst[:, j0:j0 + nb, :], [("ps", b)], dkey, wg, "act")

            def fft_fwd(mov_fn, PCn, B1t, B1key, BTt, BTkey, Xt, Xkey):
                b1 = B1v(B1t)
                wg = S.newgroup()
                for j0 in range(0, J, 2):
                    b = bank(0, 8)
                    pv = psb[b][:, :].rearrange("p (r j c) -> p r j c", r=2, c=128)
                    for jj in range(2):
                        j = j0 + jj
                        for ri in range(2 if hy_stop != 3.05 else 0):
                            for pc in range(PCn):
                                mv, mk = mov_fn(pc, j)
                                S.op("pe", lambda e, ri=ri, pc=pc, mv=mv, jj=jj, j=j: e.matmul(pv[:, ri, jj, :], lhsT=ftr[:, j, pc * 2 + ri, :], rhs=mv,
                                                                                         start=(pc == 0), stop=(pc == PCn - 1)),
                                     reads=["ftr"] + list(mk), writes=[("ps", b)])
                    pv5 = psb[b][:, :].rearrange("p (r j s c) -> p r j s c", r=2, j=2, c=CS)
                    en = "dve"
                    for ri in range(2 if hy_stop not in (3.05, 3.07) else 0):
                        evac(pv5[:, ri, :, :, :], b1[:, ri, :, j0:j0 + 2, :].rearrange("p s j c -> p j s c"), [("ps", b)], B1key, wg, en)
                if hy_stop in (3.05, 3.07, 3.1, 3.11, 3.12):
                    return
                stage2(B1t, B1key, BTt, BTkey, Xt, Xkey)

            def stage2(srct, skey, BTt, BTkey, Xt, Xkey):
                b1 = B1v(srct)
                bt = A4(BTt)
                wg = S.newgroup()
                for s0 in range(0, NSUB, 4):
                    ns = min(4, NSUB - s0)
                    b = bank(0, 8)
                    pv = psb[b][:, :].bitcast(BF16).rearrange("p (s r k) -> p s r k", r=2, k=128)
                    for sl in range(ns):
                        for ri in range(2):
                            S.op("pe", lambda e, sl=sl, ri=ri: e.transpose(out=pv[:, sl, ri, :], in_=srct[:, (ri * NSUB + s0 + sl) * 128:(ri * NSUB + s0 + sl + 1) * 128],
                                                                           identity=ident_b[:, :]),
                                 reads=[skey, "ident_b"], writes=[("ps", b)])
                    evac(pv[:, 0:ns, :, :], bt[:, s0:s0 + ns, :, :], [("ps", b)], BTkey, wg, "act")
                if hy_stop == 3.2:
                    return
                dft_j(BTt, BTkey, Xt, Xkey, True)

            def dft_j(srct, skey, Xt, Xkey, fwd):
                bt = A4(srct)
                xo = A4(Xt)
                i2, i3 = (2, 1) if fwd else (1, 2)
                wg = S.newgroup()
                for s0 in range(0, NSUB, 2):
                    b = bank(0, 8)
                    pv = psb[b][:, :].rearrange("p (s r k) -> p s r k", r=2, k=128)
                    for sl in range(2):
                        sub = s0 + sl
                        o0 = sl * 256
                        S.op("pe", lambda e, o0=o0, sub=sub: e.matmul(psb[b][:, o0:o0 + 256], lhsT=w2b[:, 0, :], rhs=srct[:, sub * 256:(sub + 1) * 256], start=True, stop=False),
                             reads=[skey, "w2b"], writes=[("ps", b)])
                        S.op("pe", lambda e, o0=o0, sub=sub: e.matmul(psb[b][:, o0:o0 + 128], lhsT=w2b[:, i2, :], rhs=srct[:, sub * 256 + 128:(sub + 1) * 256], start=False, stop=False),
                             reads=[skey, "w2b"], writes=[("ps", b)])
                        S.op("pe", lambda e, o0=o0, sub=sub: e.matmul(psb[b][:, o0 + 128:o0 + 256], lhsT=w2b[:, i3, :], rhs=srct[:, sub * 256:sub * 256 + 128], start=False, stop=True),
                             reads=[skey, "w2b"], writes=[("ps", b)])
                    evac(psb[b][:, :], Xt[:, s0 * 256:(s0 + 2) * 256], [("ps", b)], Xkey, wg)

            def make_kappa(sig, cg, dstt, dkey):
                kv = dstt[:, :].rearrange("p (t c) -> p t c", c=128)
                wg = S.newgroup()
                for t0 in range(0, 2 * J, 4):
                    hi = cnt["h3"] % 3
                    cnt["h3"] += 1
                    q0 = (sig * 2 * J + t0) * 128
                    S.dma(h3l[hi][:, :], h3_s[:, q0:q0 + 512], writes=[("h3l", hi)])
                    b = bank(0, 8)
                    for tt in range(4):
                        S.op("pe", lambda e, tt=tt: e.matmul(psb[b][:, tt * 128:(tt + 1) * 128], lhsT=h3l[hi][:, tt * 128:(tt + 1) * 128],
                                                             rhs=w4b[:, cg * 128:(cg + 1) * 128], start=True, stop=True),
                             reads=[("h3l", hi), "w4b"], writes=[("ps", b)])
                    wi = cnt["w"] % 2
                    cnt["w"] += 1
                    wgw = S.newgroup()
                    for tt in range(4):
                        tcol = sig * 2 * J + t0 + tt
                        S.op("act", lambda e, tcol=tcol, tt=tt: e.activation(out=wnd[wi][:, tt, :], in_=absd[:, cg * 128:(cg + 1) * 128], func=AF.Exp,
                                                                             scale=negt[:, tcol:tcol + 1]),
                             reads=["absd", "negt"], writes=[("wnd", wi)], wg=wgw)
                    S.op("dve", lambda e: e.tensor_tensor(out=kv[:, t0:t0 + 4, :], in0=psb[b][:, :].rearrange("p (t c) -> p t c", c=128), in1=wnd[wi][:, :, :], op=ALU.mult),
                         reads=[("ps", b), ("wnd", wi)], writes=[dkey], wg=wg)

            def cmul(Gt, Gkey, Kt, Kkey, first):
                g4, k4, y4 = A4(Gt), A4(Kt), A4(Yb)
                hs = NSUB // 2
                wg = S.newgroup()
                for hf in range(2):
                    ss = slice(hf * hs, (hf + 1) * hs)
                    p1 = Pt[:, 0:hs * 64].bitcast(BF16).rearrange("p (s k) -> p s k", k=128)
                    p2 = Pt[:, hs * 128:hs * 128 + hs * 64].bitcast(BF16).rearrange("p (s k) -> p s k", k=128)
                    rd = [Gkey, Kkey]
                    for (ro, (a1, a2), (b1_, b2_), sgn) in ((0, (0, 0), (1, 1), ALU.subtract), (1, (0, 1), (1, 0), ALU.add)):
                        S.op("dve", lambda e, a1=a1, a2=a2: e.tensor_tensor(out=p1, in0=g4[:, ss, a1, :], in1=k4[:, ss, a2, :], op=ALU.mult),
                             reads=rd, writes=["P1"])
                        S.op("pool", lambda e, b1_=b1_, b2_=b2_: e.tensor_tensor(out=p2, in0=g4[:, ss, b1_, :], in1=k4[:, ss, b2_, :], op=ALU.mult),
                             reads=rd, writes=["P2"])
                        S.op("dve", lambda e, sgn=sgn: e.tensor_tensor(out=p1, in0=p1, in1=p2, op=sgn), reads=["P1", "P2"], writes=["P1"])
                        if first:
                            S.op("pool", lambda e, ro=ro: e.tensor_copy(out=y4[:, ss, ro, :], in_=p1), reads=["P1"], writes=["Yb"], wg=wg)
                        else:
                            S.op("pool", lambda e, ro=ro: e.tensor_tensor(out=y4[:, ss, ro, :], in0=y4[:, ss, ro, :], in1=p1, op=ALU.add),
                                 reads=["P1", "Yb"], writes=["Yb"], wg=wg)

            kvA = bufA[:, :].rearrange("p (t c) -> p t c", c=128)
            for cg in range(8 if hy_stop >= 5 else (1 if hy_stop >= 2 else 0)):
                def ev_x0(ps_ap, o, bkey, wg):
                    S.op("act", lambda e: e.activation(out=fm[1][:, o:o + TW], in_=ps_ap, func=AF.Identity, bias=hcb[:, cg:cg + 1]),
                         reads=[bkey, "hcb"], writes=["fm1"], wg=wg)
                def ev_x1(ps_ap, o, bkey, wg):
                    S.op("act", lambda e: e.activation(out=fm[0][:, o:o + TW], in_=ps_ap, func=AF.Identity, bias=hcb[:, 8 + cg:9 + cg]),
                         reads=[bkey, "hcb"], writes=["fm0"], wg=wg)
                def ev_v(ps_ap, o, bkey, wg):
                    S.op("dve", lambda e: e.scalar_tensor_tensor(out=fm[1][:, o:o + TW], in0=ps_ap, scalar=hcb[:, 16 + cg:17 + cg], in1=fm[0][:, o:o + TW],
                                                                 op0=ALU.add, op1=ALU.mult),
                         reads=[bkey, "hcb", "fm0"], writes=["fm1"], wg=wg)
                conv3(pj_fm, 1536 + cg * 128, hcw, "hcw", cg, ev_x0)
                to_perm(fm[1], "fm1", x0t, "x0t")
                conv3(pj_fm, 2560 + cg * 128, hcw, "hcw", 8 + cg, ev_x1)
                conv3(pj_fm, 3584 + cg * 128, hcw, "hcw", 16 + cg, ev_v)
                to_perm(fm[1], "fm1", gt, "gt")
                if hy_stop == 2:
                    break
                make_kappa(0, cg, bufA, "bufA")
                fft_fwd(lambda pc, j: (gt[:, j, :], ["gt"]), 1, bufD, "bufD", bufB, "bufB", bufC, "bufC")
                fft_fwd(lambda pc, j: (kvA[:, pc * J + j, :], ["bufA"]), 2, bufD, "bufD", bufB, "bufB", bufA, "bufA")
                cmul(bufC, "bufC", bufA, "bufA", True)
                make_kappa(1, cg, bufA, "bufA")
                conv3(ch_fm, cg * 128, hcwc, "hcwc", cg, ev_x1)
                conv3(ch_fm, 1024 + cg * 128, hcwc, "hcwc", 8 + cg, ev_v)
                to_perm(fm[1], "fm1", gct, "P1")
                fft_fwd(lambda pc, j: (gct[:, j, :], ["P1"]), 1, bufD, "bufD", bufB, "bufB", bufC, "bufC")
                fft_fwd(lambda pc, j: (kvA[:, pc * J + j, :], ["bufA"]), 2, bufD, "bufD", bufB, "bufB", bufA, "bufA")
                cmul(bufC, "bufC", bufA, "bufA", False)
                dft_j(Yb, "Yb", bufD, "bufD", False)
                zs4 = A4(bufD)
                zt4 = ZTv(bufB)
                wg = S.newgroup()
                for s0 in range(0, NSUB, 4):
                    ns = min(4, NSUB - s0)
                    b = bank(0, 8)
                    pv = psb[b][:, :].bitcast(BF16).rearrange("p (r s m) -> p r s m", r=2, m=128)
                    pv5 = psb[b][:, :].bitcast(BF16).rearrange("p (r s j c) -> p r s j c", r=2, s=4, c=CS)
                    for ri in range(2):
                        for sl in range(ns):
                            S.op("pe", lambda e, sl=sl, ri=ri: e.transpose(out=pv[:, ri, sl, :], in_=bufD[:, ((s0 + sl) * 2 + ri) * 128:((s0 + sl) * 2 + ri + 1) * 128], identity=ident_b[:, :]),
                                 reads=["bufD", "ident_b"], writes=[("ps", b)])
                    en = "dve"
                    for ri in range(2):
                        evac(pv5[:, ri, 0:ns, :, :], zt4[:, ri, :, s0:s0 + ns, :].rearrange("p j s c -> p s j c"), [("ps", b)], "bufB", wg, en)
                yh = Pt[:, :].rearrange("p (j c) -> p j c", c=128)
                wgy = S.newgroup()
                for j0 in range(0, J, 4):
                    b = bank(0, 8)
                    pv = psb[b][:, :].rearrange("p (j c) -> p j c", c=128)
                    for jj in range(4):
                        j = j0 + jj
                        for ri in range(2):
                            S.op("pe", lambda e, ri=ri, jj=jj, j=j: e.matmul(pv[:, jj, :], lhsT=etr[:, j, ri, :], rhs=bufB[:, (ri * J + j) * 128:(ri * J + j + 1) * 128],
                                                                             start=(ri == 0), stop=(ri == 1)),
                                 reads=["etr", "bufB"], writes=[("ps", b)])
                    S.op("pool", lambda e: e.tensor_tensor(out=yh[:, j0:j0 + 4, :], in0=gt[:, j0:j0 + 4, :],
                                                           in1=hyb[:, cg * 128:(cg + 1) * 128].unsqueeze(1).broadcast_to([128, 4, 128]), op=ALU.mult),
                         reads=["gt", "hyb"], writes=["P1", "P2", ("yh", j0)], wg=wgy)
                    S.op("dve", lambda e: e.tensor_tensor(out=yh[:, j0:j0 + 4, :], in0=pv, in1=yh[:, j0:j0 + 4, :], op=ALU.add),
                         reads=[("ps", b), ("yh", j0)], writes=[("yh", j0)])
                    S.op("pool", lambda e: e.tensor_tensor(out=yh[:, j0:j0 + 4, :], in0=yh[:, j0:j0 + 4, :], in1=x0t[:, j0:j0 + 4, :], op=ALU.mult),
                         reads=[("yh", j0), "x0t"], writes=[("yh", j0), "yhdone"], wg=wgy)
                sqv = bufD[:, 0:J * 128]
                S.op("act", lambda e: e.activation(out=sqv, in_=Pt[:, :], func=AF.Square), reads=["yhdone", "P1", "P2"], writes=["bufD"])
                S.op("dve", lambda e: e.tensor_reduce(out=ssq2[:, :], in_=sqv.rearrange("p (g d) -> p g d", d=64), axis=AX.X, op=ALU.add),
                     reads=["bufD"], writes=["ssq2"])
                S.op("act", lambda e: e.activation(out=ssq2[:, :], in_=ssq2[:, :], func=AF.Ln, scale=1.0 / 64, bias=eps_t[:, 0:1]), reads=["ssq2", "eps_t"], writes=["ssq2"])
                S.op("act", lambda e: e.activation(out=ssq2[:, :], in_=ssq2[:, :], func=AF.Exp, scale=-0.5), reads=["ssq2"], writes=["ssq2"])
                ynv = bufA[:, 0:J * 128]
                S.op("dve", lambda e: e.tensor_tensor(out=ynv.rearrange("p (g d) -> p g d", d=64), in0=Pt[:, :].rearrange("p (g d) -> p g d", d=64),
                                                      in1=ssq2[:, :].unsqueeze(2).broadcast_to([128, 2 * J, 64]), op=ALU.mult),
                     reads=["yhdone", "P1", "P2", "ssq2"], writes=["bufA"])
                yn3 = ynv.rearrange("p (j c) -> p j c", c=128)
                wg = S.newgroup()
                for j0 in range(0, J, 8):
                    nb = min(8, J - j0)
                    b = bank(0, 8)
                    pv = psb[b][:, :].bitcast(BF16).rearrange("p (a b) -> p a b", b=128)
                    for i in range(nb):
                        S.op("pe", lambda e, i=i: e.transpose(out=pv[:, i, :], in_=yn3[:, j0 + i, :], identity=ident_b[:, :]),
                             reads=["bufA", "ident_b"], writes=[("ps", b)])
                    for i in range(nb):
                        S.op("dve", lambda e, i=i: e.tensor_scalar(out=mixo[:, j0 + i:NT:J], in0=pv[:, i, :], scalar1=hnw[:, cg:cg + 1], scalar2=None, op0=ALU.mult),
                             reads=[("ps", b), "hnw"], writes=["fm0"], wg=wg)
                S.dma(mix[1024 + cg * 128:1024 + (cg + 1) * 128, :], mixo[:, :], reads=["fm0"], writes=[("mixh", cg)])
            S.barrier()

    if "wprep" in stages and not wprep_inline:
        with contextlib.ExitStack() as ph:
            for _ in wprep_gen(ph):
                pass
            S.barrier()

    if "p2" in stages:
        with contextlib.ExitStack() as ph:
            W2 = min(512, NT)
            xt = T(ph, "p2xt", [128, KC, W2], F32)
            mixt = T(ph, "p2mix", [128, 16, W2], BF16)
            acc = T(ph, "p2acc", [128, KC, W2], F32)
            h1 = T(ph, "p2h1", [128, KC, W2], F32)
            ub = T(ph, "p2u", [128, KC, W2], BF16)
            sq = T(ph, "p2sq", [128, KC, W2], BF16)
            hm = T(ph, "p2hm", [128, FC, W2], BF16)
            rstd = T(ph, "p2rstd", [128, W2], F32)
            av = [T(ph, "p2a%d" % i, [128, W2], F32) for i in range(2)]
            ptf = T(ph, "p2ptf", [128, 2, W2], F32)
            ptb = T(ph, "p2ptb", [128, 2, W2], BF16)
            wo = [T(ph, "p2wo%d" % i, [128, 16, 128], BF16) for i in range(3)]
            wg_ = [T(ph, "p2wg%d" % i, [128, KC, 128], BF16) for i in range(6)]
            wu_ = [T(ph, "p2wu%d" % i, [128, KC, 128], BF16) for i in range(6)]
            wd_ = [T(ph, "p2wd%d" % i, [128, FC, 128], BF16) for i in range(3)]
            wpg_ = [T(ph, "p2wpg%d" % i, [128, KC, 128], BF16) for i in range(4)]
            wpp_ = [T(ph, "p2wpp%d" % i, [128, 2, 128], BF16) for i in range(4)]
            nws = T(ph, "p2nws", [128, 5, KC], F32)
            for i, nm in enumerate(["nm_post", "nf_pre", "nf_post", "ple_npre", "ple_npost"]):
                S.dma(nws[:, i, :], I[nm][:, :], writes=[("nws", i)])
            S.merge("nws", [("nws", i) for i in range(5)])
            wn = {"wo": 0, "wg": 0, "wd": 0, "wp": 0, "a": 0}

            def normed_residual(src, skey, widx, res, rkey, dst, dkey):
                rms_rstd(src, [skey], W2, rstd, "p2rstd", sq, "p2sq", banks=(0, 8))
                wgd = S.newgroup()
                for kc in range(KC):
                    S.op("dve", lambda e, kc=kc: e.scalar_tensor_tensor(out=dst[:, kc, :], in0=src[:, kc, :], scalar=nws[:, widx, kc:kc + 1],
                                                                        in1=rstd[:, :], op0=ALU.mult, op1=ALU.mult),
                         reads=[skey, "nws", "p2rstd"], writes=[(dkey, kc)])
                    S.op("pool", lambda e, kc=kc: e.tensor_tensor(out=dst[:, kc, :], in0=dst[:, kc, :], in1=res[:, kc, :], op=ALU.add),
                         reads=[(dkey, kc), rkey], writes=[(dkey, kc), dkey], wg=wgd)

            def normed_bf16(src, skey, widx, dst, dkey):
                rms_rstd(src, [skey], W2, rstd, "p2rstd", sq, "p2sq", banks=(0, 8))
                wgd = S.newgroup()
                for kc in range(KC):
                    S.op("dve", lambda e, kc=kc: e.scalar_tensor_tensor(out=dst[:, kc, :], in0=src[:, kc, :], scalar=nws[:, widx, kc:kc + 1],
                                                                        in1=rstd[:, :], op0=ALU.mult, op1=ALU.mult),
                         reads=[skey, "nws", "p2rstd"], writes=[dkey], wg=wgd)

            for o in range(0, NT, W2):
                S.dma(xt[:, :, :], I["xo"][:, 2 + o:2 + o + W2].rearrange("(kc p) t -> p kc t", p=128), writes=["xt"])
                S.dma(mixt[:, :, :], mix[:, o:o + W2].rearrange("(kc p) t -> p kc t", p=128), writes=["mixt"])
                S.dma(ptf[:, :, :], I["pT"][:, o:o + W2].rearrange("(kc p) t -> p kc t", p=128), writes=["ptf"])
                S.op("pool", lambda e: e.tensor_copy(out=ptb[:, :, :], in_=ptf[:, :, :]), reads=["ptf"], writes=["ptb"])
                wga = S.newgroup()
                for m in range(KC):
                    wi = wn["wo"] % 3
                    wn["wo"] += 1
                    S.dma(wo[wi][:, :, :], wo_s[m, :, :].rearrange("p (kc c) -> p kc c", c=128), writes=[("wo", wi)])
                    b = bank(0, 8)
                    for kc in range(16):
                        S.op("pe", lambda e, kc=kc: e.matmul(psb[b][:, 0:W2], lhsT=wo[wi][:, kc, :], rhs=mixt[:, kc, :], start=(kc == 0), stop=(kc == 15)),
                             reads=[("wo", wi), "mixt"], writes=[("ps", b)])
                    en = S.ev()
                    S.op(en, (lambda e: e.activation(out=acc[:, m, :], in_=psb[b][:, 0:W2], func=AF.Copy)) if en == "act" else
                         (lambda e: e.tensor_copy(out=acc[:, m, :], in_=psb[b][:, 0:W2])), reads=[("ps", b)], writes=["acc"], wg=wga)
                normed_residual(acc, "acc", 0, xt, "xt", h1, "h1")
                normed_bf16(h1, "h1", 1, ub, "ub")
                wgh = S.newgroup()
                for j in range(FC):
                    wi = wn["wg"] % 6
                    wn["wg"] += 1
                    S.dma(wg_[wi][:, :, :], wg_s[j, :, :].rearrange("p (kc c) -> p kc c", c=128), writes=[("wg", wi)])
                    S.dma(wu_[wi][:, :, :], wu_s[j, :, :].rearrange("p (kc c) -> p kc c", c=128), writes=[("wu", wi)])
                    bg = bank(0, 8)
                    for kc in range(KC):
                        S.op("pe", lambda e, kc=kc: e.matmul(psb[bg][:, 0:W2], lhsT=wg_[wi][:, kc, :], rhs=ub[:, kc, :], start=(kc == 0), stop=(kc == KC - 1)),
                             reads=[("wg", wi), "ub"], writes=[("ps", bg)])
                    bu = bank(0, 8)
                    for kc in range(KC):
                        S.op("pe", lambda e, kc=kc: e.matmul(psb[bu][:, 0:W2], lhsT=wu_[wi][:, kc, :], rhs=ub[:, kc, :], start=(kc == 0), stop=(kc == KC - 1)),
                             reads=[("wu", wi), "ub"], writes=[("ps", bu)])
                    ai = wn["a"] % 2
                    wn["a"] += 1
                    S.op("act", lambda e: e.activation(out=av[ai][:, :], in_=psb[bg][:, 0:W2], func=AF.Silu), reads=[("ps", bg)], writes=[("av", ai)])
                    S.op("dve", lambda e, j=j: e.tensor_tensor(out=hm[:, j, :], in0=psb[bu][:, 0:W2], in1=av[ai][:, :], op=ALU.mult),
                         reads=[("ps", bu), ("av", ai)], writes=["hm"], wg=wgh)
                wga = S.newgroup()
                for m in range(KC):
                    wi = wn["wd"] % 3
                    wn["wd"] += 1
                    S.dma(wd_[wi][:, :, :], wd_s[m, :, :].rearrange("p (kc c) -> p kc c", c=128), writes=[("wd", wi)])
                    b = bank(0, 8)
                    for fc in range(FC):
                        S.op("pe", lambda e, fc=fc: e.matmul(psb[b][:, 0:W2], lhsT=wd_[wi][:, fc, :], rhs=hm[:, fc, :], start=(fc == 0), stop=(fc == FC - 1)),
                             reads=[("wd", wi), "hm"], writes=[("ps", b)])
                    en = S.ev()
                    S.op(en, (lambda e: e.activation(out=acc[:, m, :], in_=psb[b][:, 0:W2], func=AF.Copy)) if en == "act" else
                         (lambda e: e.tensor_copy(out=acc[:, m, :], in_=psb[b][:, 0:W2])), reads=[("ps", b)], writes=["acc"], wg=wga)
                normed_residual(acc, "acc", 2, h1, "h1", xt, "xt")
                normed_bf16(xt, "xt", 3, ub, "ub")
                wga = S.newgroup()
                for m in range(KC):
                    wi = wn["wp"] % 4
                    wn["wp"] += 1
                    S.dma(wpg_[wi][:, :, :], wpg_s[m, :, :].rearrange("p (kc c) -> p kc c", c=128), writes=[("wpg", wi)])
                    S.dma(wpp_[wi][:, :, :], wpp_s[m, :, :].rearrange("p (kc c) -> p kc c", c=128), writes=[("wpp", wi)])
                    bg = bank(0, 8)
                    for kc in range(KC):
                        S.op("pe", lambda e, kc=kc: e.matmul(psb[bg][:, 0:W2], lhsT=wpg_[wi][:, kc, :], rhs=ub[:, kc, :], start=(kc == 0), stop=(kc == KC - 1)),
                             reads=[("wpg", wi), "ub"], writes=[("ps", bg)])
                    be = bank(0, 8)
                    for kc in range(2):
                        S.op("pe", lambda e, kc=kc: e.matmul(psb[be][:, 0:W2], lhsT=wpp_[wi][:, kc, :], rhs=ptb[:, kc, :], start=(kc == 0), stop=(kc == 1)),
                             reads=[("wpp", wi), "ptb"], writes=[("ps", be)])
                    ai = wn["a"] % 2
                    wn["a"] += 1
                    S.op("act", lambda e: e.activation(out=av[ai][:, :], in_=psb[bg][:, 0:W2], func=AF.Sigmoid), reads=[("ps", bg)], writes=[("av", ai)])
                    S.op("dve", lambda e, m=m: e.tensor_tensor(out=acc[:, m, :], in0=psb[be][:, 0:W2], in1=av[ai][:, :], op=ALU.mult),
                         reads=[("ps", be), ("av", ai)], writes=["acc"], wg=wga)
                normed_residual(acc, "acc", 4, xt, "xt", h1, "h1")
                S.dma(yT[:, o:o + W2].rearrange("(kc p) t -> p kc t", p=128), h1[:, :, :], reads=["h1"], writes=[("yT", o)])
            S.barrier()

    C.I = I
    C.scr = dict(pj_fm=pj_fm, pj_z=pj_z, pj_dt=pj_dt, cs_fm=cs_fm, cs_dt=cs_dt, ch_fm=ch_fm, mix=mix,
                 wg_s=wg_s, wu_s=wu_s, wd_s=wd_s, wo_s=wo_s, wpg_s=wpg_s, wpp_s=wpp_s)

    S.barrier()
    st.close()
    return nc


def _pcol(v):
    return np.ascontiguousarray(np.asarray(v, np.float32).reshape(-1, 128).T)


def host_consts(NT):
    J = NT // 128
    N = 2 * NT
    c = {}
    c["ident"] = np.eye(128, dtype=np.float32)
    r = np.arange(128)
    c["tri"] = (r[:, None] <= r[None, :]).astype(np.float32)
    nm = np.zeros((128, 2, 128), np.float32)
    nm[:, 0, :] = np.where(r[:, None] <= r[None, :], 0.0, NEG)
    nm[:, 1, :] = np.where(r[:, None] >= r[None, :], 0.0, NEG)
    c["negmask"] = nm.reshape(128, 256)
    sm = np.zeros((32, 32, 128), np.float32)
    for k in range(32):
        sm[k, k, :] = 1.0
    c["selmat"] = sm.reshape(32, 32 * 128)
    p = np.arange(128, dtype=np.float64)
    k1 = np.arange(128, dtype=np.float64)
    ft = np.zeros((128, J, 2, 2, 128), np.float64)
    et = np.zeros((128, J, 2, 128), np.float64)
    for j in range(J):
        for pc in range(2):
            n = J * (p + 128 * pc) + j
            ang = 2 * np.pi * (k1[None, :] + 0.5) * n[:, None] / N
            ft[:, j, pc, 0, :] = np.cos(ang)
            ft[:, j, pc, 1, :] = -np.sin(ang)
        n = J * p + j
        ang = 2 * np.pi * (k1[:, None] + 0.5) * n[None, :] / N
        et[:, j, 0, :] = (2.0 / N) * np.cos(ang)
        et[:, j, 1, :] = -(2.0 / N) * np.sin(ang)
    c["ftab"] = ft.reshape(128, -1).astype(np.float32)
    c["etab"] = et.reshape(128, -1).astype(np.float32)
    CS = 128 // J
    w2 = np.zeros((128, 3, 128), np.float64)
    for a in range(J):
        for b in range(J):
            ang = 2 * np.pi * a * b / J
            for cc in range(CS):
                w2[a * CS + cc, 0, b * CS + cc] = np.cos(ang)
                w2[a * CS + cc, 1, b * CS + cc] = -np.sin(ang)
                w2[a * CS + cc, 2, b * CS + cc] = np.sin(ang)
    c["w2tab"] = w2.reshape(128, -1).astype(np.float32)
    return c


def _grp(w2d):
    taps, Cc = w2d.shape
    a = np.asarray(w2d, np.float32).T.reshape(Cc // 128, 128, taps).transpose(1, 0, 2)
    return np.ascontiguousarray(a.reshape(128, -1))


def _hyena_tables(NT, L, mode):
    J = NT // 128
    N = 2 * NT
    sgn = np.zeros((2, N), np.float64)
    dr = np.zeros((2, N), np.int64)
    pos = np.zeros((2, N), np.float64)
    i = np.arange(N)
    lo = i < NT
    hi = i > NT
    sgn[0, lo] = 1.0; dr[0, lo] = 0; pos[0, lo] = i[lo]
    sgn[0, hi] = -1.0; dr[0, hi] = 1; pos[0, hi] = N - i[hi]
    if mode == "first":
        sgn[1, lo] = 1.0; dr[1, lo] = 1; pos[1, lo] = NT - i[lo]
        sgn[1, hi] = -1.0; dr[1, hi] = 1; pos[1, hi] = NT + N - i[hi]
    elif mode == "second":
        sgn[1, lo] = 1.0; dr[1, lo] = 0; pos[1, lo] = NT + i[lo]
        sgn[1, hi] = -1.0; dr[1, hi] = 0; pos[1, hi] = i[hi] - NT
    t = pos / (L - 1)
    fb = np.linspace(1e-4, 15.0, 16)
    ang = fb[None, None, :] * (2.0 * np.pi * pos[..., None] / L)
    z = np.concatenate([t[..., None], np.cos(ang), -np.sin(ang)], axis=-1)
    def reord(a):
        sh = a.shape[2:]
        a = a.reshape((2, 2, 128, J) + sh)
        a = np.moveaxis(a, 3, 2)
        return a.reshape((2 * N,) + sh)
    zt = np.ascontiguousarray(reord(z).T.astype(np.float32))
    mf = reord(sgn * (dr == 0))
    mb = reord(sgn * (dr == 1))
    mfb = np.concatenate([np.tile(mf[None], (64, 1)), np.tile(mb[None], (64, 1))], 0).astype(np.float32)
    tq = reord(t).reshape(2 * 2 * J, 128)
    negt = np.ascontiguousarray((-tq.T).astype(np.float32))
    return zt, mfb, negt


def host_inputs(inp, NT):
    J = NT // 128
    NTH = NT + 4
    g = lambda k: np.asarray(inp[k], np.float32)
    W = g("w_in")[0]
    consts = host_consts(NT)
    xp, xs = g("x_prompt"), g("x_sample")
    pp, psm = g("p_prompt")[0], g("p_sample")[0]
    cw, cb = g("ssd_conv_w")[0], g("ssd_conv_b")[0]
    hw, hb = g("hy_conv_w")[0], g("hy_conv_b")[0]
    dtb, alog = g("ssd_dt_bias")[0], g("ssd_a_log")[0]
    max_decay = math.log(1e-2) / 0.3
    min_decay = math.log(1e-2) / 1.5
    deltas = np.abs(np.linspace(min_decay, max_decay, 1024)).astype(np.float32)
    w4 = g("hy_f_w4")[0]
    common = dict(consts)
    common.update(
        w_in_fm=np.ascontiguousarray(np.concatenate([W[:, 1024:2560], W[:, 2592:5664]], 1)),
        w_in_tm=np.ascontiguousarray(np.concatenate([W[:, 0:1024], W[:, 2560:2592]], 1)),
        w_cs_fm=np.ascontiguousarray(W[:, 1024:2304]),
        w_ch_fm=np.ascontiguousarray(W[:, 3616:5664]),
        nmp=_pcol(g("norm_mix_pre")[0]),
        ssd_cw=_grp(cw), ssd_cb=_pcol(cb),
        dtb=dtb.reshape(1, 32), alog=alog.reshape(1, 32),
        ssd_d=g("ssd_d")[0].reshape(1, 16), ssd_nw=_pcol(g("ssd_norm_w")[0]),
        hy_cw=_grp(hw), hy_cw_ctx=_grp(hw[:, 1024:]), hy_cb=_pcol(hb),
        fw1=g("hy_f_w1")[0], fb1=g("hy_f_b1")[0].reshape(64, 1), fw2=g("hy_f_w2")[0], fb2=g("hy_f_b2")[0].reshape(64, 1),
        fw3=np.ascontiguousarray(np.concatenate([g("hy_f_w3")[0]] * 2, 1)),
        fb3=np.concatenate([g("hy_f_b3")[0]] * 2).reshape(128, 1),
        ffreq=np.concatenate([g("hy_f_freq")[0]] * 2).reshape(128, 1),
        fw4s=np.ascontiguousarray(np.concatenate([w4[:, :1024], w4[:, 1024:]], 0)),
        absd=np.ascontiguousarray(np.tile(deltas[None], (128, 1))),
        hy_bias=g("hy_bias")[0].reshape(1, 1024), hy_nw=_pcol(g("hy_norm_w")[0]),
        w_out=g("w_out")[0], nm_post=_pcol(g("norm_mix_post")[0]), nf_pre=_pcol(g("norm_ffn_pre")[0]),
        w_gate=g("w_gate")[0], w_up=g("w_up")[0], w_down=g("w_down")[0], nf_post=_pcol(g("norm_ffn_post")[0]),
        ple_npre=_pcol(g("ple_norm_pre")[0]), w_pg=g("w_ple_gate")[0], w_pp=g("w_ple_proj")[0],
        ple_npost=_pcol(g("ple_norm_post")[0]),
    )
    tabs = {m: _hyena_tables(NT, (NT if m == "prompt" else 2 * NT), m) for m in ("prompt", "first", "second")}

    def halo_T(seq, a, b):
        L = seq.shape[0]
        out = np.zeros((b - a + 4, seq.shape[1]), np.float32)
        lo, hi = max(a - 2, 0), min(b + 2, L)
        out[lo - (a - 2):hi - (a - 2)] = seq[lo:hi]
        return np.ascontiguousarray(out.T)

    maps = []
    nprompt = xp.shape[0]
    for c in range(8):
        m = dict(common)
        if c < nprompt:
            mode = "prompt"
            m["xo"] = halo_T(xp[c], 0, NT)
            m["xcs"] = np.zeros((D, NTH), np.float32)
            m["xch"] = np.zeros((D, NTH), np.float32)
            m["pT"] = np.ascontiguousarray(pp[c].T)
            dirc, rev = 0, False
            sel = (0.0, 0.0)
        else:
            s, hh = (c - nprompt) // 2, (c - nprompt) % 2
            oh = 1 - hh
            mode = "first" if hh == 0 else "second"
            m["xo"] = halo_T(xs[s], hh * NT, (hh + 1) * NT)
            ctx = halo_T(xs[s], oh * NT, (oh + 1) * NT)
            m["xch"] = ctx
            rev = hh == 0
            m["xcs"] = np.ascontiguousarray(ctx[:, ::-1]) if rev else ctx
            m["pT"] = np.ascontiguousarray(psm[s, hh * NT:(hh + 1) * NT].T)
            dirc = 1 if hh == 0 else 0
            sel = (1.0, 0.0) if hh == 1 else (0.0, 1.0)
        m["w_cs_tm"] = np.ascontiguousarray(W[:, 2560 + dirc * 16:2560 + dirc * 16 + 16])
        cwc = cw[:, :1280]
        m["ssd_cw_ctx"] = _grp(cwc[::-1] if rev else cwc)
        m["ssd_cb_ctx"] = _pcol(cb[:1280])
        m["dtb_ctx"] = dtb[dirc].reshape(1, 16)
        m["alog_ctx"] = alog[dirc].reshape(1, 16)
        m["sel"] = np.tile(np.asarray(sel, np.float32)[None], (128, 1))
        m["zt"], m["mfb"], m["negt"] = tabs[mode]
        maps.append({k: np.ascontiguousarray(v, dtype=np.float32) for k, v in m.items()})
    return maps


_NC_CACHE = {}


def kernel(**inputs):
    NT = 4096
    if NT not in _NC_CACHE:
        _NC_CACHE[NT] = build(NT)
    nc = _NC_CACHE[NT]
    maps = host_inputs(inputs, NT)
    res = run_bass_kernel_spmd(nc, maps, core_ids=list(range(8)))
    outs = [np.asarray(r["yT"], np.float32) for r in res.results]
    y_prompt = np.stack([outs[c].T for c in range(4)], 0)
    y_sample = np.stack([np.concatenate([outs[4 + 2 * s].T, outs[5 + 2 * s].T], 0) for s in range(2)], 0)
    return (np.ascontiguousarray(y_prompt), np.ascontiguousarray(y_sample))
```

```python
import contextlib
import math
import numpy as np
import concourse.bass as bass
import concourse.mybir as mybir
from concourse.bass_utils import run_bass_kernel_spmd

F32 = mybir.dt.float32
BF16 = mybir.dt.bfloat16
AF = mybir.ActivationFunctionType
ALU = mybir.AluOpType
AX = mybir.AxisListType

D = 1024
KC = 8
PLE = 256
NH = 16
HP = 64
NS = 128
FFN = 2816
FC = 22
EPS = 1e-6
NEG = -30000.0


class Sched:
    NDMA = 40

    def __init__(self, nc, stack):
        self.nc = nc
        self.stack = stack
        self.gen = 0
        self.eng = {"pe": nc.tensor, "act": nc.scalar, "dve": nc.vector, "pool": nc.gpsimd, "sp": nc.sync}
        self.sem = {}
        self.cnt = {}
        for e in ("pe", "act", "dve", "pool"):
            self.sem[e] = stack.enter_context(nc.semaphore("sem_" + e))
            self.cnt[e] = 0
        self.dsem = [stack.enter_context(nc.semaphore("dsem%d" % i)) for i in range(self.NDMA)]
        self.dcnt = [0] * self.NDMA
        self.dnext = 0
        self.waited = {e: {} for e in self.eng}
        self.last_w = {}
        self.readers = {}
        self.nops = 0
        self.rr = 0
        self.rev = {}
        self.wgid = {}
        self.pre = {}
        self.gcount = 0

    def newgroup(self):
        self.gcount += 1
        return self.gcount

    def merge(self, new_key, keys):
        toks = []
        for k in keys:
            w = self.last_w.get(k)
            if w is not None:
                toks.extend(w if isinstance(w, list) else [w])
            self.rev.setdefault(k, set()).add(new_key)
        self.last_w[new_key] = toks
        self.readers[new_key] = []

    def _wait(self, e, tok):
        sem, val, name = tok
        if self.waited[e].get(name, 0) >= val:
            return
        self.waited[e][name] = val
        self.eng[e].wait_ge(sem, val)

    def _deps(self, e, reads, writes, wg=None):
        toks = []
        for k in reads:
            w = self.last_w.get(k)
            if w is not None:
                toks.extend(w if isinstance(w, list) else [w])
        for k in writes:
            if wg is not None and self.wgid.get(k) == wg:
                toks.extend(self.pre.get(k, ()))
                toks.extend(self.readers.get(k, ()))
                continue
            w = self.last_w.get(k)
            pre = []
            if w is not None:
                pre.extend(w if isinstance(w, list) else [w])
            pre.extend(self.readers.get(k, ()))
            for m in self.rev.get(k, ()):
                pre.extend(self.readers.get(m, ()))
            toks.extend(pre)
            if wg is not None:
                self.pre[k] = pre
        for t in toks:
            if t[2] == e and e == "pe":
                continue
            self._wait(e, t)

    def _commit(self, tok, reads, writes, wg=None):
        for k in reads:
            self.readers.setdefault(k, []).append(tok)
        for k in writes:
            if wg is not None and self.wgid.get(k) == wg:
                self.last_w[k].append(tok)
                continue
            self.last_w[k] = [tok] if wg is not None else tok
            self.readers[k] = []
            self.wgid[k] = wg

    def op(self, e, fn, reads=(), writes=(), wg=None):
        self._deps(e, reads, writes, wg)
        ins = fn(self.eng[e])
        self.cnt[e] += 1
        ins.then_inc(self.sem[e], 1)
        tok = (self.sem[e], self.cnt[e], e)
        self._commit(tok, reads, writes, wg)
        self.nops += 1
        return tok

    def dma(self, out, in_, reads=(), writes=(), q="sp", wg=None, **kw):
        i = self.dnext
        self.dnext = (self.dnext + 1) % self.NDMA
        name = "d%d" % i
        if self.dcnt[i] > 0:
            self._wait(q, (self.dsem[i], self.dcnt[i], name))
        self._deps(q, reads, writes, wg)
        ins = self.eng[q].dma_start(out=out, in_=in_, **kw)
        self.dcnt[i] += 16
        ins.then_inc(self.dsem[i], 16)
        tok = (self.dsem[i], self.dcnt[i], name)
        self._commit(tok, reads, writes, wg)
        self.nops += 1
        return tok

    def barrier(self):
        toks = [(self.sem[e], self.cnt[e], e) for e in ("pe", "act", "dve", "pool") if self.cnt[e] > 0]
        toks += [(self.dsem[i], self.dcnt[i], "d%d" % i) for i in range(self.NDMA) if self.dcnt[i] > 0]
        for e in self.eng:
            for t in toks:
                if t[2] == e:
                    continue
                self._wait(e, t)
        self.last_w = {}
        self.readers = {}
        self.rev = {}
        self.wgid = {}
        self.pre = {}
        self.gen += 1
        for e in ("pe", "act", "dve", "pool"):
            self.sem[e] = self.stack.enter_context(self.nc.semaphore("sem_%s_%d" % (e, self.gen)))
            self.cnt[e] = 0
            for e2 in self.eng:
                self.waited[e2][e] = 0

    def ev(self):
        self.rr += 1
        return "act" if self.rr % 2 else "dve"


class Ctx:
    pass


def build(NT, stages=("p0", "ssd", "hy", "wprep", "p2"), dbg=False, hy_stop=99):
    J = NT // 128
    NTH = NT + 4
    NFFT = 2 * NT
    CS = 128 // J
    NSUB = J
    nc = bass.Bass("TRN2", target_bir_lowering=False)
    st = contextlib.ExitStack()
    C = Ctx()
    C.nc = nc

    def din(name, shape, dt=F32):
        return nc.dram_tensor(name, list(shape), dt, kind="ExternalInput").ap()

    def dscr(name, shape, dt):
        kind = "ExternalOutput" if dbg else "Internal"
        return nc.dram_tensor(name, list(shape), dt, kind=kind).ap()

    I = {}
    for name, shape in [
        ("xo", (D, NTH)), ("xcs", (D, NTH)), ("xch", (D, NTH)), ("pT", (PLE, NT)),
        ("w_in_fm", (D, 4608)), ("w_in_tm", (D, 1056)), ("w_cs_fm", (D, 1280)), ("w_cs_tm", (D, 16)),
        ("w_ch_fm", (D, 2048)), ("nmp", (128, KC)),
        ("ssd_cw", (128, 12 * 5)), ("ssd_cw_ctx", (128, 10 * 5)), ("ssd_cb", (128, 12)), ("ssd_cb_ctx", (128, 10)),
        ("dtb", (1, 32)), ("alog", (1, 32)), ("dtb_ctx", (1, 16)), ("alog_ctx", (1, 16)),
        ("ssd_d", (1, 16)), ("ssd_nw", (128, 8)), ("sel", (128, 2)),
        ("hy_cw", (128, 24 * 3)), ("hy_cw_ctx", (128, 16 * 3)), ("hy_cb", (128, 24)),
        ("zt", (33, 2 * NFFT)), ("fw1", (33, 64)), ("fb1", (64, 1)), ("fw2", (64, 64)), ("fb2", (64, 1)),
        ("fw3", (64, 128)), ("fb3", (128, 1)), ("ffreq", (128, 1)), ("fw4s", (128, 1024)),
        ("mfb", (128, 2 * NFFT)), ("negt", (128, 4 * J)), ("absd", (128, 1024)),
        ("hy_bias", (1, 1024)), ("hy_nw", (128, 8)),
        ("ftab", (128, J * 4 * 128)), ("etab", (128, J * 2 * 128)), ("w2tab", (128, 3 * 128)),
        ("ident", (128, 128)), ("tri", (128, 128)), ("negmask", (128, 2 * 128)), ("selmat", (32, 32 * 128)),
        ("w_out", (2048, D)), ("nm_post", (128, KC)), ("nf_pre", (128, KC)), ("w_gate", (D, FFN)),
        ("w_up", (D, FFN)), ("w_down", (FFN, D)), ("nf_post", (128, KC)), ("ple_npre", (128, KC)),
        ("w_pg", (D, D)), ("w_pp", (PLE, D)), ("ple_npost", (128, KC)),
    ]:
        I[name] = din(name, shape)
    yT = nc.dram_tensor("yT", [D, NT], F32, kind="ExternalOutput").ap()

    pj_fm = dscr("pj_fm", (4608, NTH), BF16)
    pj_z = dscr("pj_z", (NT, 1024), BF16)
    pj_dt = dscr("pj_dt", (128, J * 32), F32)
    cs_fm = dscr("cs_fm", (1280, NTH), BF16)
    cs_dt = dscr("cs_dt", (128, J * 16), F32)
    ch_fm = dscr("ch_fm", (2048, NTH), BF16)
    mix = dscr("mix", (2048, NT), BF16)
    wg_s = dscr("wg_s", (FC, 128, KC * 128), BF16)
    wu_s = dscr("wu_s", (FC, 128, KC * 128), BF16)
    wd_s = dscr("wd_s", (KC, 128, FC * 128), BF16)
    wo_s = dscr("wo_s", (KC, 128, 16 * 128), BF16)
    wpg_s = dscr("wpg_s", (KC, 128, KC * 128), BF16)
    wpp_s = dscr("wpp_s", (KC, 128, 2 * 128), BF16)

    S = Sched(nc, st)
    C.S = S

    tcount = [0]

    def T(stack, name, shape, dt):
        tcount[0] += 1
        return stack.enter_context(nc.sbuf_tensor("sb%d_%s" % (tcount[0], name), list(shape), dt))

    ident_f = T(st, "ident_f", [128, 128], F32)
    ident_b = T(st, "ident_b", [128, 128], BF16)
    ones_b = T(st, "ones_b", [128, 128], BF16)
    psb = [st.enter_context(nc.psum_tensor("psb%d" % i, [128, 512], F32)) for i in range(8)]
    S.dma(ident_f[:, :], I["ident"][:, :], writes=["ident_f"])
    S.op("dve", lambda e: e.tensor_copy(out=ident_b[:, :], in_=ident_f[:, :]), reads=["ident_f"], writes=["ident_b"])
    S.op("pool", lambda e: e.memset(ones_b[:, :], 1.0), writes=["ones_b"])

    psn = [0]

    def bank(lo=0, hi=8):
        i = lo + psn[0] % (hi - lo)
        psn[0] += 1
        return i

    def rms_rstd(src, src_keys, w, rstd, rstd_key, sq, sq_key, banks=(6, 8)):
        S.op("act", lambda e: e.activation(out=sq[:, :, 0:w], in_=src[:, :, 0:w], func=AF.Square),
             reads=list(src_keys), writes=[sq_key])
        b = bank(*banks)
        for kc in range(KC):
            S.op("pe", lambda e, kc=kc: e.matmul(psb[b][:, 0:w], lhsT=ones_b[:, :], rhs=sq[:, kc, 0:w],
                                                 start=(kc == 0), stop=(kc == KC - 1)),
                 reads=[sq_key, "ones_b"], writes=[("ps", b)])
        S.op("act", lambda e: e.activation(out=rstd[:, 0:w], in_=psb[b][:, 0:w], func=AF.Ln, scale=1.0 / D, bias=eps_t[:, 0:1]),
             reads=[("ps", b), "eps_t"], writes=[rstd_key])
        S.op("act", lambda e: e.activation(out=rstd[:, 0:w], in_=rstd[:, 0:w], func=AF.Exp, scale=-0.5),
             reads=[rstd_key], writes=[rstd_key])

    eps_t = T(st, "eps_t", [128, 1], F32)
    S.op("pool", lambda e: e.memset(eps_t[:, :], EPS), writes=["eps_t"])
    C.rms_rstd = rms_rstd

    def wprep_gen(stack):
        wst = [T(stack, "wst%d" % i, [128, KC, 128], F32) for i in range(2)]
        wsb = [T(stack, "wsb%d" % i, [128, KC, 128], BF16) for i in range(2)]
        wc = 0
        for (src, nk, ncc, dst) in ((I["w_out"], 16, KC, wo_s), (I["w_gate"], KC, FC, wg_s), (I["w_up"], KC, FC, wu_s),
                                    (I["w_down"], FC, KC, wd_s), (I["w_pg"], KC, KC, wpg_s), (I["w_pp"], 2, KC, wpp_s)):
            for m in range(ncc):
                for k0 in range(0, nk, KC):
                    kn = min(KC, nk - k0)
                    i = wc % 2
                    wc += 1
                    S.dma(wst[i][:, 0:kn, :], src[k0 * 128:(k0 + kn) * 128, m * 128:(m + 1) * 128].rearrange("(kc p) c -> p kc c", p=128), writes=[("wst", i)])
                    en = ("act", "dve")[wc % 2]
                    if en == "act":
                        S.op("act", lambda e: e.activation(out=wsb[i][:, 0:kn, :], in_=wst[i][:, 0:kn, :], func=AF.Copy), reads=[("wst", i)], writes=[("wsb", i)])
                    else:
                        S.op("dve", lambda e: e.tensor_copy(out=wsb[i][:, 0:kn, :], in_=wst[i][:, 0:kn, :]), reads=[("wst", i)], writes=[("wsb", i)])
                    S.dma(dst[m, :, k0 * 128:(k0 + kn) * 128].rearrange("p (kc c) -> p kc c", c=128), wsb[i][:, 0:kn, :],
                          reads=[("wsb", i)], writes=[("wdst", id(dst), m, k0)])
                    yield

    wprep_inline = ("p0" in stages) and ("wprep" in stages)
    if "p0" in stages:
        with contextlib.ExitStack() as ph:
            u = T(ph, "u", [128, KC, NTH], BF16)
            xt = [T(ph, "xt%d" % i, [128, KC, 512], F32) for i in range(2)]
            sq = T(ph, "sq0", [128, KC, 512], BF16)
            rstd = T(ph, "rstd0", [128, 512], F32)
            nmp = T(ph, "nmp", [128, KC], F32)
            NWB = 4
            wf = [T(ph, "wf%d" % i, [128, KC, 128], F32) for i in range(NWB)] + [T(ph, "wfL", [128, KC, 512], F32)]
            wb = [T(ph, "wb%d" % i, [128, KC, 128], BF16) for i in range(NWB)] + [T(ph, "wbL", [128, KC, 512], BF16)]
            stg = [T(ph, "stg%d" % i, [128, NTH], BF16) for i in range(2)]
            stz = [T(ph, "stz%d" % i, [128, 512], BF16) for i in range(2)]
            std = [T(ph, "std%d" % i, [128, 32], F32) for i in range(2)]
            S.dma(nmp[:, :], I["nmp"][:, :], writes=["nmp"])
            wgen = wprep_gen(ph) if wprep_inline else iter(())
            tiles = [(o, 512) for o in range(0, NT, 512)] + [(NT, 4)]
            wn = [0]
            sn = [0]

            def load_w(src, c0, ncol):
                if ncol > 128:
                    i = NWB
                else:
                    i = wn[0] % NWB
                wn[0] += 1
                S.dma(wf[i][:, :, 0:ncol], src[:, c0:c0 + ncol].rearrange("(kc p) c -> p kc c", p=128),
                      writes=[("wf", i)])
                eng = "pool"
                S.op(eng, lambda e: e.tensor_copy(out=wb[i][:, :, 0:ncol], in_=wf[i][:, :, 0:ncol]),
                     reads=[("wf", i)], writes=[("wb", i)])
                return i

            def token_set(xsrc, fm_w, fm_dst, tm_w, tm_cols, tm_dsts):
                for ti, (o, w) in enumerate(tiles):
                    xi = ti % 2
                    S.dma(xt[xi][:, :, 0:w], xsrc[:, o:o + w].rearrange("(kc p) t -> p kc t", p=128),
                          writes=[("xt", xi)])
                    rms_rstd(xt[xi], [("xt", xi)], w, rstd, "n0rstd", sq, "n0sq")
                    for kc in range(KC):
                        S.op("dve", lambda e, kc=kc: e.scalar_tensor_tensor(
                            out=u[:, kc, o:o + w], in0=xt[xi][:, kc, 0:w], scalar=nmp[:, kc:kc + 1],
                            in1=rstd[:, 0:w], op0=ALU.mult, op1=ALU.mult),
                            reads=[("xt", xi), "nmp", "n0rstd"], writes=[("u", ti, kc)])
                ncols = fm_w.shape[1]
                ng = ncols // 128
                pre = {}
                for g in range(min(2, ng)):
                    pre[g] = load_w(fm_w, g * 128, 128)
                for g in range(ng):
                    if g + 2 < ng:
                        pre[g + 2] = load_w(fm_w, (g + 2) * 128, 128)
                    wi = pre.pop(g)
                    si = sn[0] % 2
                    sn[0] += 1
                    for ti, (o, w) in enumerate(tiles):
                        b = bank(0, 6)
                        for kc in range(KC):
                            S.op("pe", lambda e, kc=kc: e.matmul(psb[b][:, 0:w], lhsT=wb[wi][:, kc, 0:128], rhs=u[:, kc, o:o + w],
                                                                 start=(kc == 0), stop=(kc == KC - 1)),
                                 reads=[("wb", wi), ("u", ti, kc)], writes=[("ps", b)])
                        en = S.ev()
                        if en == "act":
                            S.op("act", lambda e: e.activation(out=stg[si][:, o:o + w], in_=psb[b][:, 0:w], func=AF.Copy),
                                 reads=[("ps", b)], writes=[("stg", si, ti)])
                        else:
                            S.op("dve", lambda e: e.tensor_copy(out=stg[si][:, o:o + w], in_=psb[b][:, 0:w]),
                                 reads=[("ps", b)], writes=[("stg", si, ti)])
                    for _ in range(2):
                        next(wgen, None)
                    S.dma(fm_dst[g * 128:(g + 1) * 128, :], stg[si][:, :],
                          reads=[("stg", si, ti) for ti in range(len(tiles))], writes=[("fm", id(fm_dst), g)])
                if tm_w is not None:
                    c0 = 0
                    for (ncol, dst, dcol0, stt) in tm_cols:
                        wi = load_w(tm_w, c0, ncol)
                        for blk in range(NT // 128):
                            b = bank(0, 6)
                            ti = (blk * 128) // 512
                            for kc in range(KC):
                                S.op("pe", lambda e, kc=kc: e.matmul(psb[b][:, 0:ncol], lhsT=u[:, kc, 2 + blk * 128:2 + (blk + 1) * 128],
                                                                     rhs=wb[wi][:, kc, 0:ncol], start=(kc == 0), stop=(kc == KC - 1)),
                                     reads=[("wb", wi), ("u", ti, kc), ("u", min(ti + 1, len(tiles) - 1), kc)], writes=[("ps", b)])
                            si = sn[0] % 2
                            sn[0] += 1
                            stile = stz if stt == "z" else std
                            en = S.ev()
                            if en == "act":
                                S.op("act", lambda e: e.activation(out=stile[si][:, 0:ncol], in_=psb[b][:, 0:ncol], func=AF.Copy),
                                     reads=[("ps", b)], writes=[("st" + stt, si)])
                            else:
                                S.op("dve", lambda e: e.tensor_copy(out=stile[si][:, 0:ncol], in_=psb[b][:, 0:ncol]),
                                     reads=[("ps", b)], writes=[("st" + stt, si)])
                            dap = dst[:, blk * ncol:(blk + 1) * ncol] if stt == "d" else dst[blk * 128:(blk + 1) * 128, dcol0:dcol0 + ncol]
                            S.dma(dap, stile[si][:, 0:ncol],
                                  reads=[("st" + stt, si)], writes=[("tm", id(dst), blk, dcol0)])
                        c0 += ncol

            token_set(I["xo"], I["w_in_fm"], pj_fm, I["w_in_tm"],
                      [(512, pj_z, 0, "z"), (512, pj_z, 512, "z"), (32, pj_dt, 0, "d")], None)
            token_set(I["xcs"], I["w_cs_fm"], cs_fm, I["w_cs_tm"], [(16, cs_dt, 0, "d")], None)
            token_set(I["xch"], I["w_ch_fm"], ch_fm, None, [], None)
            for _ in wgen:
                pass
            S.barrier()

    TW = min(512, NT)
    if "ssd" in stages:
        with contextlib.ExitStack() as ph:
            tri_f = T(ph, "tri_f", [128, 128], F32)
            ones_f = T(ph, "ones_f", [128, 128], F32)
            nmask_f = T(ph, "nmask_f", [128, 2, 128], F32)
            nmask = T(ph, "nmask", [128, 2, 128], BF16)
            selm = T(ph, "selm", [16, 16, 128], F32)
            cw = T(ph, "cw", [128, 12, 5], F32)
            cwc = T(ph, "cwc", [128, 10, 5], F32)
            cb = T(ph, "cb", [128, 12], F32)
            cbc = T(ph, "cbc", [128, 10], F32)
            dtb_bc = T(ph, "dtb_bc", [128, 32], F32)
            a_bc = T(ph, "a_bc", [128, 32], F32)
            dtbc_bc = T(ph, "dtbc_bc", [128, 16], F32)
            ac_bc = T(ph, "ac_bc", [128, 16], F32)
            d_bc = T(ph, "d_bc", [128, 16], F32)
            ddiag = T(ph, "ddiag", [128, 16, 128], BF16)
            nw = T(ph, "nw", [128, 8], F32)
            selt = T(ph, "selt", [128, 2], F32)
            dg = T(ph, "dg", [128, 5, 128], BF16)
            cin = [T(ph, "cin%d" % i, [128, NTH], BF16) for i in range(1)]
            ctmp = T(ph, "ctmp", [128, NT], BF16)
            Bf = T(ph, "Bf", [128, NT], BF16)
            Cf = T(ph, "Cf", [128, NT], BF16)
            x_tok = T(ph, "x_tok", [128, J, 512], BF16)
            B_tok = T(ph, "B_tok", [128, J, 128], BF16)
            dtr = T(ph, "dtr", [128, J, 32], F32)
            dtrc = T(ph, "dtrc", [128, J, 16], F32)
            dt_t = T(ph, "dt_t", [128, J, 16], F32)
            a_t = T(ph, "a_t", [128, J, 16], F32)
            lam = T(ph, "lam", [128, J, 16], F32)
            tot = T(ph, "tot", [128, J, 16], F32)
            tml = T(ph, "tml", [128, J, 16], F32)
            e1 = T(ph, "e1", [128, J, 16], F32)
            e2 = T(ph, "e2", [128, J, 16], F32)
            dec = T(ph, "dec", [128, J, 16], F32)
            w_t = T(ph, "w_t", [128, J, 16], F32)
            q_t = T(ph, "q_t", [128, J, 16], F32)
            qTc = [T(ph, "qTc%d" % i, [16, 128], F32) for i in range(2)]
            selbc = T(ph, "selbc", [16, 4, 4, 128], F32)
            nm4 = T(ph, "nm4", [128, 4, 128], BF16)
            xdt = [T(ph, "xdt%d" % i, [128, 512], BF16) for i in range(4)]
            stb_all = T(ph, "stb_all", [128, J, 512], BF16)
            st_f = T(ph, "st_f", [128, 512], F32)
            st_b = T(ph, "st_b", [128, 512], F32)
            st_c = T(ph, "st_c", [128, 512], F32)
            stf_bf_l = [T(ph, "stf_bf%d" % i, [128, 512], BF16) for i in range(2)]
            xw = [T(ph, "xw%d" % i, [128, 512], BF16) for i in range(2)]
            scT_l = [T(ph, "scT%d" % i, [128, 128], F32) for i in range(2)]
            Et = [T(ph, "Et%d" % i, [128, 4, 128], F32) for i in range(2)]
            MT = [T(ph, "MT%d" % i, [128, 4, 128], BF16) for i in range(2)]
            zt_ = [T(ph, "zt%d" % i, [128, 512], BF16) for i in range(2)]
            zs_l = [T(ph, "zs%d" % i, [128, 512], F32) for i in range(2)]
            ysb_l = [T(ph, "ysb%d" % i, [128, 512], F32) for i in range(2)]
            yt1_l = [T(ph, "yt1%d" % i, [128, 512], F32) for i in range(2)]
            yn_l = [T(ph, "yn%d" % i, [128, 512], BF16) for i in range(2)]
            ssq_l = [T(ph, "ssq%d" % i, [128, 1], F32) for i in range(2)]
            sqj_l = [T(ph, "sqj%d" % i, [128, 512], BF16) for i in range(2)]
            dummy = T(ph, "dummy", [128, 1], F32)
            SC = min(4, J)
            mst = [T(ph, "mst%d" % i, [128, 4, SC * 128], BF16) for i in range(2)]

            S.dma(tri_f[:, :], I["tri"][:, :], writes=["tri_f"])
            S.op("pool", lambda e: e.memset(ones_f[:, :], 1.0), writes=["ones_f"])
            S.dma(nmask_f[:, :, :], I["negmask"].rearrange("p (d t) -> p d t", d=2), writes=["nmask_f"])
            S.op("dve", lambda e: e.tensor_copy(out=nmask[:, :, :], in_=nmask_f[:, :, :]), reads=["nmask_f"], writes=["nmask"])
            S.dma(selm[:, :, :], I["selmat"][0:16, :].rearrange("k (h s) -> k h s", s=128)[:, 0:16, :], writes=["selm"])
            S.op("pool", lambda e: e.tensor_scalar(out=selm[:, :, :], in0=selm[:, :, :], scalar1=-1.0, scalar2=None, op0=ALU.mult),
                 reads=["selm"], writes=["selm"])
            for quad in range(4):
                for i_ in range(4):
                    col_ = (i_ % 2) * 8 + 2 * quad + i_ // 2
                    S.op("dve", lambda e, quad=quad, i_=i_, col_=col_: e.tensor_scalar(out=selbc[:, quad, i_, :], in0=selm[:, col_, :], scalar1=-1.0, scalar2=None, op0=ALU.mult),
                         reads=["selm"], writes=[("selbc", quad, i_)])
            S.merge("selbc", [("selbc", q_, i_) for q_ in range(4) for i_ in range(4)])
            for i_ in range(4):
                S.op("dve", lambda e, i_=i_: e.tensor_copy(out=nm4[:, i_, :], in_=nmask[:, i_ % 2, :]), reads=["nmask"], writes=[("nm4", i_)])
            S.merge("nm4", [("nm4", i_) for i_ in range(4)])
            S.dma(cw[:, :, :], I["ssd_cw"].rearrange("p (g j) -> p g j", j=5), writes=["cw"])
            S.dma(cwc[:, :, :], I["ssd_cw_ctx"].rearrange("p (g j) -> p g j", j=5), writes=["cwc"])
            S.dma(cb[:, :], I["ssd_cb"][:, :], writes=["cb"])
            S.dma(cbc[:, :], I["ssd_cb_ctx"][:, :], writes=["cbc"])
            S.dma(dtb_bc[:, :], I["dtb"][0, :].partition_broadcast(128), writes=["dtb_bc"])
            S.dma(a_bc[:, :], I["alog"][0, :].partition_broadcast(128), writes=["a_bc"])
            S.dma(dtbc_bc[:, :], I["dtb_ctx"][0, :].partition_broadcast(128), writes=["dtbc_bc"])
            S.dma(ac_bc[:, :], I["alog_ctx"][0, :].partition_broadcast(128), writes=["ac_bc"])
            S.dma(d_bc[:, :], I["ssd_d"][0, :].partition_broadcast(128), writes=["d_bc"])
            S.dma(nw[:, :], I["ssd_nw"][:, :], writes=["nw"])
            S.dma(selt[:, :], I["sel"][:, :], writes=["selt"])
            S.dma(dtr[:, :, :], pj_dt.rearrange("p (c k) -> p c k", k=32), writes=["dtr"])
            S.dma(dtrc[:, :, :], cs_dt.rearrange("p (c k) -> p c k", k=16), writes=["dtrc"])
            for t_, k_ in ((a_bc, "a_bc"), (ac_bc, "ac_bc")):
                S.op("act", lambda e, t_=t_: e.activation(out=t_[:, :], in_=t_[:, :], func=AF.Exp), reads=[k_], writes=[k_])
                S.op("dve", lambda e, t_=t_: e.tensor_scalar(out=t_[:, :], in0=t_[:, :], scalar1=-1.0, scalar2=None, op0=ALU.mult),
                     reads=[k_], writes=[k_])
            for h in range(16):
                S.op("dve", lambda e, h=h: e.tensor_scalar(out=ddiag[:, h, :], in0=ident_f[:, :], scalar1=d_bc[:, h:h + 1], scalar2=None, op0=ALU.mult),
                     reads=["ident_f", "d_bc"], writes=[("ddiag", h)])
            cn = [0]

            def conv_group(src, row0, wt, wkey, g, bt, bkey, dst, dkey):
                ci = 0
                S.dma(cin[ci][:, :], src[row0:row0 + 128, :], writes=[("cin", ci)])
                for j in range(5):
                    S.op("pool", lambda e, j=j: e.tensor_scalar(out=dg[:, j, :], in0=ident_f[:, :], scalar1=wt[:, g, j:j + 1], scalar2=None, op0=ALU.mult),
                         reads=["ident_f", wkey], writes=[("dg", j)])
                for o in range(0, NT, TW):
                    b = bank(6, 8)
                    for j in range(5):
                        S.op("pe", lambda e, j=j: e.matmul(psb[b][:, 0:TW], lhsT=dg[:, j, :], rhs=cin[ci][:, o + j:o + j + TW],
                                                           start=(j == 0), stop=(j == 4)),
                             reads=[("dg", j), ("cin", ci)], writes=[("ps", b)])
                    S.op("act", lambda e: e.activation(out=dst[:, o:o + TW], in_=psb[b][:, 0:TW], func=AF.Silu, bias=bt[:, g:g + 1]),
                         reads=[("ps", b), bkey], writes=[(dkey, o // TW)])
                return [(dkey, o // TW) for o in range(0, NT, TW)]

            def to_tok(srct, skeys, dst, dcol0, dkey):
                for c0 in range(0, J, 8):
                    nb = min(8, J - c0)
                    b = bank(6, 8)
                    pv = psb[b][:, :].bitcast(BF16).rearrange("p (a b) -> p a b", b=128)
                    for i in range(nb):
                        S.op("pe", lambda e, i=i: e.transpose(out=pv[:, i, :], in_=srct[:, (c0 + i) * 128:(c0 + i + 1) * 128], identity=ident_b[:, :]),
                             reads=list(skeys) + ["ident_b"], writes=[("ps", b)])
                    en = S.ev()
                    if en == "act":
                        S.op("act", lambda e: e.activation(out=dst[:, c0:c0 + nb, dcol0:dcol0 + 128], in_=pv[:, 0:nb, :], func=AF.Copy),
                             reads=[("ps", b)], writes=[(dkey, c0, dcol0)])
                    else:
                        S.op("dve", lambda e: e.tensor_copy(out=dst[:, c0:c0 + nb, dcol0:dcol0 + 128], in_=pv[:, 0:nb, :]),
                             reads=[("ps", b)], writes=[(dkey, c0, dcol0)])

            def dt_math(raw, rkey, segs, bias_t, bkey, a_src, akey, ncol, bwd_cols):
                nn = ncol
                for (d0, s0, n_) in segs:
                    S.op("dve", lambda e, d0=d0, s0=s0, n_=n_: e.tensor_tensor(
                        out=dt_t[:, :, d0:d0 + n_], in0=raw[:, :, s0:s0 + n_],
                        in1=bias_t[:, s0:s0 + n_].unsqueeze(1).broadcast_to([128, J, n_]), op=ALU.add),
                        reads=[rkey, bkey], writes=["dt_t"])
                S.op("act", lambda e: e.activation(out=dt_t[:, :, 0:nn], in_=dt_t[:, :, 0:nn], func=AF.Exp), reads=["dt_t"], writes=["dt_t"])
                S.op("act", lambda e: e.activation(out=dt_t[:, :, 0:nn], in_=dt_t[:, :, 0:nn], func=AF.Ln, bias=ones_f[:, 0:1]),
                     reads=["dt_t", "ones_f"], writes=["dt_t"])
                for (d0, s0, n_) in segs:
                    S.op("dve", lambda e, d0=d0, s0=s0, n_=n_: e.tensor_tensor(
                        out=a_t[:, :, d0:d0 + n_], in0=dt_t[:, :, d0:d0 + n_],
                        in1=a_src[:, s0:s0 + n_].unsqueeze(1).broadcast_to([128, J, n_]), op=ALU.mult),
                        reads=["dt_t", akey], writes=["a_t"])
                b1 = 4
                b2 = 5
                for c in range(J):
                    S.op("pe", lambda e, c=c: e.matmul(psb[b1][:, c * 16:c * 16 + nn], lhsT=tri_f[:, :], rhs=a_t[:, c, 0:nn], start=True, stop=True),
                         reads=["tri_f", "a_t"], writes=[("ps", b1)])
                    S.op("pe", lambda e, c=c: e.matmul(psb[b2][:, c * 16:c * 16 + nn], lhsT=ones_f[:, :], rhs=a_t[:, c, 0:nn], start=True, stop=True),
                         reads=["ones_f", "a_t"], writes=[("ps", b2)])
                pv1 = psb[b1][:, 0:J * 16].rearrange("p (c k) -> p c k", k=16)
                pv2 = psb[b2][:, 0:J * 16].rearrange("p (c k) -> p c k", k=16)
                S.op("dve", lambda e: e.tensor_copy(out=lam[:, :, 0:nn], in_=pv1[:, :, 0:nn]), reads=[("ps", b1)], writes=["lam"])
                S.op("act", lambda e: e.activation(out=tot[:, :, 0:nn], in_=pv2[:, :, 0:nn], func=AF.Copy), reads=[("ps", b2)], writes=["tot"])
                if bwd_cols:
                    lo, hi = bwd_cols
                    S.op("dve", lambda e: e.tensor_tensor(out=lam[:, :, lo:hi], in0=lam[:, :, lo:hi], in1=a_t[:, :, lo:hi], op=ALU.subtract),
                         reads=["lam", "a_t"], writes=["lam"])
                S.op("dve", lambda e: e.tensor_tensor(out=tml[:, :, 0:nn], in0=tot[:, :, 0:nn], in1=lam[:, :, 0:nn], op=ALU.subtract),
                     reads=["tot", "lam"], writes=["tml"])
                S.op("act", lambda e: e.activation(out=e1[:, :, 0:nn], in_=lam[:, :, 0:nn], func=AF.Exp), reads=["lam"], writes=["e1"])
                S.op("act", lambda e: e.activation(out=e2[:, :, 0:nn], in_=tml[:, :, 0:nn], func=AF.Exp), reads=["tml"], writes=["e2"])
                S.op("act", lambda e: e.activation(out=dec[:, :, 0:nn], in_=tot[:, :, 0:nn], func=AF.Exp), reads=["tot"], writes=["dec"])
                fhi = bwd_cols[0] if bwd_cols else nn
                S.op("dve", lambda e: e.tensor_tensor(out=w_t[:, :, 0:fhi], in0=dt_t[:, :, 0:fhi], in1=e2[:, :, 0:fhi], op=ALU.mult),
                     reads=["dt_t", "e2"], writes=["w_t"])
                S.op("dve", lambda e: e.tensor_scalar(out=q_t[:, :, 0:fhi], in0=lam[:, :, 0:fhi], scalar1=-1.0, scalar2=None, op0=ALU.mult),
                     reads=["lam"], writes=["q_t"])
                if bwd_cols:
                    lo, hi = bwd_cols
                    S.op("dve", lambda e: e.tensor_tensor(out=w_t[:, :, lo:hi], in0=dt_t[:, :, lo:hi], in1=e1[:, :, lo:hi], op=ALU.mult),
                         reads=["dt_t", "e1", "w_t"], writes=["w_t"])
                    S.op("dve", lambda e: e.tensor_copy(out=q_t[:, :, lo:hi], in_=lam[:, :, lo:hi]), reads=["lam", "q_t"], writes=["q_t"])

            def make_xw(c, col0, xi):
                S.op("pool", lambda e: e.tensor_tensor(out=xw[xi][:, :].rearrange("p (h d) -> p h d", d=64),
                                                       in0=x_tok[:, c, :].rearrange("p (h d) -> p h d", d=64),
                                                       in1=w_t[:, c, col0:col0 + 8].unsqueeze(2).broadcast_to([128, 8, 64]), op=ALU.mult),
                     reads=["x_tok_all", "w_t"], writes=[("xw", xi)])

            def state_step(c, col0, xi, stt, skey, pb):
                S.op("pe", lambda e: e.matmul(psb[pb][:, :], lhsT=B_tok[:, c, :], rhs=xw[xi][:, :], start=True, stop=True),
                     reads=["B_tok_all", ("xw", xi)], writes=[("ps", pb)])
                S.op("dve", lambda e: e.tensor_tensor(out=stt[:, :].rearrange("p (h d) -> p h d", d=64),
                                                      in0=stt[:, :].rearrange("p (h d) -> p h d", d=64),
                                                      in1=dec[:, c, col0:col0 + 8].unsqueeze(2).broadcast_to([128, 8, 64]), op=ALU.mult),
                     reads=[skey, "dec"], writes=[skey])
                S.op("dve", lambda e: e.tensor_tensor(out=stt[:, :], in0=stt[:, :], in1=psb[pb][:, :], op=ALU.add),
                     reads=[skey, ("ps", pb)], writes=[skey])

            for g2 in range(2):
                keys = conv_group(cs_fm, 1024 + g2 * 128, cwc, "cwc", 8 + g2, cbc, "cbc", ctmp, "ctmp")
                to_tok(ctmp, keys, B_tok, 0, "B_tok")
                for xg in range(4):
                    keys = conv_group(cs_fm, (g2 * 4 + xg) * 128, cwc, "cwc", g2 * 4 + xg, cbc, "cbc", ctmp, "ctmp")
                    to_tok(ctmp, keys, x_tok, xg * 128, "x_tok")
                S.op("pool", lambda e: e.memset(st_c[:, :], 0.0), writes=["st_c"])
                S.merge("x_tok_all", [("x_tok", c0, x0) for c0 in range(0, J, 8) for x0 in range(0, 512, 128)])
                S.merge("B_tok_all", [("B_tok", c0, 0) for c0 in range(0, J, 8)])
                dt_math(dtrc, "dtrc", [(0, g2 * 8, 8)], dtbc_bc, "dtbc_bc", ac_bc, "ac_bc", 8, None)
                for c in range(J):
                    make_xw(c, 0, c % 2)
                    state_step(c, 0, c % 2, st_c, "st_c", 6 + c % 2)
                S.op("dve", lambda e: e.tensor_scalar(out=st_f[:, :], in0=st_c[:, :], scalar1=selt[:, 0:1], scalar2=None, op0=ALU.mult),
                     reads=["st_c", "selt"], writes=["st_f"])
                S.op("dve", lambda e: e.tensor_scalar(out=st_b[:, :], in0=st_c[:, :], scalar1=selt[:, 1:2], scalar2=None, op0=ALU.mult),
                     reads=["st_c", "selt"], writes=["st_b"])
                kB = conv_group(pj_fm, 1024 + g2 * 128, cw, "cw", 8 + g2, cb, "cb", Bf, "Bf")
                to_tok(Bf, kB, B_tok, 0, "B_tok")
                kC = conv_group(pj_fm, 1280 + g2 * 128, cw, "cw", 10 + g2, cb, "cb", Cf, "Cf")
                for xg in range(4):
                    keys = conv_group(pj_fm, (g2 * 4 + xg) * 128, cw, "cw", g2 * 4 + xg, cb, "cb", ctmp, "ctmp")
                    to_tok(ctmp, keys, x_tok, xg * 128, "x_tok")
                S.merge("x_tok_all", [("x_tok", c0, x0) for c0 in range(0, J, 8) for x0 in range(0, 512, 128)])
                S.merge("B_tok_all", [("B_tok", c0, 0) for c0 in range(0, J, 8)])
                S.merge("BC_all", kB + kC)
                dt_math(dtr, "dtr", [(0, g2 * 8, 8), (8, 16 + g2 * 8, 8)], dtb_bc, "dtb_bc", a_bc, "a_bc", 16, (8, 16))
                for c in range(J - 1, -1, -1):
                    S.op("act", lambda e, c=c: e.activation(out=stb_all[:, c, :], in_=st_b[:, :], func=AF.Copy),
                         reads=["st_b"], writes=[("stb", c)])
                    make_xw(c, 8, c % 2)
                    state_step(c, 8, c % 2, st_b, "st_b", 6 + c % 2)
                def fwdA(c):
                    tsl = slice(c * 128, (c + 1) * 128)
                    zi = c % 2
                    scT, zs, ysb, yt1, yn, ssq, sqj, stf_bf = (scT_l[zi], zs_l[zi], ysb_l[zi], yt1_l[zi], yn_l[zi], ssq_l[zi], sqj_l[zi], stf_bf_l[zi])
                    kx = lambda k_: (k_, zi)
                    qi = c % 2
                    S.dma(zt_[zi][:, :], pj_z[c * 128:(c + 1) * 128, g2 * 512:(g2 + 1) * 512], writes=[("z", zi)])
                    S.op("act", lambda e: e.activation(out=stf_bf[:, :], in_=st_f[:, :], func=AF.Copy), reads=["st_f"], writes=[kx("stf_bf")])
                    S.op("pe", lambda e: e.matmul(psb[0][:, 0:128], lhsT=Bf[:, tsl], rhs=Cf[:, tsl], start=True, stop=True),
                         reads=["BC_all"], writes=[("ps", 0)])
                    S.op("act", lambda e: e.activation(out=scT[:, :], in_=psb[0][:, 0:128], func=AF.Copy), reads=[("ps", 0)], writes=[kx("scT")])
                    qi = c % 2
                    S.op("pe", lambda e: e.transpose(out=psb[0][0:16, 128:256], in_=q_t[:, c, :], identity=ident_f[:, :]),
                         reads=["q_t", "ident_f", kx("scT")], writes=[("ps", 0)])
                    S.op("act", lambda e: e.activation(out=qTc[qi][:, :], in_=psb[0][0:16, 128:256], func=AF.Copy),
                         reads=[("ps", 0)], writes=[("qTc", qi)])
                    for d_ in range(2):
                        xi_ = qi * 2 + d_
                        S.op("pool", lambda e, d_=d_, xi_=xi_: e.tensor_tensor(out=xdt[xi_][:, :].rearrange("p (h d) -> p h d", d=64),
                                                                               in0=x_tok[:, c, :].rearrange("p (h d) -> p h d", d=64),
                                                                               in1=dt_t[:, c, d_ * 8:d_ * 8 + 8].unsqueeze(2).broadcast_to([128, 8, 64]), op=ALU.mult),
                             reads=["x_tok_all", "dt_t"], writes=[("xdt", xi_)])
                    for quad in range(4):
                        rb = 1 + quad % 2
                        ei = quad % 2
                        S.op("pe", lambda e, quad=quad: e.matmul(psb[rb][:, :], lhsT=qTc[qi][:, :], rhs=selbc[:, quad, :, :].rearrange("k i t -> k (i t)"), start=True, stop=False),
                             reads=[("qTc", qi), "selbc"], writes=[("ps", rb)])
                        S.op("pe", lambda e: e.matmul(psb[rb][:, :], lhsT=ident_b[:, :], rhs=nm4[:, :, :].rearrange("p i t -> p (i t)"), start=False, stop=False),
                             reads=["ident_b", "nm4"], writes=[("ps", rb)])
                        for i_ in range(4):
                            col = (i_ % 2) * 8 + 2 * quad + i_ // 2
                            S.op("pe", lambda e, i_=i_, col=col: e.matmul(psb[rb][:, i_ * 128:(i_ + 1) * 128], lhsT=selm[:, col, :], rhs=qTc[qi][:, :], start=False, stop=(i_ == 3)),
                                 reads=["selm", ("qTc", qi)], writes=[("ps", rb)])
                        S.op("act", lambda e: e.activation(out=Et[ei][:, :, :].rearrange("p i t -> p (i t)"), in_=psb[rb][:, :], func=AF.Exp),
                             reads=[("ps", rb)], writes=[("Et", ei)])
                        S.op("dve", lambda e: e.tensor_tensor(out=MT[ei][:, :, :], in0=Et[ei][:, :, :], in1=scT[:, :].unsqueeze(1).broadcast_to([128, 4, 128]), op=ALU.mult),
                             reads=[("Et", ei), kx("scT")], writes=[("MT", ei)])
                        for hh_ in range(2):
                            hl = 2 * quad + hh_
                            h = g2 * 8 + hl
                            osl = slice(hl * 64, (hl + 1) * 64)
                            S.op("pe", lambda e, hh_=hh_, osl=osl: e.matmul(psb[3][:, osl], lhsT=MT[ei][:, hh_ * 2, :], rhs=xdt[qi * 2][:, osl], start=True, stop=False),
                                 reads=[("MT", ei), ("xdt", qi * 2)], writes=[("ps", 3)])
                            S.op("pe", lambda e, hh_=hh_, osl=osl: e.matmul(psb[3][:, osl], lhsT=MT[ei][:, hh_ * 2 + 1, :], rhs=xdt[qi * 2 + 1][:, osl], start=False, stop=False),
                                 reads=[("MT", ei), ("xdt", qi * 2 + 1)], writes=[("ps", 3)])
                            S.op("pe", lambda e, h=h, osl=osl: e.matmul(psb[3][:, osl], lhsT=ddiag[:, h, :], rhs=x_tok[:, c, osl], start=False, stop=True),
                                 reads=[("ddiag", h), "x_tok_all"], writes=[("ps", 3)])
                    S.op("pe", lambda e: e.matmul(psb[4][:, :], lhsT=Cf[:, tsl], rhs=stf_bf[:, :], start=True, stop=True),
                         reads=["BC_all", kx("stf_bf")], writes=[("ps", 4)])
                    S.op("pe", lambda e: e.matmul(psb[5][:, :], lhsT=Cf[:, tsl], rhs=stb_all[:, c, :], start=True, stop=True),
                         reads=["BC_all", ("stb", c)], writes=[("ps", 5)])
                def fwdC(c):
                    tsl = slice(c * 128, (c + 1) * 128)
                    zi = c % 2
                    scT, zs, ysb, yt1, yn, ssq, sqj, stf_bf = (scT_l[zi], zs_l[zi], ysb_l[zi], yt1_l[zi], yn_l[zi], ssq_l[zi], sqj_l[zi], stf_bf_l[zi])
                    kx = lambda k_: (k_, zi)
                    qi = c % 2
                    S.op("act", lambda e: e.activation(out=ysb[:, :], in_=psb[3][:, :], func=AF.Copy), reads=[("ps", 3)], writes=[kx("ysb")])
                    for d_, pb in ((0, 4), (1, 5)):
                        osrc = e1 if d_ == 0 else e2
                        okey = "e1" if d_ == 0 else "e2"
                        S.op("dve", lambda e, pb=pb, osrc=osrc, d_=d_: e.tensor_tensor(
                            out=yt1[:, :].rearrange("p (h d) -> p h d", d=64), in0=psb[pb][:, :].rearrange("p (h d) -> p h d", d=64),
                            in1=osrc[:, c, d_ * 8:d_ * 8 + 8].unsqueeze(2).broadcast_to([128, 8, 64]), op=ALU.mult),
                            reads=[("ps", pb), okey], writes=[kx("yt1")])
                        S.op("pool", lambda e: e.tensor_tensor(out=ysb[:, :], in0=ysb[:, :], in1=yt1[:, :], op=ALU.add),
                             reads=[kx("ysb"), kx("yt1")], writes=[kx("ysb")])
                    make_xw(c, 0, c % 2)
                    state_step(c, 0, c % 2, st_f, "st_f", 6 + c % 2)
                def fwdB(c):
                    tsl = slice(c * 128, (c + 1) * 128)
                    zi = c % 2
                    scT, zs, ysb, yt1, yn, ssq, sqj, stf_bf = (scT_l[zi], zs_l[zi], ysb_l[zi], yt1_l[zi], yn_l[zi], ssq_l[zi], sqj_l[zi], stf_bf_l[zi])
                    kx = lambda k_: (k_, zi)
                    qi = c % 2
                    S.op("act", lambda e: e.activation(out=zs[:, :], in_=zt_[zi][:, :], func=AF.Silu), reads=[("z", zi)], writes=[kx("zs")])
                    S.op("dve", lambda e: e.tensor_tensor(out=ysb[:, :], in0=ysb[:, :], in1=zs[:, :], op=ALU.mult), reads=[kx("ysb"), kx("zs")], writes=[kx("ysb")])
                    S.op("act", lambda e: e.activation(out=sqj[:, :], in_=ysb[:, :], func=AF.Square, accum_out=ssq[:, 0:1]),
                         reads=[kx("ysb")], writes=[kx("ssq"), kx("sqj")])
                    S.op("act", lambda e: e.activation(out=ssq[:, :], in_=ssq[:, :], func=AF.Ln, scale=1.0 / 512, bias=eps_t[:, 0:1]),
                         reads=[kx("ssq"), "eps_t"], writes=[kx("ssq")])
                    S.op("act", lambda e: e.activation(out=ssq[:, :], in_=ssq[:, :], func=AF.Exp, scale=-0.5), reads=[kx("ssq")], writes=[kx("ssq")])
                    S.op("dve", lambda e: e.tensor_scalar(out=yn[:, :], in0=ysb[:, :], scalar1=ssq[:, 0:1], scalar2=None, op0=ALU.mult),
                         reads=[kx("ysb"), kx("ssq")], writes=[kx("yn")])
                    pv = psb[6 + c % 2][:, :].bitcast(BF16).rearrange("p (a b) -> p a b", b=128)
                    mi_ = (c // SC) % 2
                    for xg in range(4):
                        S.op("pe", lambda e, xg=xg: e.transpose(out=pv[:, xg, :], in_=yn[:, xg * 128:(xg + 1) * 128], identity=ident_b[:, :]),
                             reads=[kx("yn"), "ident_b"], writes=[("ps", 6 + c % 2)])
                    for xg in range(4):
                        S.op("dve", lambda e, xg=xg: e.tensor_scalar(
                            out=mst[mi_][:, xg, (c % SC) * 128:(c % SC + 1) * 128], in0=pv[:, xg, :], scalar1=nw[:, g2 * 4 + xg:g2 * 4 + xg + 1],
                            scalar2=None, op0=ALU.mult),
                            reads=[("ps", 6 + c % 2), "nw"], writes=[("mst", mi_, c % SC, xg)])
                    if c % SC == SC - 1:
                        c0 = c - SC + 1
                        for xg in range(4):
                            S.dma(mix[(g2 * 4 + xg) * 128:(g2 * 4 + xg + 1) * 128, c0 * 128:(c + 1) * 128], mst[mi_][:, xg, :],
                                  reads=[("mst", mi_, cc, xg) for cc in range(SC)], writes=[("mixd", g2, xg, c0)])
                fwdA(0)
                for c in range(J):
                    fwdC(c)
                    if c + 1 < J:
                        fwdA(c + 1)
                    fwdB(c)
            S.barrier()

    if "hy" in stages:
        ft_s = dscr("ft_s", (J, 128, 4 * 128), BF16)
        et_s = dscr("et_s", (J, 128, 2 * 128), BF16)
        h3_s = dscr("h3_s", (128, 2 * NFFT), BF16)
        with contextlib.ExitStack() as ph:
            fw1 = T(ph, "fw1", [33, 64], F32)
            fw2 = T(ph, "fw2", [64, 64], F32)
            fw3 = T(ph, "fw3", [64, 128], F32)
            fr = T(ph, "fr", [128, 1], F32)
            fbs = T(ph, "fbs", [128, 3], F32)
            S.dma(fw1[:, :], I["fw1"][:, :], writes=["fw1"])
            S.dma(fw2[:, :], I["fw2"][:, :], writes=["fw2"])
            S.dma(fw3[:, :], I["fw3"][:, :], writes=["fw3"])
            S.dma(fr[:, :], I["ffreq"][:, :], writes=["fr"])
            S.op("pool", lambda e: e.memset(fbs[:, :], 0.0), writes=["fbs"])
            S.dma(fbs[0:64, 0:1], I["fb1"][:, :], reads=["fbs"], writes=["fbs1"])
            S.dma(fbs[0:64, 1:2], I["fb2"][:, :], reads=["fbs"], writes=["fbs2"])
            S.dma(fbs[:, 2:3], I["fb3"][:, :], reads=["fbs"], writes=["fbs3"])
            S.op("dve", lambda e: e.tensor_tensor(out=fbs[:, :], in0=fbs[:, :], in1=fr[:, 0:1].broadcast_to([128, 3]), op=ALU.mult),
                 reads=["fbs", "fbs1", "fbs2", "fbs3", "fr"], writes=["fbsx"])
            NB3 = 3
            hA_l = [T(ph, "hA%d" % i, [128, 512], F32) for i in range(NB3)]
            hB_l = [T(ph, "hB%d" % i, [128, 512], F32) for i in range(NB3)]
            mw_l = [T(ph, "mwrap%d" % i, [128, 512], F32) for i in range(NB3)]
            PI = math.pi

            def mlp_layer(ps_ap, np_, col, dst, mwrap, mwk):
                S.op("dve", lambda e: e.tensor_scalar(out=dst[0:np_, :], in0=ps_ap, scalar1=fr[0:np_, 0:1], scalar2=fbs[0:np_, col:col + 1],
                                                      op0=ALU.mult, op1=ALU.add), reads=[pkey[0], "fr", "fbsx"], writes=[id(dst)])
                for (cmp, thr, add) in ((ALU.is_gt, PI, -2 * PI), (ALU.is_lt, -PI, 2 * PI)):
                    S.op("dve", lambda e, cmp=cmp, thr=thr: e.tensor_scalar(out=mwrap[0:np_, :], in0=dst[0:np_, :], scalar1=thr, scalar2=None, op0=cmp),
                         reads=[id(dst)], writes=[mwk])
                    S.op("dve", lambda e, add=add: e.scalar_tensor_tensor(out=dst[0:np_, :], in0=mwrap[0:np_, :], scalar=add, in1=dst[0:np_, :],
                                                                         op0=ALU.mult, op1=ALU.add),
                         reads=[id(dst), mwk], writes=[id(dst)])
                S.op("act", lambda e: e.activation(out=dst[0:np_, :], in_=dst[0:np_, :], func=AF.Sin), reads=[id(dst)], writes=[id(dst)])

            pkey = [None]
            ztl = [T(ph, "ztl3_%d" % i, [33, 512], F32) for i in range(NB3)]
            mft = [T(ph, "mft3_%d" % i, [128, 512], F32) for i in range(NB3)]
            h3o = [T(ph, "h3o3_%d" % i, [128, 512], BF16) for i in range(NB3)]
            NTI = 2 * NFFT // 512
            for t0_ in range(0, NTI, NB3):
                tis = list(range(t0_, min(NTI, t0_ + NB3)))
                for ti in tis:
                    i = ti % NB3
                    S.dma(ztl[i][:, :], I["zt"][:, ti * 512:(ti + 1) * 512], writes=[("ztl", i)])
                    S.dma(mft[i][:, :], I["mfb"][:, ti * 512:(ti + 1) * 512], writes=[("mft", i)])
                for ti in tis:
                    i = ti % NB3
                    b_ = bank(0, 8)
                    pkey[0] = ("ps", b_)
                    S.op("pe", lambda e, i=i, b_=b_: e.matmul(psb[b_][0:64, :], lhsT=fw1[:, :], rhs=ztl[i][:, :], start=True, stop=True),
                         reads=["fw1", ("ztl", i)], writes=[("ps", b_)])
                    mlp_layer(psb[b_][0:64, :], 64, 0, hA_l[i], mw_l[i], ("mwrap", i))
                for ti in tis:
                    i = ti % NB3
                    b_ = bank(0, 8)
                    pkey[0] = ("ps", b_)
                    S.op("pe", lambda e, i=i, b_=b_: e.matmul(psb[b_][0:64, :], lhsT=fw2[:, :], rhs=hA_l[i][0:64, :], start=True, stop=True),
                         reads=["fw2", id(hA_l[i])], writes=[("ps", b_)])
                    mlp_layer(psb[b_][0:64, :], 64, 1, hB_l[i], mw_l[i], ("mwrap", i))
                for ti in tis:
                    i = ti % NB3
                    b_ = bank(0, 8)
                    pkey[0] = ("ps", b_)
                    S.op("pe", lambda e, i=i, b_=b_: e.matmul(psb[b_][:, :], lhsT=fw3[:, :], rhs=hB_l[i][0:64, :], start=True, stop=True),
                         reads=["fw3", id(hB_l[i])], writes=[("ps", b_)])
                    mlp_layer(psb[b_][:, :], 128, 2, hA_l[i], mw_l[i], ("mwrap", i))
                for ti in tis:
                    i = ti % NB3
                    S.op("pool", lambda e, i=i: e.tensor_tensor(out=h3o[i][:, :], in0=hA_l[i][:, :], in1=mft[i][:, :], op=ALU.mult),
                         reads=[id(hA_l[i]), ("mft", i)], writes=[("h3o", i)])
                    S.dma(h3_s[:, ti * 512:(ti + 1) * 512], h3o[i][:, :], reads=[("h3o", i)], writes=[("h3_s", ti)])
            S.barrier()

        with contextlib.ExitStack() as ph:
            NQ = NSUB * 256
            bufA = T(ph, "bufA", [128, NQ], BF16)
            bufB = T(ph, "bufB", [128, NQ], BF16)
            bufC = T(ph, "bufC", [128, NQ], BF16)
            bufD = T(ph, "bufD", [128, NQ], BF16)
            Yb = T(ph, "Yb", [128, NQ], BF16)
            Pt = T(ph, "Pt", [128, J * 128], F32)
            gt = T(ph, "gt", [128, J, 128], BF16)
            x0t = T(ph, "x0t", [128, J, 128], BF16)
            fm = [T(ph, "fm%d" % i, [128, NT], BF16) for i in range(2)]
            cin = T(ph, "hcin", [128, NTH], BF16)
            ftr = T(ph, "ftr", [128, J, 4, 128], BF16)
            etr = T(ph, "etr", [128, J, 2, 128], BF16)
            ftv = I["ftab"].rearrange("p (j m) -> p j m", m=512)
            etv = I["etab"].rearrange("p (j m) -> p j m", m=256)
            JT = max(1, (J * 128) // 512)
            kf, ke = [], []
            for j0_ in range(0, J, JT):
                S.dma(Pt[:, 0:JT * 512].rearrange("p (j m) -> p j m", m=512), ftv[:, j0_:j0_ + JT, :], writes=["P1", "P2"])
                S.op("act" if (j0_ // JT) % 2 else "dve",
                     (lambda e, j0_=j0_: e.activation(out=ftr[:, j0_:j0_ + JT, :, :].rearrange("p j m k -> p j (m k)"), in_=Pt[:, 0:JT * 512].rearrange("p (j m) -> p j m", m=512), func=AF.Copy))
                     if (j0_ // JT) % 2 else
                     (lambda e, j0_=j0_: e.tensor_copy(out=ftr[:, j0_:j0_ + JT, :, :].rearrange("p j m k -> p j (m k)"), in_=Pt[:, 0:JT * 512].rearrange("p (j m) -> p j m", m=512))),
                     reads=["P1", "P2"], writes=[("ftr", j0_)])
                kf.append(("ftr", j0_))
            JE = min(J, 2 * JT)
            for j0_ in range(0, J, JE):
                S.dma(Pt[:, 0:JE * 256].rearrange("p (j m) -> p j m", m=256), etv[:, j0_:j0_ + JE, :], writes=["P1", "P2"])
                S.op("dve", lambda e, j0_=j0_: e.tensor_copy(out=etr[:, j0_:j0_ + JE, :, :].rearrange("p j m k -> p j (m k)"), in_=Pt[:, 0:JE * 256].rearrange("p (j m) -> p j m", m=256)),
                     reads=["P1", "P2"], writes=[("etr", j0_)])
                ke.append(("etr", j0_))
            S.merge("ftr", kf)
            S.merge("etr", ke)
            h3l = [T(ph, "h3l%d" % i, [128, 512], BF16) for i in range(3)]
            wnd = [T(ph, "wnd%d" % i, [128, 4, 128], F32) for i in range(2)]
            w2f = T(ph, "w2f", [128, 3, 128], F32)
            w2b = T(ph, "w2b", [128, 3, 128], BF16)
            hcw = T(ph, "hcw", [128, 24, 3], F32)
            hcwc = T(ph, "hcwc", [128, 16, 3], F32)
            hcb = T(ph, "hcb", [128, 24], F32)
            hyb = T(ph, "hyb", [128, 1024], F32)
            hnw = T(ph, "hnw", [128, 8], F32)
            absd = T(ph, "absd", [128, 1024], F32)
            negt = T(ph, "negt", [128, 4 * J], F32)
            w4b = T(ph, "w4b", [128, 1024], BF16)
            dg3 = T(ph, "dg3", [128, 3, 128], BF16)
            ssq2 = T(ph, "ssq2", [128, 2 * J], F32)
            S.dma(w2f[:, :, :], I["w2tab"].rearrange("p (i m) -> p i m", i=3), writes=["w2f"])
            S.op("dve", lambda e: e.tensor_copy(out=w2b[:, :, :], in_=w2f[:, :, :]), reads=["w2f"], writes=["w2b"])
            S.dma(hcw[:, :, :], I["hy_cw"].rearrange("p (g j) -> p g j", j=3), writes=["hcw"])
            S.dma(hcwc[:, :, :], I["hy_cw_ctx"].rearrange("p (g j) -> p g j", j=3), writes=["hcwc"])
            S.dma(hcb[:, :], I["hy_cb"][:, :], writes=["hcb"])
            S.dma(hyb[:, :], I["hy_bias"][0, :].partition_broadcast(128), writes=["hyb"])
            S.dma(hnw[:, :], I["hy_nw"][:, :], writes=["hnw"])
            S.dma(absd[:, :], I["absd"][:, :], writes=["absd"])
            S.dma(negt[:, :], I["negt"][:, :], writes=["negt"])
            WCH = min(1024, J * 128)
            for c0_ in range(0, 1024, WCH):
                S.dma(Pt[:, 0:WCH], I["fw4s"][:, c0_:c0_ + WCH], writes=["P1"])
                S.op("dve", lambda e, c0_=c0_: e.tensor_copy(out=w4b[:, c0_:c0_ + WCH], in_=Pt[:, 0:WCH]), reads=["P1"], writes=[("w4b", c0_)])
            S.merge("w4b", [("w4b", c0_) for c0_ in range(0, 1024, WCH)])
            A4 = lambda t_: t_[:, :].rearrange("p (s r k) -> p s r k", r=2, k=128)
            B1v = lambda t_: t_[:, :].rearrange("p (r s j c) -> p r s j c", r=2, s=NSUB, c=CS)
            ZTv = lambda t_: t_[:, :].rearrange("p (r j s c) -> p r j s c", r=2, j=J, c=CS)
            cnt = {"ft": 0, "et": 0, "h3": 0, "w": 0}

            mixo = fm[0]
            gct = Pt[:, 0:J * 64].bitcast(BF16).rearrange("p (j c) -> p j c", c=128)

            def evac(pv_in, out_ap, rkeys, wkey, wg, en=None):
                en = en or S.ev()
                if hy_stop == 3.11:
                    en = "dve"
                if hy_stop == 3.12:
                    en = "act"
                if en == "act":
                    S.op("act", lambda e: e.activation(out=out_ap, in_=pv_in, func=AF.Copy), reads=rkeys, writes=[wkey], wg=wg)
                else:
                    S.op("dve", lambda e: e.tensor_copy(out=out_ap, in_=pv_in), reads=rkeys, writes=[wkey], wg=wg)

            def conv3(src, row0, wt, wkey, g, dst_fn):
                S.dma(cin[:, :], src[row0:row0 + 128, :], writes=["hcin"])
                for j in range(3):
                    S.op("pool", lambda e, j=j: e.tensor_scalar(out=dg3[:, j, :], in0=ident_f[:, :], scalar1=wt[:, g, j:j + 1], scalar2=None, op0=ALU.mult),
                         reads=["ident_f", wkey], writes=[("dg3", j)])
                wg = S.newgroup()
                for o in range(0, NT, TW):
                    b = bank(0, 8)
                    for j in range(3):
                        S.op("pe", lambda e, j=j: e.matmul(psb[b][:, 0:TW], lhsT=dg3[:, j, :], rhs=cin[:, o + 1 + j:o + 1 + j + TW],
                                                           start=(j == 0), stop=(j == 2)),
                             reads=[("dg3", j), "hcin"], writes=[("ps", b)])
                    dst_fn(psb[b][:, 0:TW], o, ("ps", b), wg)

            def to_perm(srct, skey, dst, dkey):
                wg = S.newgroup()
                for j0 in range(0, J, 8):
                    nb = min(8, J - j0)
                    b = bank(0, 8)
                    pv = psb[b][:, :].bitcast(BF16).rearrange("p (a b) -> p a b", b=128)
                    for i in range(nb):
                        S.op("pe", lambda e, i=i: e.transpose(out=pv[:, i, :], in_=srct[:, j0 + i:NT:J], identity=ident_b[:, :]),
                             reads=[skey, "ident_b"], writes=[("ps", b)])
                    evac(pv[:, 0:nb, :], dst[:, j0:j0 + nb, :], [("ps", b)], dkey, wg, "act")

            def fft_fwd(mov_fn, PCn, B1t, B1key, BTt, BTkey, Xt, Xkey):
                b1 = B1v(B1t)
                wg = S.newgroup()
                for j0 in range(0, J, 2):
                    b = bank(0, 8)
                    pv = psb[b][:, :].rearrange("p (r j c) -> p r j c", r=2, c=128)
                    for jj in range(2):
                        j = j0 + jj
                        for ri in range(2 if hy_stop != 3.05 else 0):
                            for pc in range(PCn):
                                mv, mk = mov_fn(pc, j)
                                S.op("pe", lambda e, ri=ri, pc=pc, mv=mv, jj=jj, j=j: e.matmul(pv[:, ri, jj, :], lhsT=ftr[:, j, pc * 2 + ri, :], rhs=mv,
                                                                                         start=(pc == 0), stop=(pc == PCn - 1)),
                                     reads=["ftr"] + list(mk), writes=[("ps", b)])
                    pv5 = psb[b][:, :].rearrange("p (r j s c) -> p r j s c", r=2, j=2, c=CS)
                    en = "dve"
                    for ri in range(2 if hy_stop not in (3.05, 3.07) else 0):
                        evac(pv5[:, ri, :, :, :], b1[:, ri, :, j0:j0 + 2, :].rearrange("p s j c -> p j s c"), [("ps", b)], B1key, wg, en)
                if hy_stop in (3.05, 3.07, 3.1, 3.11, 3.12):
                    return
                stage2(B1t, B1key, BTt, BTkey, Xt, Xkey)

            def stage2(srct, skey, BTt, BTkey, Xt, Xkey):
                b1 = B1v(srct)
                bt = A4(BTt)
                wg = S.newgroup()
                for s0 in range(0, NSUB, 4):
                    ns = min(4, NSUB - s0)
                    b = bank(0, 8)
                    pv = psb[b][:, :].bitcast(BF16).rearrange("p (s r k) -> p s r k", r=2, k=128)
                    for sl in range(ns):
                        for ri in range(2):
                            S.op("pe", lambda e, sl=sl, ri=ri: e.transpose(out=pv[:, sl, ri, :], in_=srct[:, (ri * NSUB + s0 + sl) * 128:(ri * NSUB + s0 + sl + 1) * 128],
                                                                           identity=ident_b[:, :]),
                                 reads=[skey, "ident_b"], writes=[("ps", b)])
                    evac(pv[:, 0:ns, :, :], bt[:, s0:s0 + ns, :, :], [("ps", b)], BTkey, wg, "act")
                if hy_stop == 3.2:
                    return
                dft_j(BTt, BTkey, Xt, Xkey, True)

            def dft_j(srct, skey, Xt, Xkey, fwd):
                bt = A4(srct)
                xo = A4(Xt)
                i2, i3 = (2, 1) if fwd else (1, 2)
                wg = S.newgroup()
                for s0 in range(0, NSUB, 2):
                    b = bank(0, 8)
                    pv = psb[b][:, :].rearrange("p (s r k) -> p s r k", r=2, k=128)
                    for sl in range(2):
                        sub = s0 + sl
                        o0 = sl * 256
                        S.op("pe", lambda e, o0=o0, sub=sub: e.matmul(psb[b][:, o0:o0 + 256], lhsT=w2b[:, 0, :], rhs=srct[:, sub * 256:(sub + 1) * 256], start=True, stop=False),
                             reads=[skey, "w2b"], writes=[("ps", b)])
                        S.op("pe", lambda e, o0=o0, sub=sub: e.matmul(psb[b][:, o0:o0 + 128], lhsT=w2b[:, i2, :], rhs=srct[:, sub * 256 + 128:(sub + 1) * 256], start=False, stop=False),
                             reads=[skey, "w2b"], writes=[("ps", b)])
                        S.op("pe", lambda e, o0=o0, sub=sub: e.matmul(psb[b][:, o0 + 128:o0 + 256], lhsT=w2b[:, i3, :], rhs=srct[:, sub * 256:sub * 256 + 128], start=False, stop=True),
                             reads=[skey, "w2b"], writes=[("ps", b)])
                    evac(psb[b][:, :], Xt[:, s0 * 256:(s0 + 2) * 256], [("ps", b)], Xkey, wg)

            def make_kappa(sig, cg, dstt, dkey):
                kv = dstt[:, :].rearrange("p (t c) -> p t c", c=128)
                wg = S.newgroup()
                for t0 in range(0, 2 * J, 4):
                    hi = cnt["h3"] % 3
                    cnt["h3"] += 1
                    q0 = (sig * 2 * J + t0) * 128
                    S.dma(h3l[hi][:, :], h3_s[:, q0:q0 + 512], writes=[("h3l", hi)])
                    b = bank(0, 8)
                    for tt in range(4):
                        S.op("pe", lambda e, tt=tt: e.matmul(psb[b][:, tt * 128:(tt + 1) * 128], lhsT=h3l[hi][:, tt * 128:(tt + 1) * 128],
                                                             rhs=w4b[:, cg * 128:(cg + 1) * 128], start=True, stop=True),
                             reads=[("h3l", hi), "w4b"], writes=[("ps", b)])
                    wi = cnt["w"] % 2
                    cnt["w"] += 1
                    wgw = S.newgroup()
                    for tt in range(4):
                        tcol = sig * 2 * J + t0 + tt
                        S.op("act", lambda e, tcol=tcol, tt=tt: e.activation(out=wnd[wi][:, tt, :], in_=absd[:, cg * 128:(cg + 1) * 128], func=AF.Exp,
                                                                             scale=negt[:, tcol:tcol + 1]),
                             reads=["absd", "negt"], writes=[("wnd", wi)], wg=wgw)
                    S.op("dve", lambda e: e.tensor_tensor(out=kv[:, t0:t0 + 4, :], in0=psb[b][:, :].rearrange("p (t c) -> p t c", c=128), in1=wnd[wi][:, :, :], op=ALU.mult),
                         reads=[("ps", b), ("wnd", wi)], writes=[dkey], wg=wg)

            def cmul(Gt, Gkey, Kt, Kkey, first):
                g4, k4, y4 = A4(Gt), A4(Kt), A4(Yb)
                hs = NSUB // 2
                wg = S.newgroup()
                for hf in range(2):
                    ss = slice(hf * hs, (hf + 1) * hs)
                    p1 = Pt[:, 0:hs * 64].bitcast(BF16).rearrange("p (s k) -> p s k", k=128)
                    p2 = Pt[:, hs * 128:hs * 128 + hs * 64].bitcast(BF16).rearrange("p (s k) -> p s k", k=128)
                    rd = [Gkey, Kkey]
                    for (ro, (a1, a2), (b1_, b2_), sgn) in ((0, (0, 0), (1, 1), ALU.subtract), (1, (0, 1), (1, 0), ALU.add)):
                        S.op("dve", lambda e, a1=a1, a2=a2: e.tensor_tensor(out=p1, in0=g4[:, ss, a1, :], in1=k4[:, ss, a2, :], op=ALU.mult),
                             reads=rd, writes=["P1"])
                        S.op("pool", lambda e, b1_=b1_, b2_=b2_: e.tensor_tensor(out=p2, in0=g4[:, ss, b1_, :], in1=k4[:, ss, b2_, :], op=ALU.mult),
                             reads=rd, writes=["P2"])
                        S.op("dve", lambda e, sgn=sgn: e.tensor_tensor(out=p1, in0=p1, in1=p2, op=sgn), reads=["P1", "P2"], writes=["P1"])
                        if first:
                            S.op("pool", lambda e, ro=ro: e.tensor_copy(out=y4[:, ss, ro, :], in_=p1), reads=["P1"], writes=["Yb"], wg=wg)
                        else:
                            S.op("pool", lambda e, ro=ro: e.tensor_tensor(out=y4[:, ss, ro, :], in0=y4[:, ss, ro, :], in1=p1, op=ALU.add),
                                 reads=["P1", "Yb"], writes=["Yb"], wg=wg)

            kvA = bufA[:, :].rearrange("p (t c) -> p t c", c=128)
            for cg in range(8 if hy_stop >= 5 else (1 if hy_stop >= 2 else 0)):
                def ev_x0(ps_ap, o, bkey, wg):
                    S.op("act", lambda e: e.activation(out=fm[1][:, o:o + TW], in_=ps_ap, func=AF.Identity, bias=hcb[:, cg:cg + 1]),
                         reads=[bkey, "hcb"], writes=["fm1"], wg=wg)
                def ev_x1(ps_ap, o, bkey, wg):
                    S.op("act", lambda e: e.activation(out=fm[0][:, o:o + TW], in_=ps_ap, func=AF.Identity, bias=hcb[:, 8 + cg:9 + cg]),
                         reads=[bkey, "hcb"], writes=["fm0"], wg=wg)
                def ev_v(ps_ap, o, bkey, wg):
                    S.op("dve", lambda e: e.scalar_tensor_tensor(out=fm[1][:, o:o + TW], in0=ps_ap, scalar=hcb[:, 16 + cg:17 + cg], in1=fm[0][:, o:o + TW],
                                                                 op0=ALU.add, op1=ALU.mult),
                         reads=[bkey, "hcb", "fm0"], writes=["fm1"], wg=wg)
                conv3(pj_fm, 1536 + cg * 128, hcw, "hcw", cg, ev_x0)
                to_perm(fm[1], "fm1", x0t, "x0t")
                conv3(pj_fm, 2560 + cg * 128, hcw, "hcw", 8 + cg, ev_x1)
                conv3(pj_fm, 3584 + cg * 128, hcw, "hcw", 16 + cg, ev_v)
                to_perm(fm[1], "fm1", gt, "gt")
                if hy_stop == 2:
                    break
                make_kappa(0, cg, bufA, "bufA")
                fft_fwd(lambda pc, j: (gt[:, j, :], ["gt"]), 1, bufD, "bufD", bufB, "bufB", bufC, "bufC")
                fft_fwd(lambda pc, j: (kvA[:, pc * J + j, :], ["bufA"]), 2, bufD, "bufD", bufB, "bufB", bufA, "bufA")
                cmul(bufC, "bufC", bufA, "bufA", True)
                make_kappa(1, cg, bufA, "bufA")
                conv3(ch_fm, cg * 128, hcwc, "hcwc", cg, ev_x1)
                conv3(ch_fm, 1024 + cg * 128, hcwc, "hcwc", 8 + cg, ev_v)
                to_perm(fm[1], "fm1", gct, "P1")
                fft_fwd(lambda pc, j: (gct[:, j, :], ["P1"]), 1, bufD, "bufD", bufB, "bufB", bufC, "bufC")
                fft_fwd(lambda pc, j: (kvA[:, pc * J + j, :], ["bufA"]), 2, bufD, "bufD", bufB, "bufB", bufA, "bufA")
                cmul(bufC, "bufC", bufA, "bufA", False)
                dft_j(Yb, "Yb", bufD, "bufD", False)
                zs4 = A4(bufD)
                zt4 = ZTv(bufB)
                wg = S.newgroup()
                for s0 in range(0, NSUB, 4):
                    ns = min(4, NSUB - s0)
                    b = bank(0, 8)
                    pv = psb[b][:, :].bitcast(BF16).rearrange("p (r s m) -> p r s m", r=2, m=128)
                    pv5 = psb[b][:, :].bitcast(BF16).rearrange("p (r s j c) -> p r s j c", r=2, s=4, c=CS)
                    for ri in range(2):
                        for sl in range(ns):
                            S.op("pe", lambda e, sl=sl, ri=ri: e.transpose(out=pv[:, ri, sl, :], in_=bufD[:, ((s0 + sl) * 2 + ri) * 128:((s0 + sl) * 2 + ri + 1) * 128], identity=ident_b[:, :]),
                                 reads=["bufD", "ident_b"], writes=[("ps", b)])
                    en = "dve"
                    for ri in range(2):
                        evac(pv5[:, ri, 0:ns, :, :], zt4[:, ri, :, s0:s0 + ns, :].rearrange("p j s c -> p s j c"), [("ps", b)], "bufB", wg, en)
                yh = Pt[:, :].rearrange("p (j c) -> p j c", c=128)
                wgy = S.newgroup()
                for j0 in range(0, J, 4):
                    b = bank(0, 8)
                    pv = psb[b][:, :].rearrange("p (j c) -> p j c", c=128)
                    for jj in range(4):
                        j = j0 + jj
                        for ri in range(2):
                            S.op("pe", lambda e, ri=ri, jj=jj, j=j: e.matmul(pv[:, jj, :], lhsT=etr[:, j, ri, :], rhs=bufB[:, (ri * J + j) * 128:(ri * J + j + 1) * 128],
                                                                             start=(ri == 0), stop=(ri == 1)),
                                 reads=["etr", "bufB"], writes=[("ps", b)])
                    S.op("pool", lambda e: e.tensor_tensor(out=yh[:, j0:j0 + 4, :], in0=gt[:, j0:j0 + 4, :],
                                                           in1=hyb[:, cg * 128:(cg + 1) * 128].unsqueeze(1).broadcast_to([128, 4, 128]), op=ALU.mult),
                         reads=["gt", "hyb"], writes=["P1", "P2", ("yh", j0)], wg=wgy)
                    S.op("dve", lambda e: e.tensor_tensor(out=yh[:, j0:j0 + 4, :], in0=pv, in1=yh[:, j0:j0 + 4, :], op=ALU.add),
                         reads=[("ps", b), ("yh", j0)], writes=[("yh", j0)])
                    S.op("pool", lambda e: e.tensor_tensor(out=yh[:, j0:j0 + 4, :], in0=yh[:, j0:j0 + 4, :], in1=x0t[:, j0:j0 + 4, :], op=ALU.mult),
                         reads=[("yh", j0), "x0t"], writes=[("yh", j0), "yhdone"], wg=wgy)
                sqv = bufD[:, 0:J * 128]
                S.op("act", lambda e: e.activation(out=sqv, in_=Pt[:, :], func=AF.Square), reads=["yhdone", "P1", "P2"], writes=["bufD"])
                S.op("dve", lambda e: e.tensor_reduce(out=ssq2[:, :], in_=sqv.rearrange("p (g d) -> p g d", d=64), axis=AX.X, op=ALU.add),
                     reads=["bufD"], writes=["ssq2"])
                S.op("act", lambda e: e.activation(out=ssq2[:, :], in_=ssq2[:, :], func=AF.Ln, scale=1.0 / 64, bias=eps_t[:, 0:1]), reads=["ssq2", "eps_t"], writes=["ssq2"])
                S.op("act", lambda e: e.activation(out=ssq2[:, :], in_=ssq2[:, :], func=AF.Exp, scale=-0.5), reads=["ssq2"], writes=["ssq2"])
                ynv = bufA[:, 0:J * 128]
                S.op("dve", lambda e: e.tensor_tensor(out=ynv.rearrange("p (g d) -> p g d", d=64), in0=Pt[:, :].rearrange("p (g d) -> p g d", d=64),
                                                      in1=ssq2[:, :].unsqueeze(2).broadcast_to([128, 2 * J, 64]), op=ALU.mult),
                     reads=["yhdone", "P1", "P2", "ssq2"], writes=["bufA"])
                yn3 = ynv.rearrange("p (j c) -> p j c", c=128)
                wg = S.newgroup()
                for j0 in range(0, J, 8):
                    nb = min(8, J - j0)
                    b = bank(0, 8)
                    pv = psb[b][:, :].bitcast(BF16).rearrange("p (a b) -> p a b", b=128)
                    for i in range(nb):
                        S.op("pe", lambda e, i=i: e.transpose(out=pv[:, i, :], in_=yn3[:, j0 + i, :], identity=ident_b[:, :]),
                             reads=["bufA", "ident_b"], writes=[("ps", b)])
                    for i in range(nb):
                        S.op("dve", lambda e, i=i: e.tensor_scalar(out=mixo[:, j0 + i:NT:J], in0=pv[:, i, :], scalar1=hnw[:, cg:cg + 1], scalar2=None, op0=ALU.mult),
                             reads=[("ps", b), "hnw"], writes=["fm0"], wg=wg)
                S.dma(mix[1024 + cg * 128:1024 + (cg + 1) * 128, :], mixo[:, :], reads=["fm0"], writes=[("mixh", cg)])
            S.barrier()

    if "wprep" in stages and not wprep_inline:
        with contextlib.ExitStack() as ph:
            for _ in wprep_gen(ph):
                pass
            S.barrier()

    if "p2" in stages:
        with contextlib.ExitStack() as ph:
            W2 = min(512, NT)
            xt = T(ph, "p2xt", [128, KC, W2], F32)
            mixt = T(ph, "p2mix", [128, 16, W2], BF16)
            acc = T(ph, "p2acc", [128, KC, W2], F32)
            h1 = T(ph, "p2h1", [128, KC, W2], F32)
            ub = T(ph, "p2u", [128, KC, W2], BF16)
            sq = T(ph, "p2sq", [128, KC, W2], BF16)
            hm = T(ph, "p2hm", [128, FC, W2], BF16)
            rstd = T(ph, "p2rstd", [128, W2], F32)
            av = [T(ph, "p2a%d" % i, [128, W2], F32) for i in range(2)]
            ptf = T(ph, "p2ptf", [128, 2, W2], F32)
            ptb = T(ph, "p2ptb", [128, 2, W2], BF16)
            wo = [T(ph, "p2wo%d" % i, [128, 16, 128], BF16) for i in range(3)]
            wg_ = [T(ph, "p2wg%d" % i, [128, KC, 128], BF16) for i in range(6)]
            wu_ = [T(ph, "p2wu%d" % i, [128, KC, 128], BF16) for i in range(6)]
            wd_ = [T(ph, "p2wd%d" % i, [128, FC, 128], BF16) for i in range(3)]
            wpg_ = [T(ph, "p2wpg%d" % i, [128, KC, 128], BF16) for i in range(4)]
            wpp_ = [T(ph, "p2wpp%d" % i, [128, 2, 128], BF16) for i in range(4)]
            nws = T(ph, "p2nws", [128, 5, KC], F32)
            for i, nm in enumerate(["nm_post", "nf_pre", "nf_post", "ple_npre", "ple_npost"]):
                S.dma(nws[:, i, :], I[nm][:, :], writes=[("nws", i)])
            S.merge("nws", [("nws", i) for i in range(5)])
            wn = {"wo": 0, "wg": 0, "wd": 0, "wp": 0, "a": 0}

            def normed_residual(src, skey, widx, res, rkey, dst, dkey):
                rms_rstd(src, [skey], W2, rstd, "p2rstd", sq, "p2sq", banks=(0, 8))
                wgd = S.newgroup()
                for kc in range(KC):
                    S.op("dve", lambda e, kc=kc: e.scalar_tensor_tensor(out=dst[:, kc, :], in0=src[:, kc, :], scalar=nws[:, widx, kc:kc + 1],
                                                                        in1=rstd[:, :], op0=ALU.mult, op1=ALU.mult),
                         reads=[skey, "nws", "p2rstd"], writes=[(dkey, kc)])
                    S.op("pool", lambda e, kc=kc: e.tensor_tensor(out=dst[:, kc, :], in0=dst[:, kc, :], in1=res[:, kc, :], op=ALU.add),
                         reads=[(dkey, kc), rkey], writes=[(dkey, kc), dkey], wg=wgd)

            def normed_bf16(src, skey, widx, dst, dkey):
                rms_rstd(src, [skey], W2, rstd, "p2rstd", sq, "p2sq", banks=(0, 8))
                wgd = S.newgroup()
                for kc in range(KC):
                    S.op("dve", lambda e, kc=kc: e.scalar_tensor_tensor(out=dst[:, kc, :], in0=src[:, kc, :], scalar=nws[:, widx, kc:kc + 1],
                                                                        in1=rstd[:, :], op0=ALU.mult, op1=ALU.mult),
                         reads=[skey, "nws", "p2rstd"], writes=[dkey], wg=wgd)

            S.dma(mixt[:, :, :], mix[:, 0:W2].rearrange("(kc p) t -> p kc t", p=128), writes=["mixt"])
            for o in range(0, NT, W2):
                S.dma(xt[:, :, :], I["xo"][:, 2 + o:2 + o + W2].rearrange("(kc p) t -> p kc t", p=128), writes=["xt"])
                S.dma(ptf[:, :, :], I["pT"][:, o:o + W2].rearrange("(kc p) t -> p kc t", p=128), writes=["ptf"])
                S.op("pool", lambda e: e.tensor_copy(out=ptb[:, :, :], in_=ptf[:, :, :]), reads=["ptf"], writes=["ptb"])
                wga = S.newgroup()
                for m in range(KC):
                    wi = wn["wo"] % 3
                    wn["wo"] += 1
                    S.dma(wo[wi][:, :, :], wo_s[m, :, :].rearrange("p (kc c) -> p kc c", c=128), writes=[("wo", wi)])
                    b = bank(0, 8)
                    for kc in range(16):
                        S.op("pe", lambda e, kc=kc: e.matmul(psb[b][:, 0:W2], lhsT=wo[wi][:, kc, :], rhs=mixt[:, kc, :], start=(kc == 0), stop=(kc == 15)),
                             reads=[("wo", wi), "mixt"], writes=[("ps", b)])
                    en = S.ev()
                    S.op(en, (lambda e: e.activation(out=acc[:, m, :], in_=psb[b][:, 0:W2], func=AF.Copy)) if en == "act" else
                         (lambda e: e.tensor_copy(out=acc[:, m, :], in_=psb[b][:, 0:W2])), reads=[("ps", b)], writes=["acc"], wg=wga)
                if o + W2 < NT:
                    S.dma(mixt[:, :, :], mix[:, o + W2:o + 2 * W2].rearrange("(kc p) t -> p kc t", p=128), writes=["mixt"])
                normed_residual(acc, "acc", 0, xt, "xt", h1, "h1")
                normed_bf16(h1, "h1", 1, ub, "ub")
                wgh = S.newgroup()
                for j in range(FC):
                    wi = wn["wg"] % 6
                    wn["wg"] += 1
                    S.dma(wg_[wi][:, :, :], wg_s[j, :, :].rearrange("p (kc c) -> p kc c", c=128), writes=[("wg", wi)])
                    S.dma(wu_[wi][:, :, :], wu_s[j, :, :].rearrange("p (kc c) -> p kc c", c=128), writes=[("wu", wi)])
                    bg = bank(0, 8)
                    for kc in range(KC):
                        S.op("pe", lambda e, kc=kc: e.matmul(psb[bg][:, 0:W2], lhsT=wg_[wi][:, kc, :], rhs=ub[:, kc, :], start=(kc == 0), stop=(kc == KC - 1)),
                             reads=[("wg", wi), "ub"], writes=[("ps", bg)])
                    bu = bank(0, 8)
                    for kc in range(KC):
                        S.op("pe", lambda e, kc=kc: e.matmul(psb[bu][:, 0:W2], lhsT=wu_[wi][:, kc, :], rhs=ub[:, kc, :], start=(kc == 0), stop=(kc == KC - 1)),
                             reads=[("wu", wi), "ub"], writes=[("ps", bu)])
                    ai = wn["a"] % 2
                    wn["a"] += 1
                    S.op("act", lambda e: e.activation(out=av[ai][:, :], in_=psb[bg][:, 0:W2], func=AF.Silu), reads=[("ps", bg)], writes=[("av", ai)])
                    S.op("dve", lambda e, j=j: e.tensor_tensor(out=hm[:, j, :], in0=psb[bu][:, 0:W2], in1=av[ai][:, :], op=ALU.mult),
                         reads=[("ps", bu), ("av", ai)], writes=["hm"], wg=wgh)
                wga = S.newgroup()
                for m in range(KC):
                    wi = wn["wd"] % 3
                    wn["wd"] += 1
                    S.dma(wd_[wi][:, :, :], wd_s[m, :, :].rearrange("p (kc c) -> p kc c", c=128), writes=[("wd", wi)])
                    b = bank(0, 8)
                    for fc in range(FC):
                        S.op("pe", lambda e, fc=fc: e.matmul(psb[b][:, 0:W2], lhsT=wd_[wi][:, fc, :], rhs=hm[:, fc, :], start=(fc == 0), stop=(fc == FC - 1)),
                             reads=[("wd", wi), "hm"], writes=[("ps", b)])
                    en = S.ev()
                    S.op(en, (lambda e: e.activation(out=acc[:, m, :], in_=psb[b][:, 0:W2], func=AF.Copy)) if en == "act" else
                         (lambda e: e.tensor_copy(out=acc[:, m, :], in_=psb[b][:, 0:W2])), reads=[("ps", b)], writes=["acc"], wg=wga)
                normed_residual(acc, "acc", 2, h1, "h1", xt, "xt")
                normed_bf16(xt, "xt", 3, ub, "ub")
                wga = S.newgroup()
                for m in range(KC):
                    wi = wn["wp"] % 4
                    wn["wp"] += 1
                    S.dma(wpg_[wi][:, :, :], wpg_s[m, :, :].rearrange("p (kc c) -> p kc c", c=128), writes=[("wpg", wi)])
                    S.dma(wpp_[wi][:, :, :], wpp_s[m, :, :].rearrange("p (kc c) -> p kc c", c=128), writes=[("wpp", wi)])
                    bg = bank(0, 8)
                    for kc in range(KC):
                        S.op("pe", lambda e, kc=kc: e.matmul(psb[bg][:, 0:W2], lhsT=wpg_[wi][:, kc, :], rhs=ub[:, kc, :], start=(kc == 0), stop=(kc == KC - 1)),
                             reads=[("wpg", wi), "ub"], writes=[("ps", bg)])
                    be = bank(0, 8)
                    for kc in range(2):
                        S.op("pe", lambda e, kc=kc: e.matmul(psb[be][:, 0:W2], lhsT=wpp_[wi][:, kc, :], rhs=ptb[:, kc, :], start=(kc == 0), stop=(kc == 1)),
                             reads=[("wpp", wi), "ptb"], writes=[("ps", be)])
                    ai = wn["a"] % 2
                    wn["a"] += 1
                    S.op("act", lambda e: e.activation(out=av[ai][:, :], in_=psb[bg][:, 0:W2], func=AF.Sigmoid), reads=[("ps", bg)], writes=[("av", ai)])
                    S.op("dve", lambda e, m=m: e.tensor_tensor(out=acc[:, m, :], in0=psb[be][:, 0:W2], in1=av[ai][:, :], op=ALU.mult),
                         reads=[("ps", be), ("av", ai)], writes=["acc"], wg=wga)
                normed_residual(acc, "acc", 4, xt, "xt", h1, "h1")
                S.dma(yT[:, o:o + W2].rearrange("(kc p) t -> p kc t", p=128), h1[:, :, :], reads=["h1"], writes=[("yT", o)])
            S.barrier()

    C.I = I
    C.scr = dict(pj_fm=pj_fm, pj_z=pj_z, pj_dt=pj_dt, cs_fm=cs_fm, cs_dt=cs_dt, ch_fm=ch_fm, mix=mix,
                 wg_s=wg_s, wu_s=wu_s, wd_s=wd_s, wo_s=wo_s, wpg_s=wpg_s, wpp_s=wpp_s)

    S.barrier()
    st.close()
    return nc


def _pcol(v):
    return np.ascontiguousarray(np.asarray(v, np.float32).reshape(-1, 128).T)


def host_consts(NT):
    J = NT // 128
    N = 2 * NT
    c = {}
    c["ident"] = np.eye(128, dtype=np.float32)
    r = np.arange(128)
    c["tri"] = (r[:, None] <= r[None, :]).astype(np.float32)
    nm = np.zeros((128, 2, 128), np.float32)
    nm[:, 0, :] = np.where(r[:, None] <= r[None, :], 0.0, NEG)
    nm[:, 1, :] = np.where(r[:, None] >= r[None, :], 0.0, NEG)
    c["negmask"] = nm.reshape(128, 256)
    sm = np.zeros((32, 32, 128), np.float32)
    for k in range(32):
        sm[k, k, :] = 1.0
    c["selmat"] = sm.reshape(32, 32 * 128)
    p = np.arange(128, dtype=np.float64)
    k1 = np.arange(128, dtype=np.float64)
    ft = np.zeros((128, J, 2, 2, 128), np.float64)
    et = np.zeros((128, J, 2, 128), np.float64)
    for j in range(J):
        for pc in range(2):
            n = J * (p + 128 * pc) + j
            ang = 2 * np.pi * (k1[None, :] + 0.5) * n[:, None] / N
            ft[:, j, pc, 0, :] = np.cos(ang)
            ft[:, j, pc, 1, :] = -np.sin(ang)
        n = J * p + j
        ang = 2 * np.pi * (k1[:, None] + 0.5) * n[None, :] / N
        et[:, j, 0, :] = (2.0 / N) * np.cos(ang)
        et[:, j, 1, :] = -(2.0 / N) * np.sin(ang)
    c["ftab"] = ft.reshape(128, -1).astype(np.float32)
    c["etab"] = et.reshape(128, -1).astype(np.float32)
    CS = 128 // J
    w2 = np.zeros((128, 3, 128), np.float64)
    for a in range(J):
        for b in range(J):
            ang = 2 * np.pi * a * b / J
            for cc in range(CS):
                w2[a * CS + cc, 0, b * CS + cc] = np.cos(ang)
                w2[a * CS + cc, 1, b * CS + cc] = -np.sin(ang)
                w2[a * CS + cc, 2, b * CS + cc] = np.sin(ang)
    c["w2tab"] = w2.reshape(128, -1).astype(np.float32)
    return c


def _grp(w2d):
    taps, Cc = w2d.shape
    a = np.asarray(w2d, np.float32).T.reshape(Cc // 128, 128, taps).transpose(1, 0, 2)
    return np.ascontiguousarray(a.reshape(128, -1))


def _hyena_tables(NT, L, mode):
    J = NT // 128
    N = 2 * NT
    sgn = np.zeros((2, N), np.float64)
    dr = np.zeros((2, N), np.int64)
    pos = np.zeros((2, N), np.float64)
    i = np.arange(N)
    lo = i < NT
    hi = i > NT
    sgn[0, lo] = 1.0; dr[0, lo] = 0; pos[0, lo] = i[lo]
    sgn[0, hi] = -1.0; dr[0, hi] = 1; pos[0, hi] = N - i[hi]
    if mode == "first":
        sgn[1, lo] = 1.0; dr[1, lo] = 1; pos[1, lo] = NT - i[lo]
        sgn[1, hi] = -1.0; dr[1, hi] = 1; pos[1, hi] = NT + N - i[hi]
    elif mode == "second":
        sgn[1, lo] = 1.0; dr[1, lo] = 0; pos[1, lo] = NT + i[lo]
        sgn[1, hi] = -1.0; dr[1, hi] = 0; pos[1, hi] = i[hi] - NT
    t = pos / (L - 1)
    fb = np.linspace(1e-4, 15.0, 16)
    ang = fb[None, None, :] * (2.0 * np.pi * pos[..., None] / L)
    z = np.concatenate([t[..., None], np.cos(ang), -np.sin(ang)], axis=-1)
    def reord(a):
        sh = a.shape[2:]
        a = a.reshape((2, 2, 128, J) + sh)
        a = np.moveaxis(a, 3, 2)
        return a.reshape((2 * N,) + sh)
    zt = np.ascontiguousarray(reord(z).T.astype(np.float32))
    mf = reord(sgn * (dr == 0))
    mb = reord(sgn * (dr == 1))
    mfb = np.concatenate([np.tile(mf[None], (64, 1)), np.tile(mb[None], (64, 1))], 0).astype(np.float32)
    tq = reord(t).reshape(2 * 2 * J, 128)
    negt = np.ascontiguousarray((-tq.T).astype(np.float32))
    return zt, mfb, negt


def host_inputs(inp, NT):
    J = NT // 128
    NTH = NT + 4
    g = lambda k: np.asarray(inp[k], np.float32)
    W = g("w_in")[0]
    consts = host_consts(NT)
    xp, xs = g("x_prompt"), g("x_sample")
    pp, psm = g("p_prompt")[0], g("p_sample")[0]
    cw, cb = g("ssd_conv_w")[0], g("ssd_conv_b")[0]
    hw, hb = g("hy_conv_w")[0], g("hy_conv_b")[0]
    dtb, alog = g("ssd_dt_bias")[0], g("ssd_a_log")[0]
    max_decay = math.log(1e-2) / 0.3
    min_decay = math.log(1e-2) / 1.5
    deltas = np.abs(np.linspace(min_decay, max_decay, 1024)).astype(np.float32)
    w4 = g("hy_f_w4")[0]
    common = dict(consts)
    common.update(
        w_in_fm=np.ascontiguousarray(np.concatenate([W[:, 1024:2560], W[:, 2592:5664]], 1)),
        w_in_tm=np.ascontiguousarray(np.concatenate([W[:, 0:1024], W[:, 2560:2592]], 1)),
        w_cs_fm=np.ascontiguousarray(W[:, 1024:2304]),
        w_ch_fm=np.ascontiguousarray(W[:, 3616:5664]),
        nmp=_pcol(g("norm_mix_pre")[0]),
        ssd_cw=_grp(cw), ssd_cb=_pcol(cb),
        dtb=dtb.reshape(1, 32), alog=alog.reshape(1, 32),
        ssd_d=g("ssd_d")[0].reshape(1, 16), ssd_nw=_pcol(g("ssd_norm_w")[0]),
        hy_cw=_grp(hw), hy_cw_ctx=_grp(hw[:, 1024:]), hy_cb=_pcol(hb),
        fw1=g("hy_f_w1")[0], fb1=g("hy_f_b1")[0].reshape(64, 1), fw2=g("hy_f_w2")[0], fb2=g("hy_f_b2")[0].reshape(64, 1),
        fw3=np.ascontiguousarray(np.concatenate([g("hy_f_w3")[0]] * 2, 1)),
        fb3=np.concatenate([g("hy_f_b3")[0]] * 2).reshape(128, 1),
        ffreq=np.concatenate([g("hy_f_freq")[0]] * 2).reshape(128, 1),
        fw4s=np.ascontiguousarray(np.concatenate([w4[:, :1024], w4[:, 1024:]], 0)),
        absd=np.ascontiguousarray(np.tile(deltas[None], (128, 1))),
        hy_bias=g("hy_bias")[0].reshape(1, 1024), hy_nw=_pcol(g("hy_norm_w")[0]),
        w_out=g("w_out")[0], nm_post=_pcol(g("norm_mix_post")[0]), nf_pre=_pcol(g("norm_ffn_pre")[0]),
        w_gate=g("w_gate")[0], w_up=g("w_up")[0], w_down=g("w_down")[0], nf_post=_pcol(g("norm_ffn_post")[0]),
        ple_npre=_pcol(g("ple_norm_pre")[0]), w_pg=g("w_ple_gate")[0], w_pp=g("w_ple_proj")[0],
        ple_npost=_pcol(g("ple_norm_post")[0]),
    )
    tabs = {m: _hyena_tables(NT, (NT if m == "prompt" else 2 * NT), m) for m in ("prompt", "first", "second")}

    def halo_T(seq, a, b):
        L = seq.shape[0]
        out = np.zeros((b - a + 4, seq.shape[1]), np.float32)
        lo, hi = max(a - 2, 0), min(b + 2, L)
        out[lo - (a - 2):hi - (a - 2)] = seq[lo:hi]
        return np.ascontiguousarray(out.T)

    maps = []
    nprompt = xp.shape[0]
    for c in range(8):
        m = dict(common)
        if c < nprompt:
            mode = "prompt"
            m["xo"] = halo_T(xp[c], 0, NT)
            m["xcs"] = np.zeros((D, NTH), np.float32)
            m["xch"] = np.zeros((D, NTH), np.float32)
            m["pT"] = np.ascontiguousarray(pp[c].T)
            dirc, rev = 0, False
            sel = (0.0, 0.0)
        else:
            s, hh = (c - nprompt) // 2, (c - nprompt) % 2
            oh = 1 - hh
            mode = "first" if hh == 0 else "second"
            m["xo"] = halo_T(xs[s], hh * NT, (hh + 1) * NT)
            ctx = halo_T(xs[s], oh * NT, (oh + 1) * NT)
            m["xch"] = ctx
            rev = hh == 0
            m["xcs"] = np.ascontiguousarray(ctx[:, ::-1]) if rev else ctx
            m["pT"] = np.ascontiguousarray(psm[s, hh * NT:(hh + 1) * NT].T)
            dirc = 1 if hh == 0 else 0
            sel = (1.0, 0.0) if hh == 1 else (0.0, 1.0)
        m["w_cs_tm"] = np.ascontiguousarray(W[:, 2560 + dirc * 16:2560 + dirc * 16 + 16])
        cwc = cw[:, :1280]
        m["ssd_cw_ctx"] = _grp(cwc[::-1] if rev else cwc)
        m["ssd_cb_ctx"] = _pcol(cb[:1280])
        m["dtb_ctx"] = dtb[dirc].reshape(1, 16)
        m["alog_ctx"] = alog[dirc].reshape(1, 16)
        m["sel"] = np.tile(np.asarray(sel, np.float32)[None], (128, 1))
        m["zt"], m["mfb"], m["negt"] = tabs[mode]
        maps.append({k: np.ascontiguousarray(v, dtype=np.float32) for k, v in m.items()})
    return maps


_NC_CACHE = {}


def kernel(**inputs):
    NT = 4096
    if NT not in _NC_CACHE:
        _NC_CACHE[NT] = build(NT)
    nc = _NC_CACHE[NT]
    maps = host_inputs(inputs, NT)
    res = run_bass_kernel_spmd(nc, maps, core_ids=list(range(8)))
    outs = [np.asarray(r["yT"], np.float32) for r in res.results]
    y_prompt = np.stack([outs[c].T for c in range(4)], 0)
    y_sample = np.stack([np.concatenate([outs[4 + 2 * s].T, outs[5 + 2 * s].T], 0) for s in range(2)], 0)
    return (np.ascontiguousarray(y_prompt), np.ascontiguousarray(y_sample))
```
